# Optimizing a Trainium2 kernel written in Bass

```python
import math
import jax, jax.numpy as jnp
from jax import lax
import numpy as np

D_MODEL = 2048
BATCH = 4
SEQ = 2048
DEPTH = 2

N_MEM = 256
N_MIXERS = 2
EXPAND = 2
INNER = EXPAND * D_MODEL
XA_HEADS = 4
XA_WIDTH = INNER // 4
XA_DIM = XA_WIDTH // XA_HEADS
MIX_WIDTH = INNER - XA_WIDTH
DN_HEAD_DIM = 128
DN_V_HEADS = MIX_WIDTH // DN_HEAD_DIM
DN_QK_HEADS = DN_V_HEADS // 2
DN_QK_WIDTH = DN_QK_HEADS * DN_HEAD_DIM
DN_CONV = 4
DN_CHUNK = 64
DN_MIX_COLS = 2 * DN_QK_WIDTH + MIX_WIDTH + 2 * DN_V_HEADS
DN_PROJ = DN_MIX_COLS + XA_WIDTH + INNER
SB_HEAD_DIM = 128
SB_HEADS = MIX_WIDTH // SB_HEAD_DIM
SB_BLOCK = 128
SB_MIX_COLS = 3 * MIX_WIDTH
SB_PROJ = SB_MIX_COLS + XA_WIDTH + INNER
EPS = 1e-6

kernel_name = "hybrid_deltanet_stickbreaking_memxattn"


def rms_norm(x, g):
    xf = x.astype(jnp.float32)
    y = xf * lax.rsqrt(jnp.mean(xf * xf, axis=-1, keepdims=True) + EPS)
    return (y * g.astype(jnp.float32)).astype(x.dtype)


def l2_norm(x):
    xf = x.astype(jnp.float32)
    return (xf * lax.rsqrt(jnp.sum(xf * xf, axis=-1, keepdims=True) + EPS)).astype(x.dtype)


def causal_depthwise_conv(x, w):
    c = x.shape[-1]
    return lax.conv_general_dilated(
        x, w[:, None, :].astype(x.dtype), window_strides=(1,),
        padding=((w.shape[0] - 1, 0),), dimension_numbers=("NWC", "WIO", "NWC"),
        feature_group_count=c)


def gated_delta_rule(q, k, v, g, beta):
    f32 = jnp.float32
    B, H, S, dk = q.shape
    dv = v.shape[-1]
    C, N = DN_CHUNK, S // DN_CHUNK
    q, k, v = (t.astype(f32).reshape(B, H, N, C, t.shape[-1]) for t in (q, k, v))
    g = g.astype(f32).reshape(B, H, N, C)
    beta = beta.astype(f32).reshape(B, H, N, C)
    gc = jnp.cumsum(g, axis=-1)
    idx = jnp.arange(C)
    incl = idx[:, None] >= idx[None, :]
    strict = idx[:, None] > idx[None, :]
    decay = jnp.exp(jnp.where(incl, gc[..., :, None] - gc[..., None, :], -jnp.inf))
    kk = jnp.einsum("bhncd,bhnjd->bhncj", k, k)
    m = jnp.where(strict, beta[..., :, None] * kk * decay, 0.0) + jnp.eye(C, dtype=f32)
    rhs = jnp.concatenate([v * beta[..., None], k * (beta * jnp.exp(gc))[..., None]], axis=-1)
    sol = lax.linalg.triangular_solve(m, rhs, left_side=True, lower=True, unit_diagonal=True)
    u0, w = sol[..., :dv], sol[..., dv:]
    qk = jnp.einsum("bhncd,bhnjd->bhncj", q, k) * decay
    q_dec = q * jnp.exp(gc)[..., None]
    k_dec = k * jnp.exp(gc[..., -1:] - gc)[..., None]
    chunk_decay = jnp.exp(gc[..., -1])

    def step(state, inp):
        u0_c, w_c, qk_c, qd_c, kd_c, cd_c = inp
        u = u0_c - jnp.einsum("bhcd,bhde->bhce", w_c, state)
        o = jnp.einsum("bhcd,bhde->bhce", qd_c, state) + jnp.einsum("bhcj,bhje->bhce", qk_c, u)
        state = cd_c[..., None, None] * state + jnp.einsum("bhcd,bhce->bhde", kd_c, u)
        return state, o

    xs = tuple(jnp.moveaxis(t, 2, 0) for t in (u0, w, qk, q_dec, k_dec, chunk_decay))
    _, o = lax.scan(step, jnp.zeros((B, H, dk, dv), f32), xs)
    return jnp.moveaxis(o, 0, 2).reshape(B, H, S, dv)


def deltanet_branch(p, conv_w, a_log, dt_bias, out_g):
    B, S, _ = p.shape
    c_qkv = 2 * DN_QK_WIDTH + MIX_WIDTH
    qkv = jax.nn.silu(causal_depthwise_conv(p[..., :c_qkv], conv_w))
    q = qkv[..., :DN_QK_WIDTH].reshape(B, S, DN_QK_HEADS, DN_HEAD_DIM)
    k = qkv[..., DN_QK_WIDTH:2 * DN_QK_WIDTH].reshape(B, S, DN_QK_HEADS, DN_HEAD_DIM)
    v = qkv[..., 2 * DN_QK_WIDTH:].reshape(B, S, DN_V_HEADS, DN_HEAD_DIM)
    a = p[..., c_qkv:c_qkv + DN_V_HEADS].astype(jnp.float32)
    b = p[..., c_qkv + DN_V_HEADS:].astype(jnp.float32)
    rep = DN_V_HEADS // DN_QK_HEADS
    q = jnp.repeat(l2_norm(q), rep, axis=2) * DN_HEAD_DIM ** -0.5
    k = jnp.repeat(l2_norm(k), rep, axis=2)
    g = -jnp.exp(a_log.astype(jnp.float32)) * jax.nn.softplus(a + dt_bias.astype(jnp.float32))
    beta = jax.nn.sigmoid(b)
    o = gated_delta_rule(q.transpose(0, 2, 1, 3), k.transpose(0, 2, 1, 3), v.transpose(0, 2, 1, 3),
                         g.transpose(0, 2, 1), beta.transpose(0, 2, 1))
    o = rms_norm(o.astype(p.dtype), out_g)
    return o.transpose(0, 2, 1, 3).reshape(B, S, MIX_WIDTH)


def stick_breaking_attention(q, k, v):
    B, H, S, d = q.shape
    scale = d ** -0.5
    outs = []
    for blk in range(S // SB_BLOCK):
        q0 = blk * SB_BLOCK
        kv_len = q0 + SB_BLOCK
        qb = q[:, :, q0:kv_len]
        kb, vb = k[:, :, :kv_len], v[:, :, :kv_len]
        z = jnp.einsum("bhtd,bhsd->bhts", qb, kb).astype(jnp.float32) * scale
        t_pos = q0 + jnp.arange(SB_BLOCK)
        s_pos = jnp.arange(kv_len)
        mask = s_pos[None, :] < t_pos[:, None]
        log_rest = jnp.where(mask, jax.nn.log_sigmoid(-z), 0.0)
        later = lax.cumsum(log_rest, axis=3, reverse=True) - log_rest
        wts = jnp.where(mask, jnp.exp(jax.nn.log_sigmoid(z) + later), 0.0)
        outs.append(jnp.einsum("bhts,bhsd->bhtd", wts.astype(v.dtype), vb))
    return jnp.concatenate(outs, axis=2)


def stick_breaking_branch(p, qn_g, kn_g):
    B, S, _ = p.shape
    q, k, v = (t.reshape(B, S, SB_HEADS, SB_HEAD_DIM) for t in jnp.split(p, 3, axis=-1))
    q, k = rms_norm(q, qn_g), rms_norm(k, kn_g)
    o = stick_breaking_attention(q.transpose(0, 2, 1, 3), k.transpose(0, 2, 1, 3), v.transpose(0, 2, 1, 3))
    return o.transpose(0, 2, 1, 3).reshape(B, S, MIX_WIDTH)


def memory_cross_attention(xq, mem_n, w_kv, qn_g, kn_g):
    B, S, _ = xq.shape
    q = rms_norm(xq.reshape(B, S, XA_HEADS, XA_DIM), qn_g)
    kv = jnp.einsum("bmd,de->bme", mem_n, w_kv)
    k = rms_norm(kv[..., :XA_WIDTH].reshape(B, -1, XA_HEADS, XA_DIM), kn_g)
    v = kv[..., XA_WIDTH:].reshape(B, -1, XA_HEADS, XA_DIM)
    s = jnp.einsum("bthd,bmhd->bhtm", q, k).astype(jnp.float32) * XA_DIM ** -0.5
    p = jax.nn.softmax(s, axis=-1).astype(v.dtype)
    return jnp.einsum("bhtm,bmhd->bthd", p, v).reshape(B, S, XA_WIDTH)


def setup_inputs(seed: int = 0) -> dict:
    key = jax.random.key(seed)
    ks = jax.random.split(key, 20)
    f32 = jnp.float32
    n_dn = (DEPTH + N_MIXERS - 1) // N_MIXERS
    n_sb = DEPTH // N_MIXERS

    def dense(k, shape, fan_in):
        return jax.random.normal(k, shape, f32) * fan_in ** -0.5

    def gain(k, shape):
        return 1.0 + 0.02 * jax.random.normal(k, shape, f32)

    dt = jnp.exp(jax.random.uniform(ks[10], (n_dn, DN_V_HEADS), f32,
                                    minval=math.log(1e-3), maxval=math.log(1e-1)))
    return {
        "x": jax.random.normal(ks[0], (BATCH, SEQ, D_MODEL), f32),
        "mem": jax.random.normal(ks[1], (BATCH, N_MEM, D_MODEL), f32),
        "norm_g": gain(ks[2], (DEPTH, D_MODEL)),
        "mem_norm_g": gain(ks[3], (D_MODEL,)),
        "mem_w_kv": dense(ks[4], (DEPTH, D_MODEL, 2 * XA_WIDTH), D_MODEL),
        "xa_q_norm_g": gain(ks[5], (DEPTH, XA_DIM)),
        "xa_k_norm_g": gain(ks[6], (DEPTH, XA_DIM)),
        "w_out": dense(ks[7], (DEPTH, INNER, D_MODEL), INNER),
        "dn_w_in": dense(ks[8], (n_dn, D_MODEL, DN_PROJ), D_MODEL),
        "dn_conv_w": dense(ks[9], (n_dn, DN_CONV, 2 * DN_QK_WIDTH + MIX_WIDTH), DN_CONV),
        "dn_a_log": jnp.log(jax.random.uniform(ks[11], (n_dn, DN_V_HEADS), f32, minval=1.0, maxval=16.0)),
        "dn_dt_bias": dt + jnp.log(-jnp.expm1(-dt)),
        "dn_out_norm_g": gain(ks[12], (n_dn, DN_HEAD_DIM)),
        "sb_w_in": dense(ks[13], (n_sb, D_MODEL, SB_PROJ), D_MODEL),
        "sb_q_norm_g": gain(ks[14], (n_sb, SB_HEAD_DIM)),
        "sb_k_norm_g": gain(ks[15], (n_sb, SB_HEAD_DIM)),
    }


def reference(x, mem, norm_g, mem_norm_g, mem_w_kv, xa_q_norm_g, xa_k_norm_g, w_out,
              dn_w_in, dn_conv_w, dn_a_log, dn_dt_bias, dn_out_norm_g,
              sb_w_in, sb_q_norm_g, sb_k_norm_g):
    mem_n = rms_norm(mem, mem_norm_g)
    for i in range(DEPTH):
        h = rms_norm(x, norm_g[i])
        j = i // N_MIXERS
        if i % N_MIXERS == 0:
            proj = jnp.einsum("bsd,de->bse", h, dn_w_in[j])
            mix = deltanet_branch(proj[..., :DN_MIX_COLS], dn_conv_w[j], dn_a_log[j],
                                  dn_dt_bias[j], dn_out_norm_g[j])
        else:
            proj = jnp.einsum("bsd,de->bse", h, sb_w_in[j])
            mix = stick_breaking_branch(proj[..., :SB_MIX_COLS], sb_q_norm_g[j], sb_k_norm_g[j])
        xq = proj[..., -(XA_WIDTH + INNER):-INNER]
        z = proj[..., -INNER:]
        xa = memory_cross_attention(xq, mem_n, mem_w_kv[i], xa_q_norm_g[i], xa_k_norm_g[i])
        y = jnp.concatenate([mix, xa], axis=-1) * jax.nn.silu(z)
        x = x + jnp.einsum("bse,ed->bsd", y, w_out[i])
    return x
```

```python
import numpy as np
import concourse.bass as bass
import concourse.mybir as mybir
from concourse.bass_utils import run_bass_kernel_spmd

F32 = mybir.dt.float32
BF16 = mybir.dt.bfloat16
F32R = mybir.dt.float32r
AF = mybir.ActivationFunctionType
ALU = mybir.AluOpType
AX = mybir.AxisListType

N_DMA_SEMS = 24
EMBED_WAITS = True
SEM_SKIP = 0
SB_K = 2
SBP = 2
XAD = 0
SB_DELAY = 300
DN_W = (3, 1, 1)
PSL = 4


class Tile:
    __slots__ = ("name", "h", "last_w", "readers", "space", "root")

    def __init__(self, name, h, space, root=None):
        self.name = name
        self.h = h
        self.space = space
        self.last_w = None
        self.readers = {}
        self.root = self if root is None else root.root

    def __getitem__(self, idx):
        return self.h[idx]


class Ctx:
    def __init__(self, nc):
        self.nc = nc
        self.E = {"pe": nc.tensor, "act": nc.scalar, "dve": nc.vector, "pool": nc.gpsimd, "sp": nc.sync}
        self.sem = {}
        self.cnt = {}
        self._skip = [nc.alloc_semaphore("skip%d" % i) for i in range(SEM_SKIP)]
        for e in self.E:
            self.sem[e] = nc.alloc_semaphore("s_" + e)
            self.cnt[e] = 0
        self.dsem = [nc.alloc_semaphore("d%d" % i) for i in range(N_DMA_SEMS)]
        self.dcnt = [0] * N_DMA_SEMS
        self.dnext = 0
        self.seen = {e: {} for e in self.E}
        self.n_inst = 0
        self.n_wait = 0

    def sb(self, name, shape, dtype):
        return Tile(name, self.nc.alloc_sbuf_tensor(name, list(shape), dtype), "sb")

    def ps(self, name, shape, dtype=F32):
        return Tile(name, self.nc.alloc_psum_tensor(name, list(shape), dtype), "ps")

    def dram(self, name, shape, dtype, kind="Internal"):
        return Tile(name, self.nc.dram_tensor(name, list(shape), dtype, kind=kind), "dram")

    def _semof(self, key):
        if isinstance(key, str):
            return self.sem[key]
        return self.dsem[key]

    def _wait(self, eng, key, count):
        if self.seen[eng].get(key, 0) >= count:
            return
        self.E[eng].wait_ge(self._semof(key), count)
        self.seen[eng][key] = count
        self.n_wait += 1

    @staticmethod
    def _rw(reads, writes):
        rs, ws = [], []
        for t in reads:
            r = t.root
            (ws if r.space == "ps" else rs).append(r)
        for t in writes:
            ws.append(t.root)
        return rs, ws

    def _deps(self, eng, reads, writes, defer=False, lhs=None):
        reads, writes = self._rw(reads, writes)
        need = {}
        for t in reads:
            if t.last_w is not None:
                k, cnt = t.last_w
                need[k] = max(need.get(k, 0), cnt)
        for t in writes:
            if t.last_w is not None:
                k, cnt = t.last_w
                need[k] = max(need.get(k, 0), cnt)
            for k, cnt in t.readers.items():
                need[k] = max(need.get(k, 0), cnt)
        hard = set()
        if lhs is not None:
            for t in lhs:
                r = t.root
                if r.last_w is not None:
                    hard.add(r.last_w[0])
        todo = [(k, cnt) for k, cnt in need.items() if self.seen[eng].get(k, 0) < cnt]
        last = None
        if defer and todo:
            soft = [x for x in todo if x[0] not in hard]
            if soft:
                last = soft[-1]
                todo.remove(last)
        for k, cnt in todo:
            self._wait(eng, k, cnt)
        return last

    def _embed(self, eng, inst, last):
        if last is not None:
            k, cnt = last
            inst._wait_ge(self._semof(k), cnt)
            self.seen[eng][k] = cnt

    def _mark(self, key, count, reads, writes):
        reads, writes = self._rw(reads, writes)
        for t in reads:
            t.readers[key] = count
        for t in writes:
            t.last_w = (key, count)
            t.readers = {}

    def op(self, eng, fn, reads=(), writes=(), multi=False, lhs=None):
        defer = EMBED_WAITS and (eng != "pe" or lhs is not None) and not multi
        last = self._deps(eng, reads, writes, defer, lhs)
        inst = fn()
        self._embed(eng, inst, last)
        self.cnt[eng] += 1
        inst.then_inc(self.sem[eng], 1)
        self._mark(eng, self.cnt[eng], reads, writes)
        self.n_inst += 1
        return inst

    def group(self, eng, fns, reads=(), writes=(), lhs=None):
        defer = EMBED_WAITS and lhs is not None
        last = self._deps(eng, reads, writes, defer, lhs)
        inst = None
        for n, fn in enumerate(fns):
            inst = fn()
            if n == 0:
                self._embed(eng, inst, last)
            self.n_inst += 1
        self.cnt[eng] += 1
        inst.then_inc(self.sem[eng], 1)
        self._mark(eng, self.cnt[eng], reads, writes)
        return inst

    def dma(self, eng, out_ap, in_ap, reads=(), writes=(), **kw):
        i = self.dnext
        self.dnext = (self.dnext + 1) % N_DMA_SEMS
        if self.dcnt[i] > 0:
            self._wait(eng, i, self.dcnt[i])
        last = self._deps(eng, reads, writes, EMBED_WAITS)
        inst = self.E[eng].dma_start(out=out_ap, in_=in_ap, **kw)
        self._embed(eng, inst, last)
        self.dcnt[i] += 16
        inst.then_inc(self.dsem[i], 16)
        self._mark(i, self.dcnt[i], reads, writes)
        self.n_inst += 1
        return inst

    def finish(self, tiles=()):
        for i in range(N_DMA_SEMS):
            if self.dcnt[i] > 0:
                self._wait("sp", i, self.dcnt[i])
        for e in self.E:
            if e != "sp" and self.cnt[e] > 0:
                self._wait("sp", e, self.cnt[e])

    def barrier(self):
        for e in self.E:
            for i in range(N_DMA_SEMS):
                if self.dcnt[i] > 0:
                    self._wait(e, i, self.dcnt[i])
            for f in self.E:
                if f != e and self.cnt[f] > 0:
                    self._wait(e, f, self.cnt[f])


from contextlib import ExitStack


class Scope:
    uid = 0

    def __init__(self, ctx):
        self.ctx = ctx
        self.st = ExitStack()

    def __enter__(self):
        self.st.__enter__()
        return self

    def __exit__(self, *a):
        return self.st.__exit__(*a)

    def sb(self, name, shape, dtype):
        Scope.uid += 1
        name = "%s_u%d" % (name, Scope.uid)
        h = self.st.enter_context(self.ctx.nc.sbuf_tensor(name, list(shape), dtype))
        return Tile(name, h, "sb")

    def ps(self, name, shape, dtype=F32):
        Scope.uid += 1
        name = "%s_u%d" % (name, Scope.uid)
        h = self.st.enter_context(self.ctx.nc.psum_tensor(name, list(shape), dtype))
        return Tile(name, h, "ps")


def view(name, ap, root=None):
    return Tile(name, ap, "view", root)


T = 2048
D = 2048
KT = 16
NMEM = 256
EPS = 1e-6
C_ID, C_ONES, C_TRI, C_NEGI, C_STRICT, C_LOW, C_NEGS = 0, 128, 256, 384, 512, 640, 768
NCST = 896


def make_consts():
    c = np.zeros((128, NCST), np.float32)
    j = np.arange(128)[:, None]
    i = np.arange(128)[None, :]
    c[:, C_ID:C_ID + 128] = (i == j)
    c[:, C_ONES:C_ONES + 128] = 1.0
    c[:, C_TRI:C_TRI + 128] = (j <= i)
    c[:, C_NEGI:C_NEGI + 128] = np.where(i >= j, 0.0, -30000.0)
    c[:, C_STRICT:C_STRICT + 128] = (i > j)
    c[:, C_LOW:C_LOW + 128] = (j > i)
    c[:, C_NEGS:C_NEGS + 128] = np.where(i > j, 0.0, -30000.0)
    return c


class Prog:
    def __init__(self, nc, dbg=None):
        self.nc = nc
        self.ctx = Ctx(nc)
        self.dbg = dbg or {}
        c = self.ctx
        self.x_in = c.dram("x", [T, D], F32, "ExternalInput")
        self.mem_in = c.dram("mem", [NMEM, D], F32, "ExternalInput")
        self.gB_in = c.dram("gB", [128, 3 * D], F32, "ExternalInput")
        self.cst_in = c.dram("cst", [128, NCST], F32, "ExternalInput")
        self.cst = c.sb("cst_sb", [128, NCST], F32)
        self.cstb = c.sb("cst_bf", [128, NCST], BF16)
        self.memT = c.sb("memT", [128, KT, NMEM], BF16)
        self.pv_in = c.dram("pv", [128, NPV], F32, "ExternalInput")
        self.pv = c.sb("pv_sb", [128, NPV], F32)
        self.onec = c.sb("onec", [128, 1], F32)
        self.epsc = c.sb("epsc", [128, 1], F32)
        self.wst = [c.sb("wst%d" % i, [128, 2048], F32) for i in range(2)]
        self.wbf = [c.sb("wbf%d" % i, [128, 2048], BF16) for i in range(2)]
        self.wi = 0

    def load_consts(self):
        c = self.ctx
        nc = self.nc
        c.dma("sp", self.cst[:, :], self.cst_in[:, :], writes=[self.cst])
        c.op("dve", lambda: nc.vector.tensor_copy(self.cstb[:, :], self.cst[:, :]), reads=[self.cst], writes=[self.cstb])
        c.op("dve", lambda: nc.vector.memset(self.epsc[:, :], EPS), writes=[self.epsc])
        c.op("dve", lambda: nc.vector.memset(self.onec[:, :], 1.0), writes=[self.onec])
        c.dma("sp", self.pv[:, :], self.pv_in[:, :], writes=[self.pv])
        self.gsc = c.sb("gsc", [128, 8], F32)
        c.op("dve", lambda: nc.vector.memset(self.gsc[:, 5:6], 128.0 ** -0.5), writes=[self.gsc])
        c.op("dve", lambda: nc.vector.memset(self.gsc[:, 6:7], 1.0), writes=[self.gsc])
        c.op("dve", lambda: nc.vector.tensor_scalar(out=self.gsc[:, 0:4], in0=self.pv[:, PV_XQ:PV_XQ + 4], scalar1=1.0 / 16.0, scalar2=None, op0=ALU.mult),
             reads=[self.pv], writes=[self.gsc])
        c.op("dve", lambda: nc.vector.tensor_scalar(out=self.gsc[:, 4:5], in0=self.pv[:, PV_SBQ:PV_SBQ + 1], scalar1=128.0 ** -0.5, scalar2=None, op0=ALU.mult),
             reads=[self.pv], writes=[self.gsc])

    def ident_b(self):
        return self.cstb[:, C_ID:C_ID + 128]

    def w_fetch(self, src_ap):
        c = self.ctx
        s = self.wi
        self.wi ^= 1
        c.dma("sp", self.wst[s][:, :], src_ap, writes=[self.wst[s]])
        return s

    def w_cast(self, s, dst=None, dst_ap=None):
        c = self.ctx
        nc = self.nc
        if dst is None:
            dst = self.wbf[s]
            dst_ap = dst[:, :]
        c.op("pool", lambda: nc.gpsimd.tensor_copy(dst_ap, self.wst[s][:, :]),
             reads=[self.wst[s]], writes=[dst])
        return dst

    def norm_to_T(self, sc, src, nrows, g_off, dstT, add=None, xout=None):
        c = self.ctx
        nc = self.nc
        gB = sc.sb("gB", [128, D], F32)
        c.dma("sp", gB[:, :], self.gB_in[:, g_off:g_off + D], writes=[gB])
        xb = [sc.sb("xb%d" % i, [128, D], F32) for i in range(2)]
        ab = [sc.sb("ab%d" % i, [128, D], F32) for i in range(2)] if add is not None else None
        hb = [sc.sb("hb%d" % i, [128, D], BF16) for i in range(2)]
        junk = sc.sb("junk", [128, D], BF16)
        ss = [sc.sb("ss%d" % i, [128, 1], F32) for i in range(2)]
        rs = [sc.sb("rs%d" % i, [128, 1], F32) for i in range(2)]
        pt = [sc.ps("ptr%d" % i, [128, 1024], BF16) for i in range(2)]
        ident = self.ident_b()
        nt = nrows // 128
        pi = 0
        for tt in range(nt):
            b = tt % 2
            x = xb[b]
            r0 = tt * 128
            if isinstance(src, tuple):
                half = D // 2
                for hf in range(2):
                    c.dma("sp", x[:, hf * half:(hf + 1) * half], src[hf][r0:r0 + 128, :], writes=[x])
            else:
                c.dma("sp", x[:, :], src[r0:r0 + 128, :], writes=[x])
            if add is not None:
                a = ab[b]
                c.dma("sp", a[:, :], add[r0:r0 + 128, :], writes=[a])
                c.op("pool", lambda: nc.gpsimd.tensor_tensor(x[:, :], x[:, :], a[:, :], ALU.add), reads=[x, a], writes=[x])
                if xout is not None:
                    c.dma("sp", xout[r0:r0 + 128, :], x[:, :], reads=[x])
            c.op("act", lambda: nc.scalar.activation(out=junk[:, :], in_=x[:, :], func=AF.Square, accum_out=ss[b][:, 0:1]),
                 reads=[x], writes=[junk, ss[b]], multi=True)
            c.op("act", lambda: nc.scalar.activation(out=rs[b][:, :], in_=ss[b][:, :], func=AF.Sqrt, bias=self.epsc[:, 0:1], scale=1.0 / D),
                 reads=[ss[b], self.epsc], writes=[rs[b]])
            c.op("dve", lambda: nc.vector.reciprocal(rs[b][:, :], rs[b][:, :]), reads=[rs[b]], writes=[rs[b]])
            h = hb[b]
            c.op("dve", lambda: nc.vector.scalar_tensor_tensor(out=h[:, :], in0=x[:, :], scalar=rs[b][:, 0:1], in1=gB[:, :], op0=ALU.mult, op1=ALU.mult),
                 reads=[x, rs[b], gB], writes=[h])
            for g8 in range(2):
                p = pt[pi]
                pi ^= 1
                c.group("pe", [(lambda j=j: nc.tensor.transpose(p[:, j * 128:(j + 1) * 128], h[:, (g8 * 8 + j) * 128:(g8 * 8 + j + 1) * 128], ident))
                               for j in range(8)], reads=[h, self.cstb], writes=[p])
                c.op("act", lambda: nc.scalar.copy(out=dstT[:, g8 * 8:(g8 + 1) * 8, r0:r0 + 128],
                                                   in_=p[:, :].rearrange("p (a b) -> p a b", a=8)),
                     reads=[p], writes=[dstT])

    def proj_block(self, wtile, rhsT, ntok, psb, evac):
        c = self.ctx
        nc = self.nc
        nch = (ntok + 511) // 512
        for tc in range(nch):
            n = min(512, ntok - tc * 512)
            p = psb[self.pj]
            self.pj ^= 1
            c.group("pe", [(lambda kt=kt: nc.tensor.matmul(p[:, 0:n], wtile[:, kt * 128:(kt + 1) * 128], rhsT[:, kt, tc * 512:tc * 512 + n],
                                                          start=(kt == 0), stop=(kt == KT - 1))) for kt in range(KT)],
                    reads=[wtile, rhsT], writes=[p])
            evac(tc, p, n)


NPV = 256
GS_XQ, GS_SBQ = 0, 4
PV_XQ, PV_XK, PV_DNO, PV_SBQ, PV_SBK, PV_CONV, PV_ALOG, PV_DTB = 0, 4, 8, 9, 10, 16, 208, 232
NH = 24
NXA = 4
NYB = 32


def make_pv(inp):
    pv = np.zeros((128, NPV), np.float32)
    for li in range(2):
        for j in range(2):
            pv[:, PV_XQ + li * 2 + j] = inp["xa_q_norm_g"][li][j * 128:(j + 1) * 128]
            pv[:, PV_XK + li * 2 + j] = inp["xa_k_norm_g"][li][j * 128:(j + 1) * 128]
    pv[:, PV_DNO] = inp["dn_out_norm_g"][0]
    pv[:, PV_SBQ] = inp["sb_q_norm_g"][0]
    pv[:, PV_SBK] = inp["sb_k_norm_g"][0]
    cw = inp["dn_conv_w"][0]
    pv[:, PV_CONV:PV_CONV + 192] = cw.reshape(4, 48, 128).transpose(2, 1, 0).reshape(128, 192)
    pv[:, PV_ALOG:PV_ALOG + 24] = inp["dn_a_log"][0][None, :]
    pv[:, PV_DTB:PV_DTB + 24] = inp["dn_dt_bias"][0][None, :]
    return pv


def lhsT_blocks(W, cols):
    out = np.empty((len(cols), 128, 2048), np.float32)
    for i, c0 in enumerate(cols):
        out[i] = W[:, c0:c0 + 128].reshape(16, 128, 128).transpose(1, 0, 2).reshape(128, 2048)
    return out


def rhs_pieces(W, c0, n):
    kpp = 2048 // n
    A = W[:, c0:c0 + n].reshape(16, 128, n).transpose(1, 0, 2)
    A = A.reshape(128, 16 // kpp, kpp * n).transpose(1, 0, 2)
    return np.ascontiguousarray(A)


class Layers(Prog):
    def __init__(self, nc, dbg=None, nh=NH, nxa=NXA):
        super().__init__(nc, dbg)
        c = self.ctx
        self.nh = nh
        self.nxa = nxa
        self.yT_d = c.dram("yT_d", [nh + 2 * nxa, 128, T], BF16)
        self.x1_d = c.dram("x1_d", [T, D], F32)
        self.vtok_d = c.dram("vtok_d", [16, 128, nh * 128], BF16)
        self.pj = 0

    def fnorm(self, sc_bufs, raws, gcols, outs, ntok, scale_dim):
        c = self.ctx
        nc = self.nc
        sq, ssum, rstd = sc_bufs
        onesf = self.cst[:, C_ONES:C_ONES + 128]
        nb = len(raws)
        for tc in range((ntok + 511) // 512):
            n = min(512, ntok - tc * 512)
            sl = slice(tc * 512, tc * 512 + n)
            for j in range(nb):
                c.op("act", lambda: nc.scalar.activation(out=sq[j][:, 0:n], in_=raws[j][:, sl], func=AF.Square),
                     reads=[raws[j]], writes=[sq[j]])
            c.group("pe", [(lambda j=j: nc.tensor.matmul(ssum[:, 0:n], onesf, sq[j][:, 0:n], start=(j == 0), stop=(j == nb - 1)))
                           for j in range(nb)], reads=[self.cst] + [sq[j] for j in range(nb)], writes=[ssum])
            c.op("act", lambda: nc.scalar.activation(out=rstd[:, 0:n], in_=ssum[:, 0:n], func=AF.Sqrt, bias=self.epsc[:, 0:1], scale=1.0 / scale_dim),
                 reads=[ssum, self.epsc], writes=[rstd])
            c.op("dve", lambda: nc.vector.reciprocal(rstd[:, 0:n], rstd[:, 0:n]), reads=[rstd], writes=[rstd])
            for j in range(nb):
                c.op("dve", lambda: nc.vector.scalar_tensor_tensor(out=outs[j][:, sl], in0=raws[j][:, sl], scalar=gcols[j], in1=rstd[:, 0:n],
                                                                   op0=ALU.mult, op1=ALU.mult),
                     reads=[raws[j], rstd, self.gsc], writes=[outs[j]])

    def wblocks(self, w_d, idxs):
        slots = [self.w_fetch(w_d[idxs[0], :, :])]
        for n, i in enumerate(idxs):
            if n + 1 < len(idxs):
                slots.append(self.w_fetch(w_d[idxs[n + 1], :, :]))
            yield i, self.w_cast(slots[n])

    def xa_prep(self, sc, li, wk_d, wv_d, psb):
        c = self.ctx
        nc = self.nc
        kxT = sc.sb("kxT", [128, 2 * self.nxa, NMEM], BF16)
        vx = sc.sb("vx", [128, 2, 256 * self.nxa], BF16)
        with Scope(c) as s2:
            kraw = [s2.sb("kraw%d" % j, [128, NMEM], F32) for j in range(2)]
            sq = [s2.sb("ksq%d" % j, [128, 512], F32) for j in range(2)]
            rstd = s2.sb("krstd", [128, 512], F32)
            ssum = s2.ps("kssum", [128, 512], F32)
            wvb = s2.sb("wvb", [128, 8192], BF16)
            it = self.wblocks(wk_d, list(range(2 * self.nxa)))
            for a in range(self.nxa):
                for j in range(2):
                    _, wt = next(it)

                    def evac(tc, p, n, j=j):
                        c.op("act", lambda: nc.scalar.copy(out=kraw[j][:, 0:n], in_=p[:, 0:n]), reads=[p], writes=[kraw[j]])
                    self.proj_block(wt, self.memT, NMEM, psb, evac)
                outs = [view("kx%d" % (a * 2 + j), kxT[:, a * 2 + j, :]) for j in range(2)]
                self.fnorm((sq, ssum, rstd), kraw, [self.pv[:, PV_XK + li * 2 + j:PV_XK + li * 2 + j + 1] for j in range(2)], outs, NMEM, 256.0)
            c.barrier()
            for g in range(self.nxa // 2):
                for pc in range(4):
                    s = self.w_fetch(wv_d[g * 4 + pc, :, :])
                    self.w_cast(s, wvb, wvb[:, pc * 2048:(pc + 1) * 2048])
                for mt in range(2):
                    p = psb[self.pj]
                    self.pj ^= 1
                    c.group("pe", [(lambda kt=kt: nc.tensor.matmul(p[:, 0:512], self.memT[:, kt, mt * 128:(mt + 1) * 128], wvb[:, kt * 512:(kt + 1) * 512],
                                                                  start=(kt == 0), stop=(kt == KT - 1))) for kt in range(KT)],
                            reads=[self.memT, wvb], writes=[p])
                    c.op("act", lambda: nc.scalar.copy(out=vx[:, mt, g * 512:(g + 1) * 512], in_=p[:, 0:512]), reads=[p], writes=[vx])
            c.barrier()
        return kxT, vx

    def xa_heads(self, li, hT, w_d, xq_idx, z_idx, kxT, vx, psb):
        c = self.ctx
        nc = self.nc
        with Scope(c) as sc:
            qraw = [sc.sb("xqraw%d" % j, [128, T], F32) for j in range(2)]
            qn = [sc.sb("xqn%d" % j, [128, T], BF16) for j in range(2)]
            sz = [sc.sb("xsz%d" % j, [128, T], BF16) for j in range(2)]
            yb = [sc.sb("xyb%d" % j, [128, T], BF16) for j in range(2)]
            sq = [sc.sb("xsq%d" % j, [128, 512], F32) for j in range(2)]
            rstd = sc.sb("xrstd", [128, 512], F32)
            pe_ = [sc.sb("xp%d" % j, [128, 512], BF16) for j in range(2)]
            rden = sc.sb("xrden", [128, 512], F32)
            t1 = sc.sb("xt1", [128, 512], F32)
            ssum = sc.ps("xssum", [128, 512], F32)
            sps = [sc.ps("xs%d" % j, [128, 512], F32) for j in range(2)]
            den = sc.ps("xden", [128, 512], F32)
            ops = sc.ps("xo", [128, 512], F32)
            onesb = self.cstb[:, C_ONES:C_ONES + 128]
            for a in range(self.nxa):
                idxs = [xq_idx[a * 2], xq_idx[a * 2 + 1], z_idx[a * 2], z_idx[a * 2 + 1]]
                it = self.wblocks(w_d, idxs)
                for j in range(2):
                    _, wt = next(it)

                    def evac(tc, p, n, j=j):
                        c.op("act", lambda: nc.scalar.copy(out=qraw[j][:, tc * 512:tc * 512 + n], in_=p[:, 0:n]), reads=[p], writes=[qraw[j]])
                    self.proj_block(wt, hT, T, psb, evac)
                for j in range(2):
                    _, wt = next(it)

                    def evac(tc, p, n, j=j):
                        c.op("act", lambda: nc.scalar.activation(out=sz[j][:, tc * 512:tc * 512 + n], in_=p[:, 0:n], func=AF.Silu), reads=[p], writes=[sz[j]])
                    self.proj_block(wt, hT, T, psb, evac)
                self.fnorm((sq, ssum, rstd), qraw, [self.gsc[:, GS_XQ + li * 2 + j:GS_XQ + li * 2 + j + 1] for j in range(2)], qn, T, 256.0)
                for tc in range(4):
                    sl = slice(tc * 512, (tc + 1) * 512)
                    for mt in range(2):
                        c.group("pe", [(lambda j=j: nc.tensor.matmul(sps[mt][:, :], kxT[:, a * 2 + j, mt * 128:(mt + 1) * 128], qn[j][:, sl],
                                                                    start=(j == 0), stop=(j == 1))) for j in range(2)],
                                reads=[kxT, qn[0], qn[1]], writes=[sps[mt]])
                        c.op("act", lambda: nc.scalar.activation(out=pe_[mt][:, :], in_=sps[mt][:, :], func=AF.Exp), reads=[sps[mt]], writes=[pe_[mt]])
                    c.group("pe", [(lambda mt=mt: nc.tensor.matmul(den[:, :], onesb, pe_[mt][:, :], start=(mt == 0), stop=(mt == 1))) for mt in range(2)],
                            reads=[self.cstb, pe_[0], pe_[1]], writes=[den])
                    c.op("dve", lambda: nc.vector.reciprocal(rden[:, :], den[:, :]), reads=[den], writes=[rden])
                    for eb in range(2):
                        c.group("pe", [(lambda mt=mt: nc.tensor.matmul(ops[:, :], vx[:, mt, a * 256 + eb * 128:a * 256 + (eb + 1) * 128], pe_[mt][:, :],
                                                                      start=(mt == 0), stop=(mt == 1))) for mt in range(2)],
                                reads=[vx, pe_[0], pe_[1]], writes=[ops])
                        c.op("dve", lambda: nc.vector.tensor_tensor(t1[:, :], ops[:, :], rden[:, :], ALU.mult), reads=[ops, rden], writes=[t1])
                        c.op("pool", lambda: nc.gpsimd.tensor_tensor(yb[eb][:, sl], t1[:, :], sz[eb][:, sl], ALU.mult), reads=[t1, sz[eb]], writes=[yb[eb]])
                for eb in range(2):
                    c.dma("sp", self.yT_d[self.nh + a * 2 + eb, :, :], yb[eb][:, :], reads=[yb[eb]])
            c.barrier()

    def out_proj(self, wo_d, x_src, x_dst):
        c = self.ctx
        nc = self.nc
        with Scope(c) as sc:
            wo = sc.sb("wo", [128, NYB, D], BF16)
            yt = [sc.sb("yt%d" % i, [128, NYB, 128], BF16) for i in range(2)]
            xr = [sc.sb("xr%d" % i, [128, D], F32) for i in range(2)]
            psb = [sc.ps("op%d" % i, [128, 512], F32) for i in range(4)]
            for et in range(NYB):
                s = self.w_fetch(wo_d[et * 128:(et + 1) * 128, :])
                eng = ("pool", "act", "dve")[et % 3]
                dst_ap = wo[:, et, :]
                src_ap = self.wst[s][:, :]
                if eng == "pool":
                    c.op("pool", lambda: nc.gpsimd.tensor_copy(dst_ap, src_ap), reads=[self.wst[s]], writes=[wo])
                elif eng == "act":
                    c.op("act", lambda: nc.scalar.copy(out=dst_ap, in_=src_ap), reads=[self.wst[s]], writes=[wo])
                else:
                    c.op("dve", lambda: nc.vector.tensor_copy(dst_ap, src_ap), reads=[self.wst[s]], writes=[wo])
            pi = 0

            def loads(tt):
                c.dma("sp", yt[tt % 2][:, :, :], self.yT_d[:, :, tt * 128:(tt + 1) * 128].rearrange("e p t -> p e t"), writes=[yt[tt % 2]])
                c.dma("sp", xr[tt % 2][:, :], x_src[tt * 128:(tt + 1) * 128, :], writes=[xr[tt % 2]])
            loads(0)
            for tt in range(T // 128):
                y = yt[tt % 2]
                x = xr[tt % 2]
                if tt + 1 < T // 128 and tt >= 1:
                    loads(tt + 1)
                for ch in range(4):
                    p = psb[pi]
                    pi = (pi + 1) % 4
                    c.group("pe", [(lambda et=et: nc.tensor.matmul(p[:, :], y[:, et, :], wo[:, et, ch * 512:(ch + 1) * 512],
                                                                  start=(et == 0), stop=(et == NYB - 1))) for et in range(NYB)],
                            reads=[y, wo], writes=[p])
                    c.op("dve", lambda: nc.vector.tensor_tensor(x[:, ch * 512:(ch + 1) * 512], p[:, :], x[:, ch * 512:(ch + 1) * 512], ALU.add),
                         reads=[p, x], writes=[x])
                c.dma("sp", x_dst[tt * 128:(tt + 1) * 128, :], x[:, :], reads=[x])
                if tt == 0 and T // 128 > 1:
                    loads(1)
            c.barrier()

    def sb_heads(self, hT, w_d, q_idx, k_idx, z_idx, wv_d, psb, heads=None):
        c = self.ctx
        nc = self.nc
        heads = list(range(NH)) if heads is None else heads
        with Scope(c) as sc:
            qraw = sc.sb("qraw", [128, T], F32)
            kraw = sc.sb("kraw", [128, T], F32)
            qT = sc.sb("qT", [128, T], BF16)
            kT = sc.sb("kT", [128, T], BF16)
            sz = sc.sb("sz", [128, T], BF16)
            yb = [sc.sb("yb%d" % i, [128, T], BF16) for i in range(2)]
            vtok = sc.sb("vtok", [128, 16, 512], BF16)
            wvb = sc.sb("wvb", [128, 8192], BF16)
            sq = [sc.sb("sq0", [128, 512], F32)]
            rstd = sc.sb("rstd", [128, 512], F32)
            Eb = [sc.sb("E%d" % i, [128, 512], F32) for i in range(2)]
            spb = [sc.sb("sp%d" % i, [128, 512], F32) for i in range(2)]
            Lhi = [sc.sb("Lhi%d" % i, [128, 512], BF16) for i in range(2)]
            Llo = [sc.sb("Llo%d" % i, [128, 512], BF16) for i in range(2)]
            zms = [sc.sb("zms%d" % i, [128, 512], F32) for i in range(2)]
            wb = [sc.sb("wb%d" % i, [128, 512], BF16) for i in range(2)]
            ssum = sc.ps("ssum", [128, 512], F32)
            zps = [sc.ps("zps%d" % i, [128, 512], F32) for i in range(2)]
            Pps = sc.ps("Pps", [128, 512], F32)
            Ops = sc.ps("Ops", [128, 512], F32)
            Mlow = self.cstb[:, C_LOW:C_LOW + 128]
            Mtri = self.cstb[:, C_TRI:C_TRI + 128]
            strict = self.cst[:, C_STRICT:C_STRICT + 128]
            negs = self.cst[:, C_NEGS:C_NEGS + 128]
            if len(heads) < NH:
                c.op("dve", lambda: nc.vector.memset(yb[0][:, :], 0.0), writes=[yb[0]])
                for h in range(NH):
                    if h not in heads:
                        c.dma("sp", self.yT_d[h, :, :], yb[0][:, :], reads=[yb[0]])
            for hi, h in enumerate(heads):
                if hi % 4 == 0:
                    g = h // 4
                    for pc in range(4):
                        s = self.w_fetch(wv_d[g * 4 + pc, :, :])
                        self.w_cast(s, wvb, wvb[:, pc * 2048:(pc + 1) * 2048])
                    for tt in range(16):
                        p = psb[self.pj]
                        self.pj ^= 1
                        c.group("pe", [(lambda kt=kt: nc.tensor.matmul(p[:, 0:512], hT[:, kt, tt * 128:(tt + 1) * 128], wvb[:, kt * 512:(kt + 1) * 512],
                                                                      start=(kt == 0), stop=(kt == KT - 1))) for kt in range(KT)],
                                reads=[hT, wvb], writes=[p])
                        c.op("act", lambda: nc.scalar.copy(out=vtok[:, tt, :], in_=p[:, 0:512]), reads=[p], writes=[vtok])
                hh = h % 4
                it = self.wblocks(w_d, [z_idx[h], q_idx[h], k_idx[h]])
                _, wt = next(it)

                def evz(tc, p, n):
                    c.op("act", lambda: nc.scalar.activation(out=sz[:, tc * 512:tc * 512 + n], in_=p[:, 0:n], func=AF.Silu), reads=[p], writes=[sz])
                self.proj_block(wt, hT, T, psb, evz)
                _, wt = next(it)

                def evq(tc, p, n):
                    c.op("act", lambda: nc.scalar.copy(out=qraw[:, tc * 512:tc * 512 + n], in_=p[:, 0:n]), reads=[p], writes=[qraw])
                self.proj_block(wt, hT, T, psb, evq)
                _, wt = next(it)

                def evk(tc, p, n):
                    c.op("act", lambda: nc.scalar.copy(out=kraw[:, tc * 512:tc * 512 + n], in_=p[:, 0:n]), reads=[p], writes=[kraw])
                self.proj_block(wt, hT, T, psb, evk)
                self.fnorm((sq, ssum, rstd), [qraw], [self.gsc[:, GS_SBQ:GS_SBQ + 1]], [qT], T, 128.0)
                self.fnorm((sq, ssum, rstd), [kraw], [self.pv[:, PV_SBK:PV_SBK + 1]], [kT], T, 128.0)
                y = yb[hi % 2]
                blocks = [(qc, kb) for qc in range(4) for kb in range(4 * qc + 3, -1, -1)]

                def stageA(i):
                    qc, kb = blocks[i]
                    b = i % 2
                    lo = max(0, kb * 128 - qc * 512)
                    diag = kb >= 4 * qc
                    z = zps[b]
                    c.op("pe", lambda: nc.tensor.matmul(z[:, lo:512], kT[:, kb * 128:(kb + 1) * 128], qT[:, qc * 512 + lo:(qc + 1) * 512], start=True, stop=True),
                         reads=[kT, qT], writes=[z])
                    c.op("act", lambda: nc.scalar.activation(out=Eb[b][:, lo:512], in_=z[:, lo:512], func=AF.Exp), reads=[z], writes=[Eb[b]])
                    c.op("act", lambda: nc.scalar.activation(out=spb[b][:, lo:512], in_=Eb[b][:, lo:512], func=AF.Ln, bias=self.onec[:, 0:1], scale=1.0),
                         reads=[Eb[b], self.onec], writes=[spb[b]])
                    if diag:
                        c.op("pool", lambda: nc.gpsimd.tensor_tensor(spb[b][:, lo:lo + 128], spb[b][:, lo:lo + 128], strict, ALU.mult),
                             reads=[spb[b], self.cst], writes=[spb[b]])
                    c.op("pool", lambda: nc.gpsimd.tensor_scalar(out=Lhi[b][:, lo:512], in0=spb[b][:, lo:512], scalar1=-1.0, scalar2=None, op0=ALU.mult),
                         reads=[spb[b]], writes=[Lhi[b]])
                    c.op("dve", lambda: nc.vector.scalar_tensor_tensor(out=Llo[b][:, lo:512], in0=spb[b][:, lo:512], scalar=-1.0, in1=Lhi[b][:, lo:512],
                                                                       op0=ALU.mult, op1=ALU.subtract),
                         reads=[spb[b], Lhi[b]], writes=[Llo[b]])
                    c.op("dve", lambda: nc.vector.tensor_tensor(zms[b][:, lo:512], z[:, lo:512], spb[b][:, lo:512], ALU.subtract),
                         reads=[z, spb[b]], writes=[zms[b]])
                    if diag:
                        c.op("pool", lambda: nc.gpsimd.tensor_tensor(zms[b][:, lo:lo + 128], zms[b][:, lo:lo + 128], negs, ALU.add),
                             reads=[zms[b], self.cst], writes=[zms[b]])

                def stageB(i):
                    qc, kb = blocks[i]
                    b = i % 2
                    lo = max(0, kb * 128 - qc * 512)
                    first = kb == 4 * qc + 3
                    last = kb == 0
                    c.group("pe", [lambda: nc.tensor.matmul(Pps[:, lo:512], Mlow, Lhi[b][:, lo:512], start=first, stop=False),
                                   lambda: nc.tensor.matmul(Pps[:, lo:512], Mlow, Llo[b][:, lo:512], start=False, stop=last)],
                            reads=[self.cstb, Lhi[b], Llo[b]], writes=[Pps])
                    c.op("dve", lambda: nc.vector.tensor_tensor(zms[b][:, lo:512], Pps[:, lo:512], zms[b][:, lo:512], ALU.add),
                         reads=[Pps, zms[b]], writes=[zms[b]])
                    c.op("act", lambda: nc.scalar.activation(out=wb[b][:, lo:512], in_=zms[b][:, lo:512], func=AF.Exp), reads=[zms[b]], writes=[wb[b]])
                    if not last:
                        c.group("pe", [lambda: nc.tensor.matmul(Pps[:, lo:512], Mtri, Lhi[b][:, lo:512], start=False, stop=False),
                                       lambda: nc.tensor.matmul(Pps[:, lo:512], Mtri, Llo[b][:, lo:512], start=False, stop=False)],
                                reads=[self.cstb, Lhi[b], Llo[b]], writes=[Pps])
                    c.op("pe", lambda: nc.tensor.matmul(Ops[:, lo:512], vtok[:, kb, hh * 128:(hh + 1) * 128], wb[b][:, lo:512], start=first, stop=last),
                         reads=[vtok, wb[b]], writes=[Ops])
                    if last:
                        c.op("dve", lambda: nc.vector.tensor_tensor(y[:, qc * 512:(qc + 1) * 512], Ops[:, :], sz[:, qc * 512:(qc + 1) * 512], ALU.mult),
                             reads=[Ops, sz], writes=[y])

                for i in range(len(blocks) + 1):
                    if i < len(blocks):
                        stageA(i)
                    if i >= 1:
                        stageB(i - 1)
                c.dma("sp", self.yT_d[h, :, :], y[:, :], reads=[y])
            c.barrier()

    def layer(self, li, kind, W, x_src, x_dst, add=None, xout=None, first=False):
        c = self.ctx
        with Scope(c) as sA:
            hT = sA.sb("hT", [128, KT, T], BF16)
            with Scope(c) as s1:
                if first:
                    self.norm_to_T(s1, self.mem_in, NMEM, 2 * D, self.memT)
            c.barrier()
            with Scope(c) as s1:
                self.norm_to_T(s1, x_src, T, li * D, hT, add=add, xout=xout)
                c.barrier()
            with Scope(c) as s2:
                psb = [s2.ps("pj%d" % i, [128, 512], F32) for i in range(2)]
                with Scope(c) as sx:
                    kxT, vx = self.xa_prep(sx, li, W["wk"], W["wv"], psb)
                    self.xa_heads2(li, hT, W["w"], W["xq_idx"], W["zx_idx"], kxT, vx, psb)
                    c.barrier()
                if kind == "sb":
                    if self.dbg.get("v1"):
                        self.sb_heads(hT, W["w"], W["q_idx"], W["k_idx"], W["z_idx"], W["wvm"], psb, heads=self.dbg.get("heads"))
                    else:
                        self.sb_heads2(hT, W, psb)
                elif self.dbg.get("v1"):
                    self.dn_heads(hT, W, psb)
                elif self.dbg.get("v2"):
                    self.dn_heads2(hT, W, psb)
                else:
                    self.dn_heads3(hT, W, psb)
                c.barrier()
        xs = xout if xout is not None else x_src
        self.out_proj(W["wo"], xs, x_dst)


def sb_weights_host(inp):
    W = inp["sb_w_in"][0]
    cols = [h * 128 for h in range(24)] + [3072 + h * 128 for h in range(24)] + [9216 + j * 128 for j in range(8)] + [10240 + j * 128 for j in range(32)]
    w = lhsT_blocks(W, cols)
    wvm = np.concatenate([rhs_pieces(W, 6144 + g * 512, 512) for g in range(6)], 0)
    return {"w1": w, "wvm1": wvm}


def kv_weights_host(inp, li):
    Wkv = inp["mem_w_kv"][li]
    wk = lhsT_blocks(Wkv, [j * 128 for j in range(8)])
    wv = np.concatenate([rhs_pieces(Wkv, 1024 + g * 512, 512) for g in range(2)], 0)
    return {"wk%d" % li: wk, "wv%d" % li: wv}


def sb_decl(P, li=1):
    c = P.ctx
    return {
        "w": c.dram("w%d" % li, [88, 128, 2048], F32, "ExternalInput"),
        "wvm": c.dram("wvm%d" % li, [24, 128, 2048], F32, "ExternalInput"),
        "wk": c.dram("wk%d" % li, [8, 128, 2048], F32, "ExternalInput"),
        "wv": c.dram("wv%d" % li, [8, 128, 2048], F32, "ExternalInput"),
        "wo": c.dram("wo%d" % li, [4096, 2048], F32, "ExternalInput"),
        "q_idx": list(range(24)), "k_idx": list(range(24, 48)), "xq_idx": list(range(48, 56)),
        "z_idx": list(range(56, 80)), "zx_idx": list(range(80, 88)),
    }


def dn_weights_host(inp):
    W = inp["dn_w_in"][0]
    cols = [h * 128 for h in range(12)] + [1536 + h * 128 for h in range(12)] + [3072 + h * 128 for h in range(24)] \
        + [6192 + j * 128 for j in range(8)] + [7216 + j * 128 for j in range(32)]
    w = lhsT_blocks(W, cols)
    wab = np.ascontiguousarray(W[:, 6144:6192].reshape(16, 128, 48).transpose(1, 0, 2).reshape(128, 768))
    return {"w0": w, "wab0": wab}


def dn_decl(P, li=0):
    c = P.ctx
    return {
        "w": c.dram("w%d" % li, [88, 128, 2048], F32, "ExternalInput"),
        "wab": c.dram("wab%d" % li, [128, 768], F32, "ExternalInput"),
        "wk": c.dram("wk%d" % li, [8, 128, 2048], F32, "ExternalInput"),
        "wv": c.dram("wv%d" % li, [8, 128, 2048], F32, "ExternalInput"),
        "wo": c.dram("wo%d" % li, [4096, 2048], F32, "ExternalInput"),
        "q_idx": list(range(12)), "k_idx": list(range(12, 24)), "v_idx": list(range(24, 48)), "xq_idx": list(range(48, 56)),
        "z_idx": list(range(56, 80)), "zx_idx": list(range(80, 88)),
    }


def dn_heads(self, hT, W, psb):
    c = self.ctx
    nc = self.nc
    pairs = self.dbg.get("pairs")
    pairs = list(range(12)) if pairs is None else pairs
    NC_ = T // 128
    with Scope(c) as sc:
        onesf = self.cst[:, C_ONES:C_ONES + 128]
        trif = self.cst[:, C_TRI:C_TRI + 128]
        negi = self.cst[:, C_NEGI:C_NEGI + 128]
        strictf = self.cst[:, C_STRICT:C_STRICT + 128]
        identf = self.cst[:, C_ID:C_ID + 128]
        identb = self.cstb[:, C_ID:C_ID + 128]
        g_all = sc.sb("g_all", [128, NC_, 24], F32)
        bt_all = sc.sb("bt_all", [128, NC_, 24], F32)
        nbt_all = sc.sb("nbt_all", [128, NC_, 24], F32)
        gc_all = sc.sb("gc_all", [128, NC_, 24], F32)
        egc_all = sc.sb("egc_all", [128, NC_, 24], F32)
        ekd_all = sc.sb("ekd_all", [128, NC_, 24], F32)
        ecd_all = sc.sb("ecd_all", [128, NC_, 24], F32)
        negA = sc.sb("negA", [128, 24], F32)
        wabf = sc.sb("wabf", [128, 768], F32)
        wabb = sc.sb("wabb", [128, 768], BF16)
        gtmp = sc.sb("gtmp", [128, 24], F32)
        bankA = sc.ps("bankA", [128, 512], F32)
        bankB = sc.ps("bankB", [128, 512], F32)
        bankC = sc.ps("bankC", [128, 512], F32)
        bankD = sc.ps("bankD", [128, 512], F32)
        bankE = sc.ps("bankE", [128, 1024], BF16)
        ssum = sc.ps("ssum", [128, 512], F32)
        c.dma("sp", wabf[:, :], W["wab"][:, :], writes=[wabf])
        c.op("dve", lambda: nc.vector.tensor_copy(wabb[:, :], wabf[:, :]), reads=[wabf], writes=[wabb])
        c.op("act", lambda: nc.scalar.activation(out=negA[:, :], in_=self.pv[:, PV_ALOG:PV_ALOG + 24], func=AF.Exp), reads=[self.pv], writes=[negA])
        c.op("dve", lambda: nc.vector.tensor_scalar(out=negA[:, :], in0=negA[:, :], scalar1=-1.0, scalar2=None, op0=ALU.mult), reads=[negA], writes=[negA])
        abp = view("abp", bankA[:, 0:48], bankA)
        gcp = view("gcp", bankB[:, 0:24], bankB)
        glp = view("glp", bankC[:, 0:24], bankC)
        for tt in range(NC_):
            c.group("pe", [(lambda kt=kt: nc.tensor.matmul(abp[:, :], hT[:, kt, tt * 128:(tt + 1) * 128], wabb[:, kt * 48:(kt + 1) * 48],
                                                          start=(kt == 0), stop=(kt == KT - 1))) for kt in range(KT)],
                    reads=[hT, wabb], writes=[abp])
            c.op("dve", lambda: nc.vector.tensor_tensor(gtmp[:, :], abp[:, 0:24], self.pv[:, PV_DTB:PV_DTB + 24], ALU.add), reads=[abp, self.pv], writes=[gtmp])
            c.op("act", lambda: nc.scalar.activation(out=gtmp[:, :], in_=gtmp[:, :], func=AF.Exp), reads=[gtmp], writes=[gtmp])
            c.op("act", lambda: nc.scalar.activation(out=gtmp[:, :], in_=gtmp[:, :], func=AF.Ln, bias=self.onec[:, 0:1], scale=1.0), reads=[gtmp, self.onec], writes=[gtmp])
            c.op("dve", lambda: nc.vector.tensor_tensor(g_all[:, tt, :], gtmp[:, :], negA[:, :], ALU.mult), reads=[gtmp, negA], writes=[g_all])
            c.op("act", lambda: nc.scalar.activation(out=bt_all[:, tt, :], in_=abp[:, 24:48], func=AF.Sigmoid), reads=[abp], writes=[bt_all])
            c.op("pool", lambda: nc.gpsimd.tensor_scalar(out=nbt_all[:, tt, :], in0=bt_all[:, tt, :], scalar1=-1.0, scalar2=None, op0=ALU.mult), reads=[bt_all], writes=[nbt_all])
            c.op("pe", lambda: nc.tensor.matmul(gcp[:, :], trif, g_all[:, tt, :], start=True, stop=True), reads=[self.cst, g_all], writes=[gcp])
            c.op("pe", lambda: nc.tensor.matmul(glp[:, :], onesf, g_all[:, tt, :], start=True, stop=True), reads=[self.cst, g_all], writes=[glp])
            c.op("act", lambda: nc.scalar.copy(out=gc_all[:, tt, :], in_=gcp[:, :]), reads=[gcp], writes=[gc_all])
            c.op("act", lambda: nc.scalar.activation(out=egc_all[:, tt, :], in_=gcp[:, :], func=AF.Exp), reads=[gcp], writes=[egc_all])
            c.op("act", lambda: nc.scalar.activation(out=ecd_all[:, tt, :], in_=glp[:, :], func=AF.Exp), reads=[glp], writes=[ecd_all])
            c.op("dve", lambda: nc.vector.tensor_tensor(ekd_all[:, tt, :], glp[:, :], gc_all[:, tt, :], ALU.subtract), reads=[glp, gc_all], writes=[ekd_all])
            c.op("act", lambda: nc.scalar.activation(out=ekd_all[:, tt, :], in_=ekd_all[:, tt, :], func=AF.Exp), reads=[ekd_all], writes=[ekd_all])
        c.barrier()
        raw = sc.sb("raw", [128, T + 3], F32)
        acc = sc.sb("acc", [128, T], F32)
        sil = sc.sb("sil", [128, T], F32)
        qT = sc.sb("qT", [128, T], BF16)
        kT = sc.sb("kT", [128, T], BF16)
        vT = [sc.sb("vT%d" % i, [128, T], BF16) for i in range(2)]
        sz = [sc.sb("sz%d" % i, [128, T], BF16) for i in range(2)]
        yb = [sc.sb("yb%d" % i, [128, T], BF16) for i in range(2)]
        sq = [sc.sb("sq", [128, 512], F32)]
        rstd = sc.sb("rstd", [128, 512], F32)
        S = [sc.sb("S%d" % i, [128, 128], F32) for i in range(2)]
        Sb = [sc.sb("Sb%d" % i, [128, 128], BF16) for i in range(2)]
        c.op("dve", lambda: nc.vector.memset(raw[:, 0:3], 0.0), writes=[raw])

        def cb(name, dt, n=128):
            return [sc.sb("%s%d" % (name, i), [128, n], dt) for i in range(2)]
        rhsg, argT, DT, egB, tmp = cb("rhsg", F32), cb("argT", F32), cb("DT", F32), cb("egB", F32), cb("tmp", F32)
        XTb, Xb, PTb = cb("XTb", BF16), cb("Xb", BF16), cb("PTb", BF16)
        XT2, X2, PT2 = cb("XT2", BF16), cb("X2", BF16), cb("PT2", BF16)
        qkm, qd, rhsR, kd = cb("qkm", BF16), cb("qd", BF16), cb("rhsR", BF16, 256), cb("kd", BF16)
        u0, wsb, wT, ub, on = cb("u0", F32), cb("wsb", BF16), cb("wT", BF16), cb("ub", BF16), cb("on", BF16)
        oss, ors = cb("oss", F32, 1), cb("ors", F32, 1)
        junk = sc.sb("junk", [128, 128], BF16)
        gcB = [view("gcB%d" % i, bankA[:, i * 128:(i + 1) * 128], bankA) for i in range(2)]
        kkp = view("kkp", bankA[:, 256:384], bankA)
        qkp = view("qkp", bankA[:, 384:512], bankA)
        Xp = view("Xp", bankB[:, 0:128], bankB)
        PTp = view("PTp", bankB[:, 256:384], bankB)
        XTp = view("XTp", bankD[:, 128:256], bankD)
        Rp = view("Rp", bankC[:, 0:256], bankC)
        wSp = view("wSp", bankC[:, 256:384], bankC)
        op_ = view("op", bankC[:, 384:512], bankC)
        Snp = view("Snp", bankD[:, 0:128], bankD)
        tp = [view("tp%d" % i, bankE[:, i * 128:(i + 1) * 128], bankE) for i in range(8)]
        X0Tp, wTp, oTp, ktp, vtp = tp[0], tp[1], tp[2], tp[3], [tp[4], tp[5]]

        def conv_silu(cbi, out_tile, out_dt_is_f32):
            wc = [self.pv[:, PV_CONV + cbi * 4 + k:PV_CONV + cbi * 4 + k + 1] for k in range(4)]
            c.op("dve", lambda: nc.vector.tensor_scalar(out=acc[:, :], in0=raw[:, 3:T + 3], scalar1=wc[3], scalar2=None, op0=ALU.mult), reads=[raw, self.pv], writes=[acc])
            for k in (2, 1, 0):
                c.op("dve", lambda: nc.vector.scalar_tensor_tensor(out=acc[:, :], in0=raw[:, k:T + k], scalar=wc[k], in1=acc[:, :], op0=ALU.mult, op1=ALU.add),
                     reads=[raw, acc, self.pv], writes=[acc])
            c.op("act", lambda: nc.scalar.activation(out=out_tile[:, :], in_=acc[:, :], func=AF.Silu), reads=[acc], writes=[out_tile])

        def evraw(tc, p, n):
            c.op("act", lambda: nc.scalar.copy(out=raw[:, 3 + tc * 512:3 + tc * 512 + n], in_=p[:, 0:n]), reads=[p], writes=[raw])

        for hp in pairs:
            idxs = [W["q_idx"][hp], W["k_idx"][hp], W["v_idx"][2 * hp], W["v_idx"][2 * hp + 1], W["z_idx"][2 * hp], W["z_idx"][2 * hp + 1]]
            it = self.wblocks(W["w"], idxs)
            _, wt = next(it)
            self.proj_block(wt, hT, T, psb, evraw)
            conv_silu(hp, sil, True)
            self.fnorm((sq, ssum, rstd), [sil], [self.gsc[:, 5:6]], [qT], T, 1.0)
            _, wt = next(it)
            self.proj_block(wt, hT, T, psb, evraw)
            conv_silu(12 + hp, sil, True)
            self.fnorm((sq, ssum, rstd), [sil], [self.gsc[:, 6:7]], [kT], T, 1.0)
            for i in range(2):
                _, wt = next(it)
                self.proj_block(wt, hT, T, psb, evraw)
                conv_silu(24 + 2 * hp + i, vT[i], False)
            for i in range(2):
                _, wt = next(it)

                def evz(tc, p, n, i=i):
                    c.op("act", lambda: nc.scalar.activation(out=sz[i][:, tc * 512:tc * 512 + n], in_=p[:, 0:n], func=AF.Silu), reads=[p], writes=[sz[i]])
                self.proj_block(wt, hT, T, psb, evz)
            for i in range(2):
                c.op("dve", lambda: nc.vector.memset(S[i][:, :], 0.0), writes=[S[i]])
                c.op("dve", lambda: nc.vector.memset(Sb[i][:, :], 0.0), writes=[Sb[i]])
            for ci in range(NC_):
                csl = slice(ci * 128, (ci + 1) * 128)
                c.op("pe", lambda: nc.tensor.matmul(kkp[:, :], kT[:, csl], kT[:, csl], start=True, stop=True), reads=[kT], writes=[kkp])
                c.op("pe", lambda: nc.tensor.matmul(qkp[:, :], kT[:, csl], qT[:, csl], start=True, stop=True), reads=[kT, qT], writes=[qkp])
                c.op("pe", lambda: nc.tensor.transpose(ktp[:, :], kT[:, csl], identb), reads=[kT, self.cstb], writes=[ktp])
                for i in range(2):
                    vh = 2 * hp + i
                    b = i
                    gcol = g_all[:, ci, vh:vh + 1]
                    gccol = gc_all[:, ci, vh:vh + 1]
                    btcol = bt_all[:, ci, vh:vh + 1]
                    nbcol = nbt_all[:, ci, vh:vh + 1]
                    c.op("pe", lambda: nc.tensor.transpose(vtp[i][:, :], vT[i][:, csl], identb), reads=[vT[i], self.cstb], writes=[vtp[i]])
                    c.op("act", lambda: nc.scalar.copy(out=rhsR[b][:, 0:128], in_=vtp[i][:, :]), reads=[vtp[i]], writes=[rhsR[b]])
                    c.op("dve", lambda: nc.vector.tensor_scalar(out=rhsR[b][:, 128:256], in0=ktp[:, :], scalar1=egc_all[:, ci, vh:vh + 1], scalar2=None, op0=ALU.mult),
                         reads=[ktp, egc_all], writes=[rhsR[b]])
                    c.op("dve", lambda: nc.vector.tensor_scalar(out=kd[b][:, :], in0=ktp[:, :], scalar1=ekd_all[:, ci, vh:vh + 1], scalar2=None, op0=ALU.mult),
                         reads=[ktp, ekd_all], writes=[kd[b]])
                    c.op("pool", lambda: nc.gpsimd.tensor_scalar(out=rhsg[b][:, :], in0=trif, scalar1=gcol, scalar2=None, op0=ALU.mult), reads=[self.cst, g_all], writes=[rhsg[b]])
                    c.op("pe", lambda: nc.tensor.matmul(gcB[b][:, :], onesf, rhsg[b][:, :], start=True, stop=True), reads=[self.cst, rhsg[b]], writes=[gcB[b]])
                    c.op("dve", lambda: nc.vector.scalar_tensor_tensor(out=argT[b][:, :], in0=gcB[b][:, :], scalar=gccol, in1=negi, op0=ALU.subtract, op1=ALU.add),
                         reads=[gcB[b], gc_all, self.cst], writes=[argT[b]])
                    c.op("act", lambda: nc.scalar.activation(out=DT[b][:, :], in_=argT[b][:, :], func=AF.Exp), reads=[argT[b]], writes=[DT[b]])
                    c.op("act", lambda: nc.scalar.activation(out=egB[b][:, :], in_=gcB[b][:, :], func=AF.Exp), reads=[gcB[b]], writes=[egB[b]])
                    c.op("dve", lambda: nc.vector.scalar_tensor_tensor(out=tmp[b][:, :], in0=kkp[:, :], scalar=nbcol, in1=DT[b][:, :], op0=ALU.mult, op1=ALU.mult),
                         reads=[kkp, nbt_all, DT[b]], writes=[tmp[b]])
                    c.op("pool", lambda: nc.gpsimd.tensor_tensor(XTb[b][:, :], tmp[b][:, :], strictf, ALU.mult), reads=[tmp[b], self.cst], writes=[XTb[b]])
                    c.op("pool", lambda: nc.gpsimd.tensor_tensor(PTb[b][:, :], XTb[b][:, :], identf, ALU.add), reads=[XTb[b], self.cst], writes=[PTb[b]])
                    c.op("pe", lambda: nc.tensor.transpose(X0Tp[:, :], XTb[b][:, :], identb), reads=[XTb[b], self.cstb], writes=[X0Tp])
                    c.op("act", lambda: nc.scalar.copy(out=Xb[b][:, :], in_=X0Tp[:, :]), reads=[X0Tp], writes=[Xb[b]])
                    c.op("dve", lambda: nc.vector.tensor_tensor(qkm[b][:, :], qkp[:, :], DT[b][:, :], ALU.mult), reads=[qkp, DT[b]], writes=[qkm[b]])
                    c.op("pool", lambda: nc.gpsimd.tensor_tensor(qd[b][:, :], qT[:, csl], egB[b][:, :], ALU.mult), reads=[qT, egB[b]], writes=[qd[b]])
                    Xc, XTc, PTc = Xb[b], XTb[b], PTb[b]
                    Xn_, XTn_, PTn_ = X2[b], XT2[b], PT2[b]
                    for l in range(1, 7):
                        c.op("pe", lambda: nc.tensor.matmul(Xp[:, :], XTc[:, :], Xc[:, :], start=True, stop=True), reads=[XTc, Xc], writes=[Xp])
                        if l < 6:
                            c.op("pe", lambda: nc.tensor.matmul(XTp[:, :], Xc[:, :], XTc[:, :], start=True, stop=True), reads=[XTc, Xc], writes=[XTp])
                        c.op("act", lambda: nc.scalar.copy(out=Xn_[:, :], in_=Xp[:, :]), reads=[Xp], writes=[Xn_])
                        if l < 6:
                            c.op("dve", lambda: nc.vector.tensor_copy(XTn_[:, :], XTp[:, :]), reads=[XTp], writes=[XTn_])
                        c.group("pe", [lambda: nc.tensor.matmul(PTp[:, :], identb, PTc[:, :], start=True, stop=False),
                                       lambda: nc.tensor.matmul(PTp[:, :], Xn_[:, :], PTc[:, :], start=False, stop=True)],
                                reads=[self.cstb, PTc, Xn_], writes=[PTp])
                        if l % 2:
                            c.op("act", lambda: nc.scalar.copy(out=PTn_[:, :], in_=PTp[:, :]), reads=[PTp], writes=[PTn_])
                        else:
                            c.op("dve", lambda: nc.vector.tensor_copy(PTn_[:, :], PTp[:, :]), reads=[PTp], writes=[PTn_])
                        Xc, Xn_ = Xn_, Xc
                        XTc, XTn_ = XTn_, XTc
                        PTc, PTn_ = PTn_, PTc
                    c.op("pe", lambda: nc.tensor.matmul(Rp[:, :], PTc[:, :], rhsR[b][:, :], start=True, stop=True), reads=[PTc, rhsR[b]], writes=[Rp])
                    c.op("dve", lambda: nc.vector.tensor_scalar(out=u0[b][:, :], in0=Rp[:, 0:128], scalar1=btcol, scalar2=None, op0=ALU.mult), reads=[Rp, bt_all], writes=[u0[b]])
                    c.op("dve", lambda: nc.vector.tensor_scalar(out=wsb[b][:, :], in0=Rp[:, 128:256], scalar1=btcol, scalar2=None, op0=ALU.mult), reads=[Rp, bt_all], writes=[wsb[b]])
                    c.op("pe", lambda: nc.tensor.transpose(wTp[:, :], wsb[b][:, :], identb), reads=[wsb[b], self.cstb], writes=[wTp])
                    c.op("act", lambda: nc.scalar.copy(out=wT[b][:, :], in_=wTp[:, :]), reads=[wTp], writes=[wT[b]])
                    c.op("pe", lambda: nc.tensor.matmul(wSp[:, :], wT[b][:, :], Sb[i][:, :], start=True, stop=True), reads=[wT[b], Sb[i]], writes=[wSp])
                    c.op("dve", lambda: nc.vector.tensor_tensor(ub[b][:, :], u0[b][:, :], wSp[:, :], ALU.subtract), reads=[u0[b], wSp], writes=[ub[b]])
                    c.group("pe", [lambda: nc.tensor.matmul(op_[:, :], qd[b][:, :], Sb[i][:, :], start=True, stop=False),
                                   lambda: nc.tensor.matmul(op_[:, :], qkm[b][:, :], ub[b][:, :], start=False, stop=True)],
                            reads=[qd[b], Sb[i], qkm[b], ub[b]], writes=[op_])
                    c.op("pe", lambda: nc.tensor.matmul(Snp[:, :], kd[b][:, :], ub[b][:, :], start=True, stop=True), reads=[kd[b], ub[b]], writes=[Snp])
                    c.op("dve", lambda: nc.vector.scalar_tensor_tensor(out=S[i][:, :], in0=S[i][:, :], scalar=ecd_all[:, ci, vh:vh + 1], in1=Snp[:, :], op0=ALU.mult, op1=ALU.add),
                         reads=[S[i], ecd_all, Snp], writes=[S[i]])
                    c.op("pool", lambda: nc.gpsimd.tensor_copy(Sb[i][:, :], S[i][:, :]), reads=[S[i]], writes=[Sb[i]])
                    c.op("act", lambda: nc.scalar.activation(out=junk[:, :], in_=op_[:, :], func=AF.Square, accum_out=oss[b][:, 0:1]), reads=[op_], writes=[junk, oss[b]], multi=True)
                    c.op("act", lambda: nc.scalar.activation(out=ors[b][:, :], in_=oss[b][:, :], func=AF.Sqrt, bias=self.epsc[:, 0:1], scale=1.0 / 128.0),
                         reads=[oss[b], self.epsc], writes=[ors[b]])
                    c.op("dve", lambda: nc.vector.reciprocal(ors[b][:, :], ors[b][:, :]), reads=[ors[b]], writes=[ors[b]])
                    c.op("dve", lambda: nc.vector.tensor_scalar(out=on[b][:, :], in0=op_[:, :], scalar1=ors[b][:, 0:1], scalar2=None, op0=ALU.mult), reads=[op_, ors[b]], writes=[on[b]])
                    c.op("pe", lambda: nc.tensor.transpose(oTp[:, :], on[b][:, :], identb), reads=[on[b], self.cstb], writes=[oTp])
                    c.op("dve", lambda: nc.vector.scalar_tensor_tensor(out=yb[i][:, csl], in0=oTp[:, :], scalar=self.pv[:, PV_DNO:PV_DNO + 1], in1=sz[i][:, csl], op0=ALU.mult, op1=ALU.mult),
                         reads=[oTp, self.pv, sz[i]], writes=[yb[i]])
            for i in range(2):
                c.dma("sp", self.yT_d[2 * hp + i, :, :], yb[i][:, :], reads=[yb[i]])
        if len(pairs) < 12:
            zt = sc.sb("zt", [128, T], BF16)
            c.op("dve", lambda: nc.vector.memset(zt[:, :], 0.0), writes=[zt])
            for h in range(NH):
                if h // 2 not in pairs:
                    c.dma("sp", self.yT_d[h, :, :], zt[:, :], reads=[zt])
        c.barrier()


Layers.dn_heads = dn_heads


def build_full():
    nc = bass.Bass("TRN2", target_bir_lowering=False)
    P = Layers(nc)
    c = P.ctx
    out = c.dram("out", [T, D], F32, "ExternalOutput")
    W0 = dn_decl(P, 0)
    W1 = sb_decl(P, 1)
    P.load_consts()
    P.layer(0, "dn", W0, P.x_in, P.x1_d, first=True)
    P.layer(1, "sb", W1, P.x1_d, out)
    c.finish()
    return nc, c


def kernel(x, mem, norm_g, mem_norm_g, mem_w_kv, xa_q_norm_g, xa_k_norm_g, w_out,
           dn_w_in, dn_conv_w, dn_a_log, dn_dt_bias, dn_out_norm_g,
           sb_w_in, sb_q_norm_g, sb_k_norm_g):
    inp = {k: np.asarray(v, dtype=np.float32) for k, v in dict(
        x=x, mem=mem, norm_g=norm_g, mem_norm_g=mem_norm_g, mem_w_kv=mem_w_kv, xa_q_norm_g=xa_q_norm_g,
        xa_k_norm_g=xa_k_norm_g, w_out=w_out, dn_w_in=dn_w_in, dn_conv_w=dn_conv_w, dn_a_log=dn_a_log,
        dn_dt_bias=dn_dt_bias, dn_out_norm_g=dn_out_norm_g, sb_w_in=sb_w_in, sb_q_norm_g=sb_q_norm_g,
        sb_k_norm_g=sb_k_norm_g).items()}
    nc, c = build_full()
    g = np.concatenate([inp["norm_g"][0], inp["norm_g"][1], inp["mem_norm_g"]])
    shared = {"gB": np.ascontiguousarray(np.broadcast_to(g, (128, 3 * D))), "cst": make_consts(), "pv": make_pv(inp),
              "wo0": np.ascontiguousarray(inp["w_out"][0]), "wo1": np.ascontiguousarray(inp["w_out"][1])}
    shared.update(dn_weights_host(inp))
    shared.update(sb_weights_host(inp))
    shared.update(kv_weights_host(inp, 0))
    shared.update(kv_weights_host(inp, 1))
    B = inp["x"].shape[0]
    in_maps = []
    for core in range(8):
        m = dict(shared)
        if core < B:
            m["x"] = np.ascontiguousarray(inp["x"][core])
            m["mem"] = np.ascontiguousarray(inp["mem"][core])
        else:
            m["x"] = np.zeros((T, D), np.float32)
            m["mem"] = np.zeros((NMEM, D), np.float32)
        in_maps.append(m)
    res = run_bass_kernel_spmd(nc, in_maps, core_ids=list(range(8)))
    return np.stack([res.results[b]["out"] for b in range(B)], 0).astype(np.float32)


def run_streams(gens, weights=None):
    gens = list(gens)
    weights = list(weights) if weights is not None else [1] * len(gens)
    items = list(zip(gens, weights))
    while items:
        for it in list(items):
            g, w = it
            for _ in range(w):
                try:
                    next(g)
                except StopIteration:
                    items.remove(it)
                    break


class _B:
    pass


def _pool_scale(nc, out, in0, s):
    return nc.gpsimd.tensor_scalar(out=out, in0=in0, scalar1=s, scalar2=0.0, op0=ALU.mult, op1=ALU.add)


def sb_heads2(self, hT, W, psb):
    c = self.ctx
    nc = self.nc
    heads = self.dbg.get("heads")
    heads = list(range(self.nh)) if heads is None else heads
    w_d, q_idx, k_idx, z_idx, wv_d = W["w"], W["q_idx"], W["k_idx"], W["z_idx"], W["wvm"]
    with Scope(c) as sc:
        if len(heads) < self.nh:
            zt = sc.sb("zt", [128, T], BF16)
            c.op("dve", lambda: nc.vector.memset(zt[:, :], 0.0), writes=[zt])
            for h in range(self.nh):
                if h not in heads:
                    c.dma("sp", self.yT_d[h, :, :], zt[:, :], reads=[zt])
        with Scope(c) as s0:
            wvb = s0.sb("wvb", [128, 8192], BF16)
            vb = [s0.sb("vb%d" % i, [128, 512], BF16) for i in range(2)]
            for g in sorted({h // 4 for h in heads}):
                for pc in range(4):
                    s = self.w_fetch(wv_d[g * 4 + pc, :, :])
                    self.w_cast(s, wvb, wvb[:, pc * 2048:(pc + 1) * 2048])
                for tt in range(16):
                    p = psb[tt % 2]
                    v = vb[tt % 2]
                    c.group("pe", [(lambda kt=kt: nc.tensor.matmul(p[:, 0:512], hT[:, kt, tt * 128:(tt + 1) * 128], wvb[:, kt * 512:(kt + 1) * 512],
                                                                  start=(kt == 0), stop=(kt == KT - 1))) for kt in range(KT)],
                            reads=[hT, wvb], writes=[p])
                    c.op("act", lambda: nc.scalar.copy(out=v[:, :], in_=p[:, 0:512]), reads=[p], writes=[v])
                    c.dma("sp", self.vtok_d[tt, :, g * 512:(g + 1) * 512], v[:, :], reads=[v])
            c.barrier()
        banks = list(psb) + [sc.ps("sbk%d" % i, [128, 512], F32) for i in range(6)]
        nmask = sc.sb("nmask", [128, 256], F32)
        mtmp = sc.sb("mtmp", [128, 128], F32)
        c.op("dve", lambda: nc.vector.tensor_tensor(mtmp[:, :], self.cst[:, C_LOW:C_LOW + 128], self.cst[:, C_ID:C_ID + 128], ALU.add), reads=[self.cst], writes=[mtmp])
        c.op("dve", lambda: nc.vector.tensor_scalar(out=nmask[:, 0:128].bitcast(F32R), in0=mtmp[:, :], scalar1=-1.0, scalar2=None, op0=ALU.mult),
             reads=[mtmp], writes=[nmask])
        c.op("dve", lambda: nc.vector.tensor_scalar(out=nmask[:, 128:256].bitcast(F32R), in0=self.cst[:, C_STRICT:C_STRICT + 128], scalar1=-1.0, scalar2=None, op0=ALU.mult),
             reads=[self.cst], writes=[nmask])
        Mge = nmask[:, 0:128].bitcast(F32R)
        Mlt = nmask[:, 128:256].bitcast(F32R)
        strict = self.cst[:, C_STRICT:C_STRICT + 128]
        negs = self.cst[:, C_NEGS:C_NEGS + 128]
        onesf = self.cst[:, C_ONES:C_ONES + 128]

        def stream(s, hs, delay):
            for _ in range(delay):
                yield
            b0, b1, b2, b3 = banks[4 * s:4 * s + 4]
            zps = [b0, b1]
            Pps = b2
            Ops = b3
            wst = self.wst[s]
            wbf = self.wbf if s == 0 else [sc.sb("wbfs%d" % i, [128, 2048], BF16) for i in range(2)]
            qT = sc.sb("qT%d" % s, [128, T], BF16)
            kT = sc.sb("kT%d" % s, [128, T], BF16)
            nkT = sc.sb("nkT%d" % s, [128, T], BF16)
            sz = sc.sb("sz%d" % s, [128, T], BF16)
            y = sc.sb("y%d" % s, [128, T], BF16)
            vt = sc.sb("vt%d" % s, [128, 16, 128], BF16)
            sq = sc.sb("sq%d" % s, [128, 512], F32)
            rstd = sc.sb("rstd%d" % s, [128, 512], F32)
            ez = [sc.sb("ez%d_%d" % (s, i), [128, 512], F32) for i in range(2)]
            spr = [sc.sb("spr%d_%d" % (s, i), [128, 512], F32) for i in range(2)]
            wb = [sc.sb("wb%d_%d" % (s, i), [128, 512], BF16) for i in range(2)]
            pj = 0
            for h in hs:
                c.dma("sp", vt[:, :, :], self.vtok_d[:, :, h * 128:(h + 1) * 128].rearrange("t p e -> p t e"), writes=[vt])
                yield
                widx = [z_idx[h], q_idx[h], k_idx[h]]
                c.dma("sp", wst[:, :], w_d[widx[0], :, :], writes=[wst])
                for bi in range(3):
                    wt = wbf[bi % 2]
                    c.op("pool", lambda: nc.gpsimd.tensor_copy(wt[:, :], wst[:, :]), reads=[wst], writes=[wt])
                    if bi + 1 < 3:
                        c.dma("sp", wst[:, :], w_d[widx[bi + 1], :, :], writes=[wst])
                    yield
                    for tc in range(4):
                        sl = slice(tc * 512, (tc + 1) * 512)
                        p = zps[pj]
                        pj ^= 1
                        for part in range(KT // SBP):
                            c.group("pe", [(lambda kt=kt: nc.tensor.matmul(p[:, :], wt[:, kt * 128:(kt + 1) * 128], hT[:, kt, sl],
                                                                          start=(kt == 0), stop=(kt == KT - 1))) for kt in range(part * SBP, part * SBP + SBP)],
                                    reads=[wt, hT], writes=[p] if part in (0, KT // SBP - 1) else [], lhs=[wt])
                            yield
                        if bi == 0:
                            c.op("act", lambda: nc.scalar.activation(out=sq[:, :], in_=p[:, :], func=AF.Exp, scale=-1.0), reads=[p], writes=[sq])
                            yield
                            c.op("act", lambda: nc.scalar.activation(out=sq[:, :], in_=sq[:, :], func=AF.Ln, bias=self.onec[:, 0:1], scale=1.0), reads=[sq, self.onec], writes=[sq])
                            yield
                            c.op("act", lambda: nc.scalar.activation(out=sq[:, :], in_=sq[:, :], func=AF.Exp, scale=-1.0), reads=[sq], writes=[sq])
                            yield
                            c.op("dve", lambda: nc.vector.tensor_tensor(sz[:, sl], p[:, :], sq[:, :], ALU.mult), reads=[p, sq], writes=[sz])
                            yield
                        else:
                            dst = qT if bi == 1 else kT
                            gcol = self.gsc[:, GS_SBQ:GS_SBQ + 1] if bi == 1 else self.pv[:, PV_SBK:PV_SBK + 1]
                            c.op("act", lambda: nc.scalar.activation(out=sq[:, :], in_=p[:, :], func=AF.Square), reads=[p], writes=[sq])
                            yield
                            c.op("pe", lambda: nc.tensor.matmul(Pps[:, :], onesf, sq[:, :], start=True, stop=True), reads=[self.cst, sq], writes=[Pps], lhs=[self.cst])
                            yield
                            c.op("act", lambda: nc.scalar.activation(out=rstd[:, :], in_=Pps[:, :], func=AF.Ln, bias=self.epsc[:, 0:1], scale=1.0 / 128.0),
                                 reads=[Pps, self.epsc], writes=[rstd])
                            yield
                            c.op("act", lambda: nc.scalar.activation(out=rstd[:, :], in_=rstd[:, :], func=AF.Exp, scale=-0.5), reads=[rstd], writes=[rstd])
                            yield
                            c.op("dve", lambda: nc.vector.scalar_tensor_tensor(out=dst[:, sl], in0=p[:, :], scalar=gcol, in1=rstd[:, :], op0=ALU.mult, op1=ALU.mult),
                                 reads=[p, rstd, self.gsc, self.pv], writes=[dst])
                            yield
                blocks = [(qc, kb) for qc in range(4) for kb in range(4 * qc + 3, -1, -1)]

                c.op("pool", lambda: _pool_scale(nc, nkT[:, :], kT[:, :], -1.0), reads=[kT], writes=[nkT])
                yield

                def stageA(i):
                    qc, kb = blocks[i]
                    b = i % 2
                    lo = max(0, kb * 128 - qc * 512)
                    diag = kb >= 4 * qc
                    z = zps[b]
                    e = ez[b]
                    sp_ = spr[b]
                    c.op("pe", lambda: nc.tensor.matmul(z[:, lo:512], kT[:, kb * 128:(kb + 1) * 128], qT[:, qc * 512 + lo:(qc + 1) * 512], start=True, stop=True),
                         reads=[kT, qT], writes=[z], lhs=[kT])
                    yield
                    c.op("act", lambda: nc.scalar.activation(out=e[:, lo:512], in_=z[:, lo:512], func=AF.Exp), reads=[z], writes=[e])
                    yield
                    c.op("act", lambda: nc.scalar.activation(out=sp_[:, lo:512].bitcast(F32R), in_=e[:, lo:512], func=AF.Ln, bias=self.onec[:, 0:1], scale=1.0),
                         reads=[e, self.onec], writes=[sp_])
                    yield
                    if diag:
                        c.op("dve", lambda: nc.vector.tensor_tensor(sp_[:, lo:lo + 128].bitcast(F32R), sp_[:, lo:lo + 128], strict, ALU.mult), reads=[sp_, self.cst], writes=[sp_])
                        yield

                def stageB(i):
                    qc, kb = blocks[i]
                    b = i % 2
                    lo = max(0, kb * 128 - qc * 512)
                    first = kb == 4 * qc + 3
                    last = kb == 0
                    diag = kb >= 4 * qc
                    sp_ = spr[b]
                    ksl = slice(kb * 128, (kb + 1) * 128)
                    qsl = slice(qc * 512 + lo, (qc + 1) * 512)
                    c.group("pe", [lambda: nc.tensor.matmul(Pps[:, lo:512], Mge, sp_[:, lo:512].bitcast(F32R), start=first, stop=False),
                                   lambda: nc.tensor.matmul(Pps[:, lo:512], kT[:, ksl], qT[:, qsl], start=False, stop=last)],
                            reads=[nmask, sp_, kT, qT], writes=[Pps], lhs=[nmask, kT])
                    yield
                    c.op("act", lambda: nc.scalar.activation(out=wb[b][:, lo:512], in_=Pps[:, lo:512], func=AF.Exp), reads=[Pps], writes=[wb[b]])
                    yield
                    if diag:
                        c.op("pool", lambda: nc.gpsimd.tensor_tensor(wb[b][:, lo:lo + 128], wb[b][:, lo:lo + 128], strict, ALU.mult), reads=[wb[b], self.cst], writes=[wb[b]])
                        yield
                    if not last:
                        c.group("pe", [lambda: nc.tensor.matmul(Pps[:, lo:512], nkT[:, ksl], qT[:, qsl], start=False, stop=False),
                                       lambda: nc.tensor.matmul(Pps[:, lo:512], Mlt, sp_[:, lo:512].bitcast(F32R), start=False, stop=False)],
                                reads=[nmask, sp_, nkT, qT], writes=[Pps], lhs=[nmask, nkT])
                        yield
                    c.op("pe", lambda: nc.tensor.matmul(Ops[:, lo:512], vt[:, kb, :], wb[b][:, lo:512], start=first, stop=last), reads=[vt, wb[b]], writes=[Ops], lhs=[vt])
                    yield
                    if last:
                        c.op("dve", lambda: nc.vector.tensor_tensor(y[:, qc * 512:(qc + 1) * 512], Ops[:, :], sz[:, qc * 512:(qc + 1) * 512], ALU.mult),
                             reads=[Ops, sz], writes=[y])
                        yield

                def att():
                    for i in range(len(blocks) + 1):
                        if i < len(blocks):
                            yield from stageA(i)
                        if i >= 1:
                            yield from stageB(i - 1)
                kk_ = 0
                for _ in att():
                    kk_ += 1
                    if kk_ % SB_K == 0:
                        yield
                c.dma("sp", self.yT_d[h, :, :], y[:, :], reads=[y])
                yield

        run_streams([stream(0, heads[0::2], 0), stream(1, heads[1::2], SB_DELAY)])
        c.barrier()


Layers.sb_heads2 = sb_heads2


def dn_heads2(self, hT, W, psb):
    c = self.ctx
    nc = self.nc
    pairs = self.dbg.get("pairs")
    pairs = list(range(12)) if pairs is None else pairs
    NC_ = T // 128
    with Scope(c) as sc:
        onesf = self.cst[:, C_ONES:C_ONES + 128]
        trif = self.cst[:, C_TRI:C_TRI + 128]
        negi = self.cst[:, C_NEGI:C_NEGI + 128]
        strictf = self.cst[:, C_STRICT:C_STRICT + 128]
        identf = self.cst[:, C_ID:C_ID + 128]
        identb = self.cstb[:, C_ID:C_ID + 128]
        g_all = sc.sb("g_all", [128, NC_, 24], F32)
        bt_all = sc.sb("bt_all", [128, NC_, 24], F32)
        nbt_all = sc.sb("nbt_all", [128, NC_, 24], F32)
        gc_all = sc.sb("gc_all", [128, NC_, 24], F32)
        egc_all = sc.sb("egc_all", [128, NC_, 24], F32)
        ekd_all = sc.sb("ekd_all", [128, NC_, 24], F32)
        ecd_all = sc.sb("ecd_all", [128, NC_, 24], F32)
        bankA = sc.ps("bankA", [128, 512], F32)
        bankB = [sc.ps("bankB%d" % i, [128, 512], F32) for i in range(2)]
        bankC = [sc.ps("bankC%d" % i, [128, 512], F32) for i in range(2)]
        bankE = sc.ps("bankE", [128, 1024], BF16)
        with Scope(c) as s0:
            negA = s0.sb("negA", [128, 24], F32)
            wabf = s0.sb("wabf", [128, 768], F32)
            wabb = s0.sb("wabb", [128, 768], BF16)
            gtmp = s0.sb("gtmp", [128, 48], F32)
            c.dma("sp", wabf[:, :], W["wab"][:, :], writes=[wabf])
            c.op("dve", lambda: nc.vector.tensor_copy(wabb[:, :], wabf[:, :]), reads=[wabf], writes=[wabb])
            c.op("act", lambda: nc.scalar.activation(out=negA[:, :], in_=self.pv[:, PV_ALOG:PV_ALOG + 24], func=AF.Exp), reads=[self.pv], writes=[negA])
            c.op("dve", lambda: nc.vector.tensor_scalar(out=negA[:, :], in0=negA[:, :], scalar1=-1.0, scalar2=None, op0=ALU.mult), reads=[negA], writes=[negA])
            abp = view("abp", bankA[:, 0:48], bankA)
            gcp = view("gcp", bankB[0][:, 0:24], bankB[0])
            glp = view("glp", bankC[0][:, 0:24], bankC[0])
            for tt in range(NC_):
                c.group("pe", [(lambda kt=kt: nc.tensor.matmul(abp[:, :], hT[:, kt, tt * 128:(tt + 1) * 128], wabb[:, kt * 48:(kt + 1) * 48],
                                                              start=(kt == 0), stop=(kt == KT - 1))) for kt in range(KT)],
                        reads=[hT, wabb], writes=[abp])
                c.op("dve", lambda: nc.vector.tensor_tensor(gtmp[:, 0:24], abp[:, 0:24], self.pv[:, PV_DTB:PV_DTB + 24], ALU.add), reads=[abp, self.pv], writes=[gtmp])
                c.op("dve", lambda: nc.vector.tensor_scalar(out=gtmp[:, 24:48], in0=abp[:, 24:48], scalar1=-1.0, scalar2=None, op0=ALU.mult), reads=[abp], writes=[gtmp])
                c.op("act", lambda: nc.scalar.activation(out=gtmp[:, :], in_=gtmp[:, :], func=AF.Exp), reads=[gtmp], writes=[gtmp])
                c.op("act", lambda: nc.scalar.activation(out=gtmp[:, :], in_=gtmp[:, :], func=AF.Ln, bias=self.onec[:, 0:1], scale=1.0), reads=[gtmp, self.onec], writes=[gtmp])
                c.op("dve", lambda: nc.vector.tensor_tensor(g_all[:, tt, :], gtmp[:, 0:24], negA[:, :], ALU.mult), reads=[gtmp, negA], writes=[g_all])
                c.op("act", lambda: nc.scalar.activation(out=bt_all[:, tt, :], in_=gtmp[:, 24:48], func=AF.Exp, scale=-1.0), reads=[gtmp], writes=[bt_all])
                c.op("pool", lambda: _pool_scale(nc, nbt_all[:, tt, :], bt_all[:, tt, :], -1.0), reads=[bt_all], writes=[nbt_all])
                c.op("pe", lambda: nc.tensor.matmul(gcp[:, :], trif, g_all[:, tt, :], start=True, stop=True), reads=[self.cst, g_all], writes=[gcp])
                c.op("pe", lambda: nc.tensor.matmul(glp[:, :], onesf, g_all[:, tt, :], start=True, stop=True), reads=[self.cst, g_all], writes=[glp])
                c.op("act", lambda: nc.scalar.copy(out=gc_all[:, tt, :], in_=gcp[:, :]), reads=[gcp], writes=[gc_all])
                c.op("act", lambda: nc.scalar.activation(out=egc_all[:, tt, :], in_=gcp[:, :], func=AF.Exp), reads=[gcp], writes=[egc_all])
                c.op("act", lambda: nc.scalar.activation(out=ecd_all[:, tt, :], in_=glp[:, :], func=AF.Exp), reads=[glp], writes=[ecd_all])
                c.op("dve", lambda: nc.vector.tensor_tensor(ekd_all[:, tt, :], glp[:, :], gc_all[:, tt, :], ALU.subtract), reads=[glp, gc_all], writes=[ekd_all])
                c.op("act", lambda: nc.scalar.activation(out=ekd_all[:, tt, :], in_=ekd_all[:, tt, :], func=AF.Exp), reads=[ekd_all], writes=[ekd_all])
            c.barrier()
        raw = sc.sb("raw", [128, T + 3], F32)
        acc = sc.sb("acc", [128, 512], F32)
        stmp = sc.sb("stmp", [128, 512], F32)
        sil = acc
        sqb = stmp
        rstd = sc.sb("rstd", [128, 512], F32)
        vTc = sc.sb("vTc", [128, 512], BF16)
        c.op("dve", lambda: nc.vector.memset(raw[:, 0:3], 0.0), writes=[raw])
        OUT = []
        for i in range(2):
            o = _B()
            o.qT = sc.sb("qT%d" % i, [128, T], BF16)
            o.kT = sc.sb("kT%d" % i, [128, T], BF16)
            o.ktok = sc.sb("ktok%d" % i, [128, NC_, 128], BF16)
            o.vtok = [sc.sb("vtok%d_%d" % (i, j), [128, NC_, 128], BF16) for j in range(2)]
            o.sz = [sc.sb("sz%d_%d" % (i, j), [128, T], BF16) for j in range(2)]
            OUT.append(o)
        kkp = view("kkp", bankA[:, 256:384], bankA)
        qkp = view("qkp", bankA[:, 384:512], bankA)
        Snp = [view("Snp%d" % i, bankA[:, i * 128:(i + 1) * 128], bankA) for i in range(2)]
        tp = [view("tp%d" % i, bankE[:, i * 128:(i + 1) * 128], bankE) for i in range(8)]
        VB = []
        for i in range(2):
            b = _B()
            b.gcB = view("gcB%d" % i, bankB[i][:, 0:128], bankB[i])
            b.Xp = view("Xp%d" % i, bankB[i][:, 128:256], bankB[i])
            b.XTp = view("XTp%d" % i, bankB[i][:, 256:384], bankB[i])
            b.PTp = view("PTp%d" % i, bankB[i][:, 384:512], bankB[i])
            b.Rp = view("Rp%d" % i, bankC[i][:, 0:256], bankC[i])
            b.wSp = view("wSp%d" % i, bankC[i][:, 256:384], bankC[i])
            b.op_ = view("op%d" % i, bankC[i][:, 384:512], bankC[i])
            b.X0Tp, b.wTp, b.oTp = tp[i * 3], tp[i * 3 + 1], tp[i * 3 + 2]
            for nm, dt in (("rhsg", F32), ("argT", F32), ("DT", F32), ("egB", F32), ("tmp", F32)):
                setattr(b, nm, sc.sb("%s_%d" % (nm, i), [128, 128], dt))
            for nm in ("XTa", "Xa", "PTa", "XTb", "Xb", "PTb", "kg", "wsb", "ub", "on", "junk", "Sb"):
                setattr(b, nm, sc.sb("%s_%d" % (nm, i), [128, 128], BF16))
            b.S = sc.sb("S_%d" % i, [128, 128], F32)
            b.oss = sc.sb("oss_%d" % i, [128, 1], F32)
            b.ors = sc.sb("ors_%d" % i, [128, 1], F32)
            b.u0 = [sc.sb("u0_%d_%d" % (i, j), [128, 128], F32) for j in range(2)]
            for nm in ("wT", "qd", "qkm", "kd"):
                setattr(b, nm, [sc.sb("%s_%d_%d" % (nm, i, j), [128, 128], BF16) for j in range(2)])
            b.y = [sc.sb("y_%d_%d" % (i, j), [128, 128], BF16) for j in range(2)]
            VB.append(b)
        ptp = [tp[6], tp[7]]
        st = _B()
        st.pj = 0
        st.tpj = 0

        def silu_to(src_ap, n, dst_ap, reads, dst_tile):
            c.op("act", lambda: nc.scalar.activation(out=stmp[:, 0:n], in_=src_ap, func=AF.Exp, scale=-1.0), reads=reads, writes=[stmp])
            yield
            c.op("act", lambda: nc.scalar.activation(out=stmp[:, 0:n], in_=stmp[:, 0:n], func=AF.Ln, bias=self.onec[:, 0:1], scale=1.0), reads=[stmp, self.onec], writes=[stmp])
            yield
            c.op("act", lambda: nc.scalar.activation(out=stmp[:, 0:n], in_=stmp[:, 0:n], func=AF.Exp, scale=-1.0), reads=[stmp], writes=[stmp])
            yield
            c.op("dve", lambda: nc.vector.tensor_tensor(dst_ap, src_ap, stmp[:, 0:n], ALU.mult), reads=reads + [stmp], writes=[dst_tile])
            yield

        def prologue(hp, o):
            idxs = [W["q_idx"][hp], W["k_idx"][hp], W["v_idx"][2 * hp], W["v_idx"][2 * hp + 1], W["z_idx"][2 * hp], W["z_idx"][2 * hp + 1]]
            cbis = [hp, 12 + hp, 24 + 2 * hp, 24 + 2 * hp + 1]
            slot = self.w_fetch(W["w"][idxs[0], :, :])
            for bi in range(6):
                wt = self.w_cast(slot)
                if bi + 1 < 6:
                    slot = self.w_fetch(W["w"][idxs[bi + 1], :, :])
                yield
                for tc in range(4):
                    sl = slice(tc * 512, (tc + 1) * 512)
                    p = psb[st.pj]
                    st.pj ^= 1
                    for part in range(4):
                        c.group("pe", [(lambda kt=kt: nc.tensor.matmul(p[:, :], wt[:, kt * 128:(kt + 1) * 128], hT[:, kt, sl],
                                                                      start=(kt == 0), stop=(kt == KT - 1))) for kt in range(part * 4, part * 4 + 4)],
                                reads=[wt, hT], writes=[p] if part in (0, 3) else [])
                        yield
                    if bi >= 4:
                        yield from silu_to(p[:, :], 512, o.sz[bi - 4][:, sl], [p], o.sz[bi - 4])
                        continue
                    c.op("act", lambda: nc.scalar.copy(out=raw[:, 3 + tc * 512:3 + (tc + 1) * 512], in_=p[:, :]), reads=[p], writes=[raw])
                    yield
                    wc = [self.pv[:, PV_CONV + cbis[bi] * 4 + k:PV_CONV + cbis[bi] * 4 + k + 1] for k in range(4)]
                    c.op("dve", lambda: nc.vector.tensor_scalar(out=acc[:, :], in0=raw[:, 3 + tc * 512:3 + (tc + 1) * 512], scalar1=wc[3], scalar2=None, op0=ALU.mult),
                         reads=[raw, self.pv], writes=[acc])
                    yield
                    for k in (2, 1, 0):
                        c.op("dve", lambda: nc.vector.scalar_tensor_tensor(out=acc[:, :], in0=raw[:, k + tc * 512:k + (tc + 1) * 512], scalar=wc[k], in1=acc[:, :],
                                                                           op0=ALU.mult, op1=ALU.add), reads=[raw, acc, self.pv], writes=[acc])
                        yield
                    if bi < 2:
                        yield from silu_to(acc[:, :], 512, sil[:, :], [acc], sil)
                        c.op("act", lambda: nc.scalar.activation(out=sqb[:, :], in_=sil[:, :], func=AF.Square), reads=[sil], writes=[sqb])
                        yield
                        p2 = psb[st.pj]
                        st.pj ^= 1
                        c.op("pe", lambda: nc.tensor.matmul(p2[:, :], onesf, sqb[:, :], start=True, stop=True), reads=[self.cst, sqb], writes=[p2])
                        yield
                        c.op("act", lambda: nc.scalar.activation(out=rstd[:, :], in_=p2[:, :], func=AF.Ln, bias=self.epsc[:, 0:1], scale=1.0), reads=[p2, self.epsc], writes=[rstd])
                        yield
                        c.op("act", lambda: nc.scalar.activation(out=rstd[:, :], in_=rstd[:, :], func=AF.Exp, scale=-0.5), reads=[rstd], writes=[rstd])
                        yield
                        dst = o.qT if bi == 0 else o.kT
                        gcol = self.gsc[:, 5:6] if bi == 0 else self.gsc[:, 6:7]
                        c.op("dve", lambda: nc.vector.scalar_tensor_tensor(out=dst[:, sl], in0=sil[:, :], scalar=gcol, in1=rstd[:, :], op0=ALU.mult, op1=ALU.mult),
                             reads=[sil, rstd, self.gsc], writes=[dst])
                        yield
                        if bi == 1:
                            for j in range(4):
                                tk = ptp[st.tpj]
                                st.tpj ^= 1
                                c.op("pe", lambda: nc.tensor.transpose(tk[:, :], dst[:, tc * 512 + j * 128:tc * 512 + (j + 1) * 128], identb), reads=[dst, self.cstb], writes=[tk])
                                yield
                                c.op("act", lambda: nc.scalar.copy(out=o.ktok[:, tc * 4 + j, :], in_=tk[:, :]), reads=[tk], writes=[o.ktok])
                                yield
                    else:
                        yield from silu_to(acc[:, :], 512, vTc[:, :], [acc], vTc)
                        for j in range(4):
                            tk = ptp[st.tpj]
                            st.tpj ^= 1
                            c.op("pe", lambda: nc.tensor.transpose(tk[:, :], vTc[:, j * 128:(j + 1) * 128], identb), reads=[vTc, self.cstb], writes=[tk])
                            yield
                            c.op("act", lambda: nc.scalar.copy(out=o.vtok[bi - 2][:, tc * 4 + j, :], in_=tk[:, :]), reads=[tk], writes=[o.vtok[bi - 2]])
                            yield

        prog = _B()

        def front(i, hp, o):
            b = VB[i]
            vh = 2 * hp + i
            for ci in range(NC_):
                par = ci % 2
                csl = slice(ci * 128, (ci + 1) * 128)
                while prog.gdone[i] < ci - 1:
                    yield
                if i == 0:
                    while prog.fdone[1] < ci:
                        yield
                    c.op("pe", lambda: nc.tensor.matmul(kkp[:, :], o.kT[:, csl], o.kT[:, csl], start=True, stop=True), reads=[o.kT], writes=[kkp])
                    yield
                    c.op("pe", lambda: nc.tensor.matmul(qkp[:, :], o.kT[:, csl], o.qT[:, csl], start=True, stop=True), reads=[o.kT, o.qT], writes=[qkp])
                    yield
                    prog.common = ci + 1
                else:
                    while prog.common < ci + 1:
                        yield
                gcol = g_all[:, ci, vh:vh + 1]
                gccol = gc_all[:, ci, vh:vh + 1]
                btcol = bt_all[:, ci, vh:vh + 1]
                nbcol = nbt_all[:, ci, vh:vh + 1]
                c.op("pool", lambda: _pool_scale(nc, b.rhsg[:, :], trif, gcol), reads=[self.cst, g_all], writes=[b.rhsg])
                yield
                c.op("pe", lambda: nc.tensor.matmul(b.gcB[:, :], onesf, b.rhsg[:, :], start=True, stop=True), reads=[self.cst, b.rhsg], writes=[b.gcB])
                yield
                c.op("dve", lambda: nc.vector.scalar_tensor_tensor(out=b.argT[:, :], in0=b.gcB[:, :], scalar=gccol, in1=negi, op0=ALU.subtract, op1=ALU.add),
                     reads=[b.gcB, gc_all, self.cst], writes=[b.argT])
                yield
                c.op("act", lambda: nc.scalar.activation(out=b.egB[:, :], in_=b.gcB[:, :], func=AF.Exp), reads=[b.gcB], writes=[b.egB])
                yield
                c.op("act", lambda: nc.scalar.activation(out=b.DT[:, :], in_=b.argT[:, :], func=AF.Exp), reads=[b.argT], writes=[b.DT])
                yield
                c.op("dve", lambda: nc.vector.scalar_tensor_tensor(out=b.tmp[:, :], in0=kkp[:, :], scalar=nbcol, in1=b.DT[:, :], op0=ALU.mult, op1=ALU.mult),
                     reads=[kkp, nbt_all, b.DT], writes=[b.tmp])
                yield
                c.op("dve", lambda: nc.vector.tensor_tensor(b.qkm[par][:, :], qkp[:, :], b.DT[:, :], ALU.mult), reads=[qkp, b.DT], writes=[b.qkm[par]])
                yield
                prog.fread[i] = ci + 1
                c.op("pool", lambda: nc.gpsimd.tensor_tensor(b.XTa[:, :], b.tmp[:, :], strictf, ALU.mult), reads=[b.tmp, self.cst], writes=[b.XTa])
                yield
                c.op("pool", lambda: nc.gpsimd.tensor_tensor(b.PTa[:, :], b.XTa[:, :], identf, ALU.add), reads=[b.XTa, self.cst], writes=[b.PTa])
                yield
                c.op("pe", lambda: nc.tensor.transpose(b.X0Tp[:, :], b.XTa[:, :], identb), reads=[b.XTa, self.cstb], writes=[b.X0Tp])
                yield
                c.op("act", lambda: nc.scalar.copy(out=b.Xa[:, :], in_=b.X0Tp[:, :]), reads=[b.X0Tp], writes=[b.Xa])
                yield
                c.op("pool", lambda: nc.gpsimd.tensor_tensor(b.qd[par][:, :], o.qT[:, csl], b.egB[:, :], ALU.mult), reads=[o.qT, b.egB], writes=[b.qd[par]])
                yield
                c.op("pool", lambda: _pool_scale(nc, b.kg[:, :], o.ktok[:, ci, :], egc_all[:, ci, vh:vh + 1]), reads=[o.ktok, egc_all], writes=[b.kg])
                yield
                c.op("pool", lambda: _pool_scale(nc, b.kd[par][:, :], o.ktok[:, ci, :], ekd_all[:, ci, vh:vh + 1]), reads=[o.ktok, ekd_all], writes=[b.kd[par]])
                yield
                Xc, XTc, PTc = b.Xa, b.XTa, b.PTa
                Xn_, XTn_, PTn_ = b.Xb, b.XTb, b.PTb
                for l in range(1, 7):
                    c.op("pe", lambda: nc.tensor.matmul(b.Xp[:, :], XTc[:, :], Xc[:, :], start=True, stop=True), reads=[XTc, Xc], writes=[b.Xp])
                    yield
                    if l < 6:
                        c.op("pe", lambda: nc.tensor.matmul(b.XTp[:, :], Xc[:, :], XTc[:, :], start=True, stop=True), reads=[XTc, Xc], writes=[b.XTp])
                        yield
                    c.op("act", lambda: nc.scalar.copy(out=Xn_[:, :], in_=b.Xp[:, :]), reads=[b.Xp], writes=[Xn_])
                    yield
                    if l < 6:
                        c.op("dve", lambda: nc.vector.tensor_copy(XTn_[:, :], b.XTp[:, :]), reads=[b.XTp], writes=[XTn_])
                        yield
                    c.group("pe", [lambda: nc.tensor.matmul(b.PTp[:, :], identb, PTc[:, :], start=True, stop=False),
                                   lambda: nc.tensor.matmul(b.PTp[:, :], Xn_[:, :], PTc[:, :], start=False, stop=True)],
                            reads=[self.cstb, PTc, Xn_], writes=[b.PTp])
                    yield
                    if l % 2:
                        c.op("act", lambda: nc.scalar.copy(out=PTn_[:, :], in_=b.PTp[:, :]), reads=[b.PTp], writes=[PTn_])
                    else:
                        c.op("dve", lambda: nc.vector.tensor_copy(PTn_[:, :], b.PTp[:, :]), reads=[b.PTp], writes=[PTn_])
                    yield
                    Xc, Xn_ = Xn_, Xc
                    XTc, XTn_ = XTn_, XTc
                    PTc, PTn_ = PTn_, PTc
                c.group("pe", [lambda: nc.tensor.matmul(b.Rp[:, 0:128], PTc[:, :], o.vtok[i][:, ci, :], start=True, stop=True),
                               lambda: nc.tensor.matmul(b.Rp[:, 128:256], PTc[:, :], b.kg[:, :], start=True, stop=True)],
                        reads=[PTc, o.vtok[i], b.kg], writes=[b.Rp])
                yield
                c.op("act", lambda: nc.scalar.activation(out=b.u0[par][:, :], in_=b.Rp[:, 0:128], func=AF.Copy, scale=btcol), reads=[b.Rp, bt_all], writes=[b.u0[par]])
                yield
                c.op("dve", lambda: nc.vector.tensor_scalar(out=b.wsb[:, :], in0=b.Rp[:, 128:256], scalar1=btcol, scalar2=None, op0=ALU.mult), reads=[b.Rp, bt_all], writes=[b.wsb])
                yield
                c.op("pe", lambda: nc.tensor.transpose(b.wTp[:, :], b.wsb[:, :], identb), reads=[b.wsb, self.cstb], writes=[b.wTp])
                yield
                c.op("act", lambda: nc.scalar.copy(out=b.wT[par][:, :], in_=b.wTp[:, :]), reads=[b.wTp], writes=[b.wT[par]])
                yield
                prog.fdone[i] = ci + 1

        def back(i, hp, o):
            b = VB[i]
            vh = 2 * hp + i
            c.op("dve", lambda: nc.vector.memset(b.S[:, :], 0.0), writes=[b.S])
            c.op("dve", lambda: nc.vector.memset(b.Sb[:, :], 0.0), writes=[b.Sb])
            yield
            for ci in range(NC_):
                par = ci % 2
                csl = slice(ci * 128, (ci + 1) * 128)
                while prog.fdone[i] < ci + 1:
                    yield
                c.op("pe", lambda: nc.tensor.matmul(b.wSp[:, :], b.wT[par][:, :], b.Sb[:, :], start=True, stop=True), reads=[b.wT[par], b.Sb], writes=[b.wSp])
                yield
                c.op("dve", lambda: nc.vector.tensor_tensor(b.ub[:, :], b.u0[par][:, :], b.wSp[:, :], ALU.subtract), reads=[b.u0[par], b.wSp], writes=[b.ub])
                yield
                c.group("pe", [lambda: nc.tensor.matmul(b.op_[:, :], b.qd[par][:, :], b.Sb[:, :], start=True, stop=False),
                               lambda: nc.tensor.matmul(b.op_[:, :], b.qkm[par][:, :], b.ub[:, :], start=False, stop=True)],
                        reads=[b.qd[par], b.Sb, b.qkm[par], b.ub], writes=[b.op_])
                yield
                c.op("pe", lambda: nc.tensor.matmul(Snp[i][:, :], b.kd[par][:, :], b.ub[:, :], start=True, stop=True), reads=[b.kd[par], b.ub], writes=[Snp[i]])
                yield
                c.op("dve", lambda: nc.vector.scalar_tensor_tensor(out=b.S[:, :], in0=b.S[:, :], scalar=ecd_all[:, ci, vh:vh + 1], in1=Snp[i][:, :], op0=ALU.mult, op1=ALU.add),
                     reads=[b.S, ecd_all, Snp[i]], writes=[b.S])
                yield
                c.op("pool", lambda: nc.gpsimd.tensor_copy(b.Sb[:, :], b.S[:, :]), reads=[b.S], writes=[b.Sb])
                yield
                c.op("act", lambda: nc.scalar.activation(out=b.junk[:, :], in_=b.op_[:, :], func=AF.Square, accum_out=b.oss[:, 0:1]), reads=[b.op_], writes=[b.junk, b.oss], multi=True)
                yield
                c.op("act", lambda: nc.scalar.activation(out=b.ors[:, :], in_=b.oss[:, :], func=AF.Ln, bias=self.epsc[:, 0:1], scale=1.0 / 128.0),
                     reads=[b.oss, self.epsc], writes=[b.ors])
                yield
                c.op("act", lambda: nc.scalar.activation(out=b.ors[:, :], in_=b.ors[:, :], func=AF.Exp, scale=-0.5), reads=[b.ors], writes=[b.ors])
                yield
                c.op("act", lambda: nc.scalar.activation(out=b.on[:, :], in_=b.op_[:, :], func=AF.Copy, scale=b.ors[:, 0:1]), reads=[b.op_, b.ors], writes=[b.on])
                yield
                c.op("pe", lambda: nc.tensor.transpose(b.oTp[:, :], b.on[:, :], identb), reads=[b.on, self.cstb], writes=[b.oTp])
                yield
                yv = b.y[par]
                c.op("dve", lambda: nc.vector.scalar_tensor_tensor(out=yv[:, :], in0=b.oTp[:, :], scalar=self.pv[:, PV_DNO:PV_DNO + 1], in1=o.sz[i][:, csl], op0=ALU.mult, op1=ALU.mult),
                     reads=[b.oTp, self.pv, o.sz[i]], writes=[yv])
                yield
                c.dma("sp", self.yT_d[vh, :, csl], yv[:, :], reads=[yv])
                yield
                prog.gdone[i] = ci + 1

        run_streams([prologue(pairs[0], OUT[0])])
        for n, hp in enumerate(pairs):
            o = OUT[n % 2]
            prog.fdone = [0, 0]
            prog.gdone = [0, 0]
            prog.fread = [0, 0]
            prog.common = 0
            gens = [front(0, hp, o), front(1, hp, o), back(0, hp, o), back(1, hp, o)]
            if n + 1 < len(pairs):
                gens.append(prologue(pairs[n + 1], OUT[(n + 1) % 2]))
            run_streams(gens)
        if len(pairs) < 12:
            with Scope(c) as sz_:
                zt = sz_.sb("zt", [128, T], BF16)
                c.op("dve", lambda: nc.vector.memset(zt[:, :], 0.0), writes=[zt])
                for h in range(NH):
                    if h // 2 not in pairs:
                        c.dma("sp", self.yT_d[h, :, :], zt[:, :], reads=[zt])
                c.barrier()
        c.barrier()


Layers.dn_heads2 = dn_heads2


def dn_heads3(self, hT, W, psb):
    c = self.ctx
    nc = self.nc
    pairs = self.dbg.get("pairs")
    pairs = list(range(self.nh // 2)) if pairs is None else pairs
    NC_ = T // 128
    with Scope(c) as sc:
        onesf = self.cst[:, C_ONES:C_ONES + 128]
        trif = self.cst[:, C_TRI:C_TRI + 128]
        negi = self.cst[:, C_NEGI:C_NEGI + 128]
        strictf = self.cst[:, C_STRICT:C_STRICT + 128]
        identf = self.cst[:, C_ID:C_ID + 128]
        identb = self.cstb[:, C_ID:C_ID + 128]
        g_all = sc.sb("g_all", [128, NC_, 24], F32)
        bt_all = sc.sb("bt_all", [128, NC_, 24], F32)
        nbt_all = sc.sb("nbt_all", [128, NC_, 24], F32)
        gc_all = sc.sb("gc_all", [128, NC_, 24], F32)
        egc_all = sc.sb("egc_all", [128, NC_, 24], F32)
        ekd_all = sc.sb("ekd_all", [128, NC_, 24], F32)
        ecd_all = sc.sb("ecd_all", [128, NC_, 24], F32)
        bankA = sc.ps("bankA", [128, 512], F32)
        bankB = [sc.ps("bankB%d" % i, [128, 512], F32) for i in range(2)]
        bankC = [sc.ps("bankC%d" % i, [128, 512], F32) for i in range(2)]
        bankE = sc.ps("bankE", [128, 1024], BF16)
        with Scope(c) as s0:
            negA = s0.sb("negA", [128, 24], F32)
            wabf = s0.sb("wabf", [128, 768], F32)
            wabb = s0.sb("wabb", [128, 768], BF16)
            gtmp = s0.sb("gtmp", [128, 48], F32)
            c.dma("sp", wabf[:, :], W["wab"][:, :], writes=[wabf])
            c.op("dve", lambda: nc.vector.tensor_copy(wabb[:, :], wabf[:, :]), reads=[wabf], writes=[wabb])
            c.op("act", lambda: nc.scalar.activation(out=negA[:, :], in_=self.pv[:, PV_ALOG:PV_ALOG + 24], func=AF.Exp), reads=[self.pv], writes=[negA])
            c.op("dve", lambda: nc.vector.tensor_scalar(out=negA[:, :], in0=negA[:, :], scalar1=-1.0, scalar2=None, op0=ALU.mult), reads=[negA], writes=[negA])
            abp = view("abp", bankA[:, 0:48], bankA)
            gcp = view("gcp", bankB[0][:, 0:24], bankB[0])
            glp = view("glp", bankC[0][:, 0:24], bankC[0])
            for tt in range(NC_):
                c.group("pe", [(lambda kt=kt: nc.tensor.matmul(abp[:, :], hT[:, kt, tt * 128:(tt + 1) * 128], wabb[:, kt * 48:(kt + 1) * 48],
                                                              start=(kt == 0), stop=(kt == KT - 1))) for kt in range(KT)],
                        reads=[hT, wabb], writes=[abp])
                c.op("dve", lambda: nc.vector.tensor_tensor(gtmp[:, 0:24], abp[:, 0:24], self.pv[:, PV_DTB:PV_DTB + 24], ALU.add), reads=[abp, self.pv], writes=[gtmp])
                c.op("dve", lambda: nc.vector.tensor_scalar(out=gtmp[:, 24:48], in0=abp[:, 24:48], scalar1=-1.0, scalar2=None, op0=ALU.mult), reads=[abp], writes=[gtmp])
                c.op("act", lambda: nc.scalar.activation(out=gtmp[:, :], in_=gtmp[:, :], func=AF.Exp), reads=[gtmp], writes=[gtmp])
                c.op("act", lambda: nc.scalar.activation(out=gtmp[:, :], in_=gtmp[:, :], func=AF.Ln, bias=self.onec[:, 0:1], scale=1.0), reads=[gtmp, self.onec], writes=[gtmp])
                c.op("dve", lambda: nc.vector.tensor_tensor(g_all[:, tt, :], gtmp[:, 0:24], negA[:, :], ALU.mult), reads=[gtmp, negA], writes=[g_all])
                c.op("act", lambda: nc.scalar.activation(out=bt_all[:, tt, :], in_=gtmp[:, 24:48], func=AF.Exp, scale=-1.0), reads=[gtmp], writes=[bt_all])
                c.op("pool", lambda: _pool_scale(nc, nbt_all[:, tt, :], bt_all[:, tt, :], -1.0), reads=[bt_all], writes=[nbt_all])
                c.op("pe", lambda: nc.tensor.matmul(gcp[:, :], trif, g_all[:, tt, :], start=True, stop=True), reads=[self.cst, g_all], writes=[gcp])
                c.op("pe", lambda: nc.tensor.matmul(glp[:, :], onesf, g_all[:, tt, :], start=True, stop=True), reads=[self.cst, g_all], writes=[glp])
                c.op("act", lambda: nc.scalar.copy(out=gc_all[:, tt, :], in_=gcp[:, :]), reads=[gcp], writes=[gc_all])
                c.op("act", lambda: nc.scalar.activation(out=egc_all[:, tt, :], in_=gcp[:, :], func=AF.Exp), reads=[gcp], writes=[egc_all])
                c.op("act", lambda: nc.scalar.activation(out=ecd_all[:, tt, :], in_=glp[:, :], func=AF.Exp), reads=[glp], writes=[ecd_all])
                c.op("dve", lambda: nc.vector.tensor_tensor(ekd_all[:, tt, :], glp[:, :], gc_all[:, tt, :], ALU.subtract), reads=[glp, gc_all], writes=[ekd_all])
                c.op("act", lambda: nc.scalar.activation(out=ekd_all[:, tt, :], in_=ekd_all[:, tt, :], func=AF.Exp), reads=[ekd_all], writes=[ekd_all])
            c.barrier()
        raw = sc.sb("raw", [128, T + 3], F32)
        acc = sc.sb("acc", [128, 512], F32)
        stmp = sc.sb("stmp", [128, 512], F32)
        sil = acc
        sqb = stmp
        rstd = sc.sb("rstd", [128, 512], F32)
        vTc = sc.sb("vTc", [128, 512], BF16)
        c.op("dve", lambda: nc.vector.memset(raw[:, 0:3], 0.0), writes=[raw])
        OUT = []
        for i in range(2):
            o = _B()
            o.qT = sc.sb("qT%d" % i, [128, T], BF16)
            o.kT = sc.sb("kT%d" % i, [128, T], BF16)
            o.ktok = sc.sb("ktok%d" % i, [128, NC_, 128], BF16)
            o.vtok = [sc.sb("vtok%d_%d" % (i, j), [128, NC_, 128], BF16) for j in range(2)]
            o.sz = [sc.sb("sz%d_%d" % (i, j), [128, T], BF16) for j in range(2)]
            OUT.append(o)
        kkp = view("kkp", bankA[:, 256:384], bankA)
        qkp = view("qkp", bankA[:, 384:512], bankA)
        Snp = view("Snp", bankA[:, 0:256], bankA)
        tp = [view("tp%d" % i, bankE[:, i * 128:(i + 1) * 128], bankE) for i in range(8)]
        XXp = view("XXp", bankB[0][:, :], bankB[0])
        PTp = view("PTp", bankB[1][:, 0:256], bankB[1])
        gcB = view("gcB", bankB[1][:, 256:512], bankB[1])
        Rp = view("Rp", bankC[0][:, :], bankC[0])
        wSp = view("wSp", bankC[1][:, 0:256], bankC[1])
        op_ = view("op", bankC[1][:, 256:512], bankC[1])
        X0Tp = view("X0Tp", bankE[:, 0:256], bankE)
        wTp = view("wTp", bankE[:, 256:512], bankE)
        oTp = view("oTp", bankE[:, 512:768], bankE)
        rhsg = sc.sb("rhsg", [128, 256], F32)
        argT = sc.sb("argT", [128, 256], F32)
        DT = sc.sb("DT", [128, 256], F32)
        egB = sc.sb("egB", [128, 256], F32)
        tmp = sc.sb("tmp", [128, 256], F32)
        XX = [sc.sb("XX%d" % j, [128, 512], F32) for j in range(2)]
        PT = [sc.sb("PT%d" % j, [128, 256], F32) for j in range(2)]
        PTb16 = sc.sb("PTb16", [128, 256], BF16)
        kg = sc.sb("kg", [128, 256], BF16)
        wsb = sc.sb("wsb", [128, 256], BF16)
        ub = sc.sb("ub", [128, 256], BF16)
        on = sc.sb("on", [128, 256], BF16)
        junk = sc.sb("junk", [128, 128], BF16)
        S = sc.sb("S", [128, 256], F32)
        Sb = sc.sb("Sb", [128, 256], BF16)
        oss = sc.sb("oss", [128, 2], F32)
        ors = sc.sb("ors", [128, 2], F32)
        u0 = [sc.sb("u0_%d" % j, [128, 256], F32) for j in range(2)]
        wT = [sc.sb("wT_%d" % j, [128, 256], BF16) for j in range(2)]
        qd = [sc.sb("qd_%d" % j, [128, 256], BF16) for j in range(2)]
        qkm = [sc.sb("qkm_%d" % j, [128, 256], BF16) for j in range(2)]
        kd = [sc.sb("kd_%d" % j, [128, 256], BF16) for j in range(2)]
        yv = [sc.sb("yv_%d" % j, [128, 256], BF16) for j in range(2)]
        ptp = [tp[6], tp[7]]
        st = _B()
        st.pj = 0
        st.tpj = 0

        def silu_to(src_ap, n, dst_ap, reads, dst_tile):
            c.op("act", lambda: nc.scalar.activation(out=stmp[:, 0:n], in_=src_ap, func=AF.Exp, scale=-1.0), reads=reads, writes=[stmp])
            yield
            c.op("act", lambda: nc.scalar.activation(out=stmp[:, 0:n], in_=stmp[:, 0:n], func=AF.Ln, bias=self.onec[:, 0:1], scale=1.0), reads=[stmp, self.onec], writes=[stmp])
            yield
            c.op("act", lambda: nc.scalar.activation(out=stmp[:, 0:n], in_=stmp[:, 0:n], func=AF.Exp, scale=-1.0), reads=[stmp], writes=[stmp])
            yield
            c.op("dve", lambda: nc.vector.tensor_tensor(dst_ap, src_ap, stmp[:, 0:n], ALU.mult), reads=reads + [stmp], writes=[dst_tile])
            yield

        def prologue(hp, o):
            idxs = [W["q_idx"][hp], W["k_idx"][hp], W["v_idx"][2 * hp], W["v_idx"][2 * hp + 1], W["z_idx"][2 * hp], W["z_idx"][2 * hp + 1]]
            cbis = [hp, 12 + hp, 24 + 2 * hp, 24 + 2 * hp + 1]
            slot = self.w_fetch(W["w"][idxs[0], :, :])
            for bi in range(6):
                wt = self.w_cast(slot)
                if bi + 1 < 6:
                    slot = self.w_fetch(W["w"][idxs[bi + 1], :, :])
                yield
                for tc in range(4):
                    sl = slice(tc * 512, (tc + 1) * 512)
                    p = psb[st.pj]
                    st.pj ^= 1
                    nparts = KT // PSL
                    for part in range(nparts):
                        c.group("pe", [(lambda kt=kt: nc.tensor.matmul(p[:, :], wt[:, kt * 128:(kt + 1) * 128], hT[:, kt, sl],
                                                                      start=(kt == 0), stop=(kt == KT - 1))) for kt in range(part * PSL, part * PSL + PSL)],
                                reads=[wt, hT], writes=[p] if part in (0, nparts - 1) else [])
                        yield
                    if bi >= 4:
                        yield from silu_to(p[:, :], 512, o.sz[bi - 4][:, sl], [p], o.sz[bi - 4])
                        continue
                    c.op("act", lambda: nc.scalar.copy(out=raw[:, 3 + tc * 512:3 + (tc + 1) * 512], in_=p[:, :]), reads=[p], writes=[raw])
                    yield
                    wc = [self.pv[:, PV_CONV + cbis[bi] * 4 + k:PV_CONV + cbis[bi] * 4 + k + 1] for k in range(4)]
                    c.op("dve", lambda: nc.vector.tensor_scalar(out=acc[:, :], in0=raw[:, 3 + tc * 512:3 + (tc + 1) * 512], scalar1=wc[3], scalar2=None, op0=ALU.mult),
                         reads=[raw, self.pv], writes=[acc])
                    yield
                    for k in (2, 1, 0):
                        c.op("dve", lambda: nc.vector.scalar_tensor_tensor(out=acc[:, :], in0=raw[:, k + tc * 512:k + (tc + 1) * 512], scalar=wc[k], in1=acc[:, :],
                                                                           op0=ALU.mult, op1=ALU.add), reads=[raw, acc, self.pv], writes=[acc])
                        yield
                    if bi < 2:
                        yield from silu_to(acc[:, :], 512, sil[:, :], [acc], sil)
                        c.op("act", lambda: nc.scalar.activation(out=sqb[:, :], in_=sil[:, :], func=AF.Square), reads=[sil], writes=[sqb])
                        yield
                        p2 = psb[st.pj]
                        st.pj ^= 1
                        c.op("pe", lambda: nc.tensor.matmul(p2[:, :], onesf, sqb[:, :], start=True, stop=True), reads=[self.cst, sqb], writes=[p2])
                        yield
                        c.op("act", lambda: nc.scalar.activation(out=rstd[:, :], in_=p2[:, :], func=AF.Ln, bias=self.epsc[:, 0:1], scale=1.0), reads=[p2, self.epsc], writes=[rstd])
                        yield
                        c.op("act", lambda: nc.scalar.activation(out=rstd[:, :], in_=rstd[:, :], func=AF.Exp, scale=-0.5), reads=[rstd], writes=[rstd])
                        yield
                        dst = o.qT if bi == 0 else o.kT
                        gcol = self.gsc[:, 5:6] if bi == 0 else self.gsc[:, 6:7]
                        c.op("dve", lambda: nc.vector.scalar_tensor_tensor(out=dst[:, sl], in0=sil[:, :], scalar=gcol, in1=rstd[:, :], op0=ALU.mult, op1=ALU.mult),
                             reads=[sil, rstd, self.gsc], writes=[dst])
                        yield
                        if bi == 1:
                            for j in range(4):
                                tk = ptp[st.tpj]
                                st.tpj ^= 1
                                c.op("pe", lambda: nc.tensor.transpose(tk[:, :], dst[:, tc * 512 + j * 128:tc * 512 + (j + 1) * 128], identb), reads=[dst, self.cstb], writes=[tk])
                                yield
                                c.op("act", lambda: nc.scalar.copy(out=o.ktok[:, tc * 4 + j, :], in_=tk[:, :]), reads=[tk], writes=[o.ktok])
                                yield
                    else:
                        yield from silu_to(acc[:, :], 512, vTc[:, :], [acc], vTc)
                        for j in range(4):
                            tk = ptp[st.tpj]
                            st.tpj ^= 1
                            c.op("pe", lambda: nc.tensor.transpose(tk[:, :], vTc[:, j * 128:(j + 1) * 128], identb), reads=[vTc, self.cstb], writes=[tk])
                            yield
                            c.op("act", lambda: nc.scalar.copy(out=o.vtok[bi - 2][:, tc * 4 + j, :], in_=tk[:, :]), reads=[tk], writes=[o.vtok[bi - 2]])
                            yield

        prog = _B()
        H = (slice(0, 128), slice(128, 256))

        def front(hp, o):
            for ci in range(NC_):
                par = ci % 2
                csl = slice(ci * 128, (ci + 1) * 128)
                while prog.gdone < ci - 1:
                    yield
                vh = [2 * hp, 2 * hp + 1]
                c.op("pe", lambda: nc.tensor.matmul(kkp[:, :], o.kT[:, csl], o.kT[:, csl], start=True, stop=True), reads=[o.kT], writes=[kkp])
                yield
                c.op("pe", lambda: nc.tensor.matmul(qkp[:, :], o.kT[:, csl], o.qT[:, csl], start=True, stop=True), reads=[o.kT, o.qT], writes=[qkp])
                yield
                for h in range(2):
                    c.op("pool", lambda: _pool_scale(nc, rhsg[:, H[h]], trif, g_all[:, ci, vh[h]:vh[h] + 1]), reads=[self.cst, g_all], writes=[rhsg])
                    yield
                c.op("pe", lambda: nc.tensor.matmul(gcB[:, :], onesf, rhsg[:, :], start=True, stop=True), reads=[self.cst, rhsg], writes=[gcB])
                yield
                for h in range(2):
                    c.op("dve", lambda: nc.vector.scalar_tensor_tensor(out=argT[:, H[h]], in0=gcB[:, H[h]], scalar=gc_all[:, ci, vh[h]:vh[h] + 1], in1=negi,
                                                                       op0=ALU.subtract, op1=ALU.add), reads=[gcB, gc_all, self.cst], writes=[argT])
                    yield
                c.op("act", lambda: nc.scalar.activation(out=egB[:, :], in_=gcB[:, :], func=AF.Exp), reads=[gcB], writes=[egB])
                yield
                c.op("act", lambda: nc.scalar.activation(out=DT[:, :], in_=argT[:, :], func=AF.Exp), reads=[argT], writes=[DT])
                yield
                cur, nxt = XX[0], XX[1]
                PTc, PTn = PT[0], PT[1]
                for h in range(2):
                    c.op("dve", lambda: nc.vector.tensor_tensor(qkm[par][:, H[h]], qkp[:, :], DT[:, H[h]], ALU.mult), reads=[qkp, DT], writes=[qkm[par]])
                    yield
                    c.op("pool", lambda: nc.gpsimd.tensor_tensor(tmp[:, H[h]], DT[:, H[h]], strictf, ALU.mult), reads=[DT, self.cst], writes=[tmp])
                    yield
                    c.op("dve", lambda: nc.vector.scalar_tensor_tensor(out=cur[:, 256 + h * 128:256 + (h + 1) * 128].bitcast(F32R), in0=kkp[:, :],
                                                                       scalar=nbt_all[:, ci, vh[h]:vh[h] + 1], in1=tmp[:, H[h]], op0=ALU.mult, op1=ALU.mult),
                         reads=[kkp, nbt_all, tmp], writes=[cur])
                    yield
                    c.op("dve", lambda: nc.vector.tensor_tensor(PTc[:, H[h]].bitcast(F32R), cur[:, 256 + h * 128:256 + (h + 1) * 128], identf, ALU.add),
                         reads=[cur, self.cst], writes=[PTc])
                    yield
                c.group("pe", [(lambda h=h: nc.tensor.transpose(XXp[:, H[h]], cur[:, 256 + h * 128:256 + (h + 1) * 128], identf)) for h in range(2)],
                        reads=[cur, self.cst], writes=[XXp])
                yield
                c.op("act", lambda: nc.scalar.copy(out=cur[:, 0:256].bitcast(F32R), in_=XXp[:, 0:256]), reads=[XXp], writes=[cur])
                yield
                for h in range(2):
                    c.op("pool", lambda: nc.gpsimd.tensor_tensor(qd[par][:, H[h]], o.qT[:, csl], egB[:, H[h]], ALU.mult), reads=[o.qT, egB], writes=[qd[par]])
                    yield
                    c.op("pool", lambda: _pool_scale(nc, kg[:, H[h]], o.ktok[:, ci, :], egc_all[:, ci, vh[h]:vh[h] + 1]), reads=[o.ktok, egc_all], writes=[kg])
                    yield
                    c.op("pool", lambda: _pool_scale(nc, kd[par][:, H[h]], o.ktok[:, ci, :], ekd_all[:, ci, vh[h]:vh[h] + 1]), reads=[o.ktok, ekd_all], writes=[kd[par]])
                    yield
                for l in range(1, 7):
                    nx = 2 if l < 6 else 1
                    mm = []
                    for h in range(2):
                        mm.append(lambda h=h: nc.tensor.matmul(XXp[:, H[h]], cur[:, 256 + h * 128:256 + (h + 1) * 128].bitcast(F32R), cur[:, H[h]].bitcast(F32R), start=True, stop=True))
                    if l < 6:
                        for h in range(2):
                            mm.append(lambda h=h: nc.tensor.matmul(XXp[:, 256 + h * 128:256 + (h + 1) * 128], cur[:, H[h]].bitcast(F32R),
                                                                   cur[:, 256 + h * 128:256 + (h + 1) * 128].bitcast(F32R), start=True, stop=True))
                    c.group("pe", mm, reads=[cur], writes=[XXp])
                    yield
                    w_ = 256 * nx
                    c.op("act", lambda: nc.scalar.copy(out=nxt[:, 0:w_].bitcast(F32R), in_=XXp[:, 0:w_]), reads=[XXp], writes=[nxt])
                    yield
                    c.group("pe", [(lambda h=h: nc.tensor.matmul(PTp[:, H[h]], nxt[:, H[h]].bitcast(F32R), PTc[:, H[h]].bitcast(F32R), start=True, stop=True)) for h in range(2)],
                            reads=[PTc, nxt], writes=[PTp])
                    yield
                    if l < 6:
                        c.op("dve", lambda: nc.vector.tensor_tensor(PTn[:, :].bitcast(F32R), PTp[:, :], PTc[:, :], ALU.add), reads=[PTp, PTc], writes=[PTn])
                    else:
                        c.op("dve", lambda: nc.vector.tensor_tensor(PTb16[:, :], PTp[:, :], PTc[:, :], ALU.add), reads=[PTp, PTc], writes=[PTb16])
                    yield
                    cur, nxt = nxt, cur
                    PTc, PTn = PTn, PTc
                PTc = PTb16
                mm = []
                for h in range(2):
                    mm.append(lambda h=h: nc.tensor.matmul(Rp[:, 256 * h:256 * h + 128], PTc[:, H[h]], o.vtok[h][:, ci, :], start=True, stop=True))
                    mm.append(lambda h=h: nc.tensor.matmul(Rp[:, 256 * h + 128:256 * h + 256], PTc[:, H[h]], kg[:, H[h]], start=True, stop=True))
                c.group("pe", mm, reads=[PTc, o.vtok[0], o.vtok[1], kg], writes=[Rp])
                yield
                for h in range(2):
                    btcol = bt_all[:, ci, vh[h]:vh[h] + 1]
                    c.op("act", lambda: nc.scalar.activation(out=u0[par][:, H[h]], in_=Rp[:, 256 * h:256 * h + 128], func=AF.Copy, scale=btcol), reads=[Rp, bt_all], writes=[u0[par]])
                    yield
                    c.op("dve", lambda: nc.vector.tensor_scalar(out=wsb[:, H[h]], in0=Rp[:, 256 * h + 128:256 * h + 256], scalar1=btcol, scalar2=None, op0=ALU.mult),
                         reads=[Rp, bt_all], writes=[wsb])
                    yield
                c.group("pe", [(lambda h=h: nc.tensor.transpose(wTp[:, H[h]], wsb[:, H[h]], identb)) for h in range(2)], reads=[wsb, self.cstb], writes=[wTp])
                yield
                c.op("act", lambda: nc.scalar.copy(out=wT[par][:, :], in_=wTp[:, :]), reads=[wTp], writes=[wT[par]])
                yield
                prog.fdone = ci + 1

        def back(hp, o):
            vh = [2 * hp, 2 * hp + 1]
            c.op("dve", lambda: nc.vector.memset(S[:, :], 0.0), writes=[S])
            c.op("dve", lambda: nc.vector.memset(Sb[:, :], 0.0), writes=[Sb])
            yield
            for ci in range(NC_):
                par = ci % 2
                csl = slice(ci * 128, (ci + 1) * 128)
                while prog.fdone < ci + 1:
                    yield
                c.group("pe", [(lambda h=h: nc.tensor.matmul(wSp[:, H[h]], wT[par][:, H[h]], Sb[:, H[h]], start=True, stop=True)) for h in range(2)],
                        reads=[wT[par], Sb], writes=[wSp])
                yield
                c.op("dve", lambda: nc.vector.tensor_tensor(ub[:, :], u0[par][:, :], wSp[:, :], ALU.subtract), reads=[u0[par], wSp], writes=[ub])
                yield
                mm = []
                for h in range(2):
                    mm.append(lambda h=h: nc.tensor.matmul(op_[:, H[h]], qd[par][:, H[h]], Sb[:, H[h]], start=True, stop=False))
                    mm.append(lambda h=h: nc.tensor.matmul(op_[:, H[h]], qkm[par][:, H[h]], ub[:, H[h]], start=False, stop=True))
                c.group("pe", mm, reads=[qd[par], Sb, qkm[par], ub], writes=[op_])
                yield
                c.group("pe", [(lambda h=h: nc.tensor.matmul(Snp[:, H[h]], kd[par][:, H[h]], ub[:, H[h]], start=True, stop=True)) for h in range(2)],
                        reads=[kd[par], ub], writes=[Snp])
                yield
                for h in range(2):
                    c.op("dve", lambda: nc.vector.scalar_tensor_tensor(out=S[:, H[h]], in0=S[:, H[h]], scalar=ecd_all[:, ci, vh[h]:vh[h] + 1], in1=Snp[:, H[h]],
                                                                       op0=ALU.mult, op1=ALU.add), reads=[S, ecd_all, Snp], writes=[S])
                    yield
                c.op("pool", lambda: nc.gpsimd.tensor_copy(Sb[:, :], S[:, :]), reads=[S], writes=[Sb])
                yield
                for h in range(2):
                    c.op("act", lambda: nc.scalar.activation(out=junk[:, :], in_=op_[:, H[h]], func=AF.Square, accum_out=oss[:, h:h + 1]), reads=[op_], writes=[junk, oss], multi=True)
                    yield
                c.op("act", lambda: nc.scalar.activation(out=ors[:, :], in_=oss[:, :], func=AF.Ln, bias=self.epsc[:, 0:1], scale=1.0 / 128.0), reads=[oss, self.epsc], writes=[ors])
                yield
                c.op("act", lambda: nc.scalar.activation(out=ors[:, :], in_=ors[:, :], func=AF.Exp, scale=-0.5), reads=[ors], writes=[ors])
                yield
                for h in range(2):
                    c.op("act", lambda: nc.scalar.activation(out=on[:, H[h]], in_=op_[:, H[h]], func=AF.Copy, scale=ors[:, h:h + 1]), reads=[op_, ors], writes=[on])
                    yield
                c.group("pe", [(lambda h=h: nc.tensor.transpose(oTp[:, H[h]], on[:, H[h]], identb)) for h in range(2)], reads=[on, self.cstb], writes=[oTp])
                yield
                y_ = yv[par]
                for h in range(2):
                    c.op("dve", lambda: nc.vector.scalar_tensor_tensor(out=y_[:, H[h]], in0=oTp[:, H[h]], scalar=self.pv[:, PV_DNO:PV_DNO + 1], in1=o.sz[h][:, csl],
                                                                       op0=ALU.mult, op1=ALU.mult), reads=[oTp, self.pv, o.sz[h]], writes=[y_])
                    yield
                c.dma("sp", self.yT_d[2 * hp:2 * hp + 2, :, csl].rearrange("i p t -> p i t"), y_[:, :].rearrange("p (i t) -> p i t", i=2), reads=[y_])
                yield
                prog.gdone = ci + 1

        run_streams([prologue(pairs[0], OUT[0])])
        for n, hp in enumerate(pairs):
            o = OUT[n % 2]
            prog.fdone = 0
            prog.gdone = 0
            gens = [front(hp, o), back(hp, o)]
            wts = [DN_W[0], DN_W[1]]
            if n + 1 < len(pairs):
                gens.append(prologue(pairs[n + 1], OUT[(n + 1) % 2]))
                wts.append(DN_W[2])
            run_streams(gens, wts)
        if len(pairs) < self.nh // 2:
            with Scope(c) as sz_:
                zt = sz_.sb("zt", [128, 512], BF16)
                c.op("dve", lambda: nc.vector.memset(zt[:, :], 0.0), writes=[zt])
                for h in range(self.nh):
                    if h // 2 not in pairs:
                        for q4 in range(4):
                            c.dma("sp", self.yT_d[h, :, q4 * 512:(q4 + 1) * 512], zt[:, :], reads=[zt])
                c.barrier()
        c.barrier()


Layers.dn_heads3 = dn_heads3


def xa_heads2(self, li, hT, w_d, xq_idx, z_idx, kxT, vx, psb):
    c = self.ctx
    nc = self.nc
    with Scope(c) as sc:
        banks = list(psb) + [sc.ps("xbk%d" % i, [128, 512], F32) for i in range(6)]
        onesb = self.cstb[:, C_ONES:C_ONES + 128]
        onesf = self.cst[:, C_ONES:C_ONES + 128]

        def stream(s, hs):
            for _ in range(XAD * s):
                yield
            b0, b1, b2, b3 = banks[4 * s:4 * s + 4]
            acc_b = [b0, b1]
            wst = self.wst[s]
            wts = [sc.sb("xw%d_%d" % (s, i), [128, 2048], BF16) for i in range(4)]
            qraw = [sc.sb("xqr%d_%d" % (s, i), [128, 512], F32) for i in range(2)]
            sq = [sc.sb("xsq%d_%d" % (s, i), [128, 512], F32) for i in range(2)]
            qn = [sc.sb("xqn%d_%d" % (s, i), [128, 512], BF16) for i in range(2)]
            sz = [sc.sb("xsz%d_%d" % (s, i), [128, 512], BF16) for i in range(2)]
            yb = [sc.sb("xyb%d_%d" % (s, i), [128, 512], BF16) for i in range(2)]
            pe_ = [sc.sb("xpe%d_%d" % (s, i), [128, 512], BF16) for i in range(2)]
            stmp = sc.sb("xst%d" % s, [128, 512], F32)
            rstd = sc.sb("xrs%d" % s, [128, 512], F32)
            rden = sc.sb("xrd%d" % s, [128, 512], F32)
            t1 = sc.sb("xt1%d" % s, [128, 512], F32)
            pj = 0
            for a in hs:
                idxs = [xq_idx[a * 2], xq_idx[a * 2 + 1], z_idx[a * 2], z_idx[a * 2 + 1]]
                for i in range(4):
                    c.dma("sp", wst[:, :], w_d[idxs[i], :, :], writes=[wst])
                    c.op("pool", lambda: nc.gpsimd.tensor_copy(wts[i][:, :], wst[:, :]), reads=[wst], writes=[wts[i]])
                    yield
                for tc in range(4):
                    sl = slice(tc * 512, (tc + 1) * 512)
                    for i in range(4):
                        p = acc_b[pj]
                        pj ^= 1
                        for part in range(4):
                            c.group("pe", [(lambda kt=kt: nc.tensor.matmul(p[:, :], wts[i][:, kt * 128:(kt + 1) * 128], hT[:, kt, sl],
                                                                          start=(kt == 0), stop=(kt == KT - 1))) for kt in range(part * 4, part * 4 + 4)],
                                    reads=[wts[i], hT], writes=[p] if part in (0, 3) else [], lhs=[wts[i]])
                            yield
                        if i < 2:
                            c.op("act", lambda: nc.scalar.copy(out=qraw[i][:, :], in_=p[:, :]), reads=[p], writes=[qraw[i]])
                            yield
                            c.op("act", lambda: nc.scalar.activation(out=sq[i][:, :], in_=qraw[i][:, :], func=AF.Square), reads=[qraw[i]], writes=[sq[i]])
                            yield
                        else:
                            j = i - 2
                            c.op("act", lambda: nc.scalar.activation(out=stmp[:, :], in_=p[:, :], func=AF.Exp, scale=-1.0), reads=[p], writes=[stmp])
                            yield
                            c.op("act", lambda: nc.scalar.activation(out=stmp[:, :], in_=stmp[:, :], func=AF.Ln, bias=self.onec[:, 0:1], scale=1.0), reads=[stmp, self.onec], writes=[stmp])
                            yield
                            c.op("act", lambda: nc.scalar.activation(out=stmp[:, :], in_=stmp[:, :], func=AF.Exp, scale=-1.0), reads=[stmp], writes=[stmp])
                            yield
                            c.op("dve", lambda: nc.vector.tensor_tensor(sz[j][:, :], p[:, :], stmp[:, :], ALU.mult), reads=[p, stmp], writes=[sz[j]])
                            yield
                    c.group("pe", [(lambda j=j: nc.tensor.matmul(b2[:, :], onesf, sq[j][:, :], start=(j == 0), stop=(j == 1))) for j in range(2)],
                            reads=[self.cst, sq[0], sq[1]], writes=[b2])
                    yield
                    c.op("act", lambda: nc.scalar.activation(out=rstd[:, :], in_=b2[:, :], func=AF.Ln, bias=self.epsc[:, 0:1], scale=1.0 / 256.0), reads=[b2, self.epsc], writes=[rstd])
                    yield
                    c.op("act", lambda: nc.scalar.activation(out=rstd[:, :], in_=rstd[:, :], func=AF.Exp, scale=-0.5), reads=[rstd], writes=[rstd])
                    yield
                    for j in range(2):
                        gcol = self.gsc[:, GS_XQ + li * 2 + j:GS_XQ + li * 2 + j + 1]
                        c.op("dve", lambda: nc.vector.scalar_tensor_tensor(out=qn[j][:, :], in0=qraw[j][:, :], scalar=gcol, in1=rstd[:, :], op0=ALU.mult, op1=ALU.mult),
                             reads=[qraw[j], rstd, self.gsc], writes=[qn[j]])
                        yield
                    for mt in range(2):
                        sp_ = acc_b[mt]
                        c.group("pe", [(lambda j=j: nc.tensor.matmul(sp_[:, :], kxT[:, a * 2 + j, mt * 128:(mt + 1) * 128], qn[j][:, :],
                                                                    start=(j == 0), stop=(j == 1))) for j in range(2)],
                                reads=[kxT, qn[0], qn[1]], writes=[sp_], lhs=[kxT])
                        yield
                        c.op("act", lambda: nc.scalar.activation(out=pe_[mt][:, :], in_=sp_[:, :], func=AF.Exp), reads=[sp_], writes=[pe_[mt]])
                        yield
                    c.group("pe", [(lambda mt=mt: nc.tensor.matmul(b2[:, :], onesb, pe_[mt][:, :], start=(mt == 0), stop=(mt == 1))) for mt in range(2)],
                            reads=[self.cstb, pe_[0], pe_[1]], writes=[b2], lhs=[self.cstb])
                    yield
                    c.op("dve", lambda: nc.vector.reciprocal(rden[:, :], b2[:, :]), reads=[b2], writes=[rden])
                    yield
                    for eb in range(2):
                        c.group("pe", [(lambda mt=mt: nc.tensor.matmul(b3[:, :], vx[:, mt, a * 256 + eb * 128:a * 256 + (eb + 1) * 128], pe_[mt][:, :],
                                                                      start=(mt == 0), stop=(mt == 1))) for mt in range(2)],
                                reads=[vx, pe_[0], pe_[1]], writes=[b3], lhs=[vx])
                        yield
                        c.op("dve", lambda: nc.vector.tensor_tensor(t1[:, :], b3[:, :], rden[:, :], ALU.mult), reads=[b3, rden], writes=[t1])
                        yield
                        c.op("pool", lambda: nc.gpsimd.tensor_tensor(yb[eb][:, :], t1[:, :], sz[eb][:, :], ALU.mult), reads=[t1, sz[eb]], writes=[yb[eb]])
                        yield
                        c.dma("sp", self.yT_d[self.nh + a * 2 + eb, :, sl], yb[eb][:, :], reads=[yb[eb]])
                        yield

        hs = list(range(self.nxa))
        run_streams([stream(0, hs[0::2]), stream(1, hs[1::2])])
        c.barrier()


Layers.xa_heads2 = xa_heads2
```

```python
import numpy as np
import concourse.bass as bass
import concourse.mybir as mybir
from concourse.bass_utils import run_bass_kernel_spmd

F32 = mybir.dt.float32
BF16 = mybir.dt.bfloat16
F32R = mybir.dt.float32r
AF = mybir.ActivationFunctionType
ALU = mybir.AluOpType
AX = mybir.AxisListType

N_DMA_SEMS = 24
EMBED_WAITS = True
SEM_SKIP = 0
SB_K = 2
SBP = 2
XAD = 0
SB_DELAY = 300
DN_W = (3, 1, 1)
PSL = 4


class Tile:
    __slots__ = ("name", "h", "last_w", "readers", "space", "root")

    def __init__(self, name, h, space, root=None):
        self.name = name
        self.h = h
        self.space = space
        self.last_w = None
        self.readers = {}
        self.root = self if root is None else root.root

    def __getitem__(self, idx):
        return self.h[idx]


class Ctx:
    def __init__(self, nc):
        self.nc = nc
        self.E = {"pe": nc.tensor, "act": nc.scalar, "dve": nc.vector, "pool": nc.gpsimd, "sp": nc.sync}
        self.sem = {}
        self.cnt = {}
        self._skip = [nc.alloc_semaphore("skip%d" % i) for i in range(SEM_SKIP)]
        for e in self.E:
            self.sem[e] = nc.alloc_semaphore("s_" + e)
            self.cnt[e] = 0
        self.dsem = [nc.alloc_semaphore("d%d" % i) for i in range(N_DMA_SEMS)]
        self.dcnt = [0] * N_DMA_SEMS
        self.dnext = 0
        self.seen = {e: {} for e in self.E}
        self.n_inst = 0
        self.n_wait = 0

    def sb(self, name, shape, dtype):
        return Tile(name, self.nc.alloc_sbuf_tensor(name, list(shape), dtype), "sb")

    def ps(self, name, shape, dtype=F32):
        return Tile(name, self.nc.alloc_psum_tensor(name, list(shape), dtype), "ps")

    def dram(self, name, shape, dtype, kind="Internal"):
        return Tile(name, self.nc.dram_tensor(name, list(shape), dtype, kind=kind), "dram")

    def _semof(self, key):
        if isinstance(key, str):
            return self.sem[key]
        return self.dsem[key]

    def _wait(self, eng, key, count):
        if self.seen[eng].get(key, 0) >= count:
            return
        self.E[eng].wait_ge(self._semof(key), count)
        self.seen[eng][key] = count
        self.n_wait += 1

    @staticmethod
    def _rw(reads, writes):
        rs, ws = [], []
        for t in reads:
            r = t.root
            (ws if r.space == "ps" else rs).append(r)
        for t in writes:
            ws.append(t.root)
        return rs, ws

    def _deps(self, eng, reads, writes, defer=False, lhs=None):
        reads, writes = self._rw(reads, writes)
        need = {}
        for t in reads:
            if t.last_w is not None:
                k, cnt = t.last_w
                need[k] = max(need.get(k, 0), cnt)
        for t in writes:
            if t.last_w is not None:
                k, cnt = t.last_w
                need[k] = max(need.get(k, 0), cnt)
            for k, cnt in t.readers.items():
                need[k] = max(need.get(k, 0), cnt)
        hard = set()
        if lhs is not None:
            for t in lhs:
                r = t.root
                if r.last_w is not None:
                    hard.add(r.last_w[0])
        todo = [(k, cnt) for k, cnt in need.items() if self.seen[eng].get(k, 0) < cnt]
        last = None
        if defer and todo:
            soft = [x for x in todo if x[0] not in hard]
            if soft:
                last = soft[-1]
                todo.remove(last)
        for k, cnt in todo:
            self._wait(eng, k, cnt)
        return last

    def _embed(self, eng, inst, last):
        if last is not None:
            k, cnt = last
            inst._wait_ge(self._semof(k), cnt)
            self.seen[eng][k] = cnt

    def _mark(self, key, count, reads, writes):
        reads, writes = self._rw(reads, writes)
        for t in reads:
            t.readers[key] = count
        for t in writes:
            t.last_w = (key, count)
            t.readers = {}

    def op(self, eng, fn, reads=(), writes=(), multi=False, lhs=None):
        defer = EMBED_WAITS and (eng != "pe" or lhs is not None) and not multi
        last = self._deps(eng, reads, writes, defer, lhs)
        inst = fn()
        self._embed(eng, inst, last)
        self.cnt[eng] += 1
        inst.then_inc(self.sem[eng], 1)
        self._mark(eng, self.cnt[eng], reads, writes)
        self.n_inst += 1
        return inst

    def group(self, eng, fns, reads=(), writes=(), lhs=None):
        defer = EMBED_WAITS and lhs is not None
        last = self._deps(eng, reads, writes, defer, lhs)
        inst = None
        for n, fn in enumerate(fns):
            inst = fn()
            if n == 0:
                self._embed(eng, inst, last)
            self.n_inst += 1
        self.cnt[eng] += 1
        inst.then_inc(self.sem[eng], 1)
        self._mark(eng, self.cnt[eng], reads, writes)
        return inst

    def dma(self, eng, out_ap, in_ap, reads=(), writes=(), **kw):
        i = self.dnext
        self.dnext = (self.dnext + 1) % N_DMA_SEMS
        if self.dcnt[i] > 0:
            self._wait(eng, i, self.dcnt[i])
        last = self._deps(eng, reads, writes, EMBED_WAITS)
        inst = self.E[eng].dma_start(out=out_ap, in_=in_ap, **kw)
        self._embed(eng, inst, last)
        self.dcnt[i] += 16
        inst.then_inc(self.dsem[i], 16)
        self._mark(i, self.dcnt[i], reads, writes)
        self.n_inst += 1
        return inst

    def finish(self, tiles=()):
        for i in range(N_DMA_SEMS):
            if self.dcnt[i] > 0:
                self._wait("sp", i, self.dcnt[i])
        for e in self.E:
            if e != "sp" and self.cnt[e] > 0:
                self._wait("sp", e, self.cnt[e])

    def barrier(self):
        for e in self.E:
            for i in range(N_DMA_SEMS):
                if self.dcnt[i] > 0:
                    self._wait(e, i, self.dcnt[i])
            for f in self.E:
                if f != e and self.cnt[f] > 0:
                    self._wait(e, f, self.cnt[f])


from contextlib import ExitStack


class Scope:
    uid = 0

    def __init__(self, ctx):
        self.ctx = ctx
        self.st = ExitStack()

    def __enter__(self):
        self.st.__enter__()
        return self

    def __exit__(self, *a):
        return self.st.__exit__(*a)

    def sb(self, name, shape, dtype):
        Scope.uid += 1
        name = "%s_u%d" % (name, Scope.uid)
        h = self.st.enter_context(self.ctx.nc.sbuf_tensor(name, list(shape), dtype))
        return Tile(name, h, "sb")

    def ps(self, name, shape, dtype=F32):
        Scope.uid += 1
        name = "%s_u%d" % (name, Scope.uid)
        h = self.st.enter_context(self.ctx.nc.psum_tensor(name, list(shape), dtype))
        return Tile(name, h, "ps")


def view(name, ap, root=None):
    return Tile(name, ap, "view", root)


T = 2048
D = 2048
KT = 16
NMEM = 256
EPS = 1e-6
C_ID, C_ONES, C_TRI, C_NEGI, C_STRICT, C_LOW, C_NEGS = 0, 128, 256, 384, 512, 640, 768
NCST = 896


def make_consts():
    c = np.zeros((128, NCST), np.float32)
    j = np.arange(128)[:, None]
    i = np.arange(128)[None, :]
    c[:, C_ID:C_ID + 128] = (i == j)
    c[:, C_ONES:C_ONES + 128] = 1.0
    c[:, C_TRI:C_TRI + 128] = (j <= i)
    c[:, C_NEGI:C_NEGI + 128] = np.where(i >= j, 0.0, -30000.0)
    c[:, C_STRICT:C_STRICT + 128] = (i > j)
    c[:, C_LOW:C_LOW + 128] = (j > i)
    c[:, C_NEGS:C_NEGS + 128] = np.where(i > j, 0.0, -30000.0)
    return c


class Prog:
    def __init__(self, nc, dbg=None):
        self.nc = nc
        self.ctx = Ctx(nc)
        self.dbg = dbg or {}
        c = self.ctx
        self.x_in = c.dram("x", [T, D], F32, "ExternalInput")
        self.mem_in = c.dram("mem", [NMEM, D], F32, "ExternalInput")
        self.gB_in = c.dram("gB", [128, 3 * D], F32, "ExternalInput")
        self.cst_in = c.dram("cst", [128, NCST], F32, "ExternalInput")
        self.cst = c.sb("cst_sb", [128, NCST], F32)
        self.cstb = c.sb("cst_bf", [128, NCST], BF16)
        self.memT = c.sb("memT", [128, KT, NMEM], BF16)
        self.pv_in = c.dram("pv", [128, NPV], F32, "ExternalInput")
        self.pv = c.sb("pv_sb", [128, NPV], F32)
        self.onec = c.sb("onec", [128, 1], F32)
        self.epsc = c.sb("epsc", [128, 1], F32)
        self.wst = [c.sb("wst%d" % i, [128, 2048], F32) for i in range(2)]
        self.wbf = [c.sb("wbf%d" % i, [128, 2048], BF16) for i in range(2)]
        self.wi = 0

    def load_consts(self):
        c = self.ctx
        nc = self.nc
        c.dma("sp", self.cst[:, :], self.cst_in[:, :], writes=[self.cst])
        c.op("dve", lambda: nc.vector.tensor_copy(self.cstb[:, :], self.cst[:, :]), reads=[self.cst], writes=[self.cstb])
        c.op("dve", lambda: nc.vector.memset(self.epsc[:, :], EPS), writes=[self.epsc])
        c.op("dve", lambda: nc.vector.memset(self.onec[:, :], 1.0), writes=[self.onec])
        c.dma("sp", self.pv[:, :], self.pv_in[:, :], writes=[self.pv])
        self.gsc = c.sb("gsc", [128, 8], F32)
        c.op("dve", lambda: nc.vector.memset(self.gsc[:, 5:6], 128.0 ** -0.5), writes=[self.gsc])
        c.op("dve", lambda: nc.vector.memset(self.gsc[:, 6:7], 1.0), writes=[self.gsc])
        c.op("dve", lambda: nc.vector.tensor_scalar(out=self.gsc[:, 0:4], in0=self.pv[:, PV_XQ:PV_XQ + 4], scalar1=1.0 / 16.0, scalar2=None, op0=ALU.mult),
             reads=[self.pv], writes=[self.gsc])
        c.op("dve", lambda: nc.vector.tensor_scalar(out=self.gsc[:, 4:5], in0=self.pv[:, PV_SBQ:PV_SBQ + 1], scalar1=128.0 ** -0.5, scalar2=None, op0=ALU.mult),
             reads=[self.pv], writes=[self.gsc])

    def ident_b(self):
        return self.cstb[:, C_ID:C_ID + 128]

    def w_fetch(self, src_ap):
        c = self.ctx
        s = self.wi
        self.wi ^= 1
        c.dma("sp", self.wst[s][:, :], src_ap, writes=[self.wst[s]])
        return s

    def w_cast(self, s, dst=None, dst_ap=None):
        c = self.ctx
        nc = self.nc
        if dst is None:
            dst = self.wbf[s]
            dst_ap = dst[:, :]
        c.op("pool", lambda: nc.gpsimd.tensor_copy(dst_ap, self.wst[s][:, :]),
             reads=[self.wst[s]], writes=[dst])
        return dst

    def norm_to_T(self, sc, src, nrows, g_off, dstT, add=None, xout=None):
        c = self.ctx
        nc = self.nc
        gB = sc.sb("gB", [128, D], F32)
        c.dma("sp", gB[:, :], self.gB_in[:, g_off:g_off + D], writes=[gB])
        xb = [sc.sb("xb%d" % i, [128, D], F32) for i in range(2)]
        ab = [sc.sb("ab%d" % i, [128, D], F32) for i in range(2)] if add is not None else None
        hb = [sc.sb("hb%d" % i, [128, D], BF16) for i in range(2)]
        junk = sc.sb("junk", [128, D], BF16)
        ss = [sc.sb("ss%d" % i, [128, 1], F32) for i in range(2)]
        rs = [sc.sb("rs%d" % i, [128, 1], F32) for i in range(2)]
        pt = [sc.ps("ptr%d" % i, [128, 1024], BF16) for i in range(2)]
        ident = self.ident_b()
        nt = nrows // 128
        pi = 0
        for tt in range(nt):
            b = tt % 2
            x = xb[b]
            r0 = tt * 128
            if isinstance(src, tuple):
                half = D // 2
                for hf in range(2):
                    c.dma("sp", x[:, hf * half:(hf + 1) * half], src[hf][r0:r0 + 128, :], writes=[x])
            else:
                c.dma("sp", x[:, :], src[r0:r0 + 128, :], writes=[x])
            if add is not None:
                a = ab[b]
                c.dma("sp", a[:, :], add[r0:r0 + 128, :], writes=[a])
                c.op("pool", lambda: nc.gpsimd.tensor_tensor(x[:, :], x[:, :], a[:, :], ALU.add), reads=[x, a], writes=[x])
                if xout is not None:
                    c.dma("sp", xout[r0:r0 + 128, :], x[:, :], reads=[x])
            c.op("act", lambda: nc.scalar.activation(out=junk[:, :], in_=x[:, :], func=AF.Square, accum_out=ss[b][:, 0:1]),
                 reads=[x], writes=[junk, ss[b]], multi=True)
            c.op("act", lambda: nc.scalar.activation(out=rs[b][:, :], in_=ss[b][:, :], func=AF.Sqrt, bias=self.epsc[:, 0:1], scale=1.0 / D),
                 reads=[ss[b], self.epsc], writes=[rs[b]])
            c.op("dve", lambda: nc.vector.reciprocal(rs[b][:, :], rs[b][:, :]), reads=[rs[b]], writes=[rs[b]])
            h = hb[b]
            c.op("dve", lambda: nc.vector.scalar_tensor_tensor(out=h[:, :], in0=x[:, :], scalar=rs[b][:, 0:1], in1=gB[:, :], op0=ALU.mult, op1=ALU.mult),
                 reads=[x, rs[b], gB], writes=[h])
            for g8 in range(2):
                p = pt[pi]
                pi ^= 1
                c.group("pe", [(lambda j=j: nc.tensor.transpose(p[:, j * 128:(j + 1) * 128], h[:, (g8 * 8 + j) * 128:(g8 * 8 + j + 1) * 128], ident))
                               for j in range(8)], reads=[h, self.cstb], writes=[p])
                c.op("act", lambda: nc.scalar.copy(out=dstT[:, g8 * 8:(g8 + 1) * 8, r0:r0 + 128],
                                                   in_=p[:, :].rearrange("p (a b) -> p a b", a=8)),
                     reads=[p], writes=[dstT])

    def proj_block(self, wtile, rhsT, ntok, psb, evac):
        c = self.ctx
        nc = self.nc
        nch = (ntok + 511) // 512
        for tc in range(nch):
            n = min(512, ntok - tc * 512)
            p = psb[self.pj]
            self.pj ^= 1
            c.group("pe", [(lambda kt=kt: nc.tensor.matmul(p[:, 0:n], wtile[:, kt * 128:(kt + 1) * 128], rhsT[:, kt, tc * 512:tc * 512 + n],
                                                          start=(kt == 0), stop=(kt == KT - 1))) for kt in range(KT)],
                    reads=[wtile, rhsT], writes=[p])
            evac(tc, p, n)


NPV = 256
GS_XQ, GS_SBQ = 0, 4
PV_XQ, PV_XK, PV_DNO, PV_SBQ, PV_SBK, PV_CONV, PV_ALOG, PV_DTB = 0, 4, 8, 9, 10, 16, 208, 232
NH = 24
NXA = 4
NYB = 32


def make_pv(inp):
    pv = np.zeros((128, NPV), np.float32)
    for li in range(2):
        for j in range(2):
            pv[:, PV_XQ + li * 2 + j] = inp["xa_q_norm_g"][li][j * 128:(j + 1) * 128]
            pv[:, PV_XK + li * 2 + j] = inp["xa_k_norm_g"][li][j * 128:(j + 1) * 128]
    pv[:, PV_DNO] = inp["dn_out_norm_g"][0]
    pv[:, PV_SBQ] = inp["sb_q_norm_g"][0]
    pv[:, PV_SBK] = inp["sb_k_norm_g"][0]
    cw = inp["dn_conv_w"][0]
    pv[:, PV_CONV:PV_CONV + 192] = cw.reshape(4, 48, 128).transpose(2, 1, 0).reshape(128, 192)
    pv[:, PV_ALOG:PV_ALOG + 24] = inp["dn_a_log"][0][None, :]
    pv[:, PV_DTB:PV_DTB + 24] = inp["dn_dt_bias"][0][None, :]
    return pv


def lhsT_blocks(W, cols):
    out = np.empty((len(cols), 128, 2048), np.float32)
    for i, c0 in enumerate(cols):
        out[i] = W[:, c0:c0 + 128].reshape(16, 128, 128).transpose(1, 0, 2).reshape(128, 2048)
    return out


def rhs_pieces(W, c0, n):
    kpp = 2048 // n
    A = W[:, c0:c0 + n].reshape(16, 128, n).transpose(1, 0, 2)
    A = A.reshape(128, 16 // kpp, kpp * n).transpose(1, 0, 2)
    return np.ascontiguousarray(A)


class Layers(Prog):
    def __init__(self, nc, dbg=None, nh=NH, nxa=NXA):
        super().__init__(nc, dbg)
        c = self.ctx
        self.nh = nh
        self.nxa = nxa
        self.yT_d = c.dram("yT_d", [nh + 2 * nxa, 128, T], BF16)
        self.x1_d = c.dram("x1_d", [T, D], F32)
        self.vtok_d = c.dram("vtok_d", [16, 128, nh * 128], BF16)
        self.pj = 0

    def fnorm(self, sc_bufs, raws, gcols, outs, ntok, scale_dim):
        c = self.ctx
        nc = self.nc
        sq, ssum, rstd = sc_bufs
        onesf = self.cst[:, C_ONES:C_ONES + 128]
        nb = len(raws)
        for tc in range((ntok + 511) // 512):
            n = min(512, ntok - tc * 512)
            sl = slice(tc * 512, tc * 512 + n)
            for j in range(nb):
                c.op("act", lambda: nc.scalar.activation(out=sq[j][:, 0:n], in_=raws[j][:, sl], func=AF.Square),
                     reads=[raws[j]], writes=[sq[j]])
            c.group("pe", [(lambda j=j: nc.tensor.matmul(ssum[:, 0:n], onesf, sq[j][:, 0:n], start=(j == 0), stop=(j == nb - 1)))
                           for j in range(nb)], reads=[self.cst] + [sq[j] for j in range(nb)], writes=[ssum])
            c.op("act", lambda: nc.scalar.activation(out=rstd[:, 0:n], in_=ssum[:, 0:n], func=AF.Sqrt, bias=self.epsc[:, 0:1], scale=1.0 / scale_dim),
                 reads=[ssum, self.epsc], writes=[rstd])
            c.op("dve", lambda: nc.vector.reciprocal(rstd[:, 0:n], rstd[:, 0:n]), reads=[rstd], writes=[rstd])
            for j in range(nb):
                c.op("dve", lambda: nc.vector.scalar_tensor_tensor(out=outs[j][:, sl], in0=raws[j][:, sl], scalar=gcols[j], in1=rstd[:, 0:n],
                                                                   op0=ALU.mult, op1=ALU.mult),
                     reads=[raws[j], rstd, self.gsc], writes=[outs[j]])

    def wblocks(self, w_d, idxs):
        slots = [self.w_fetch(w_d[idxs[0], :, :])]
        for n, i in enumerate(idxs):
            if n + 1 < len(idxs):
                slots.append(self.w_fetch(w_d[idxs[n + 1], :, :]))
            yield i, self.w_cast(slots[n])

    def xa_prep(self, sc, li, wk_d, wv_d, psb):
        c = self.ctx
        nc = self.nc
        kxT = sc.sb("kxT", [128, 2 * self.nxa, NMEM], BF16)
        vx = sc.sb("vx", [128, 2, 256 * self.nxa], BF16)
        with Scope(c) as s2:
            kraw = [s2.sb("kraw%d" % j, [128, NMEM], F32) for j in range(2)]
            sq = [s2.sb("ksq%d" % j, [128, 512], F32) for j in range(2)]
            rstd = s2.sb("krstd", [128, 512], F32)
            ssum = s2.ps("kssum", [128, 512], F32)
            wvb = s2.sb("wvb", [128, 8192], BF16)
            it = self.wblocks(wk_d, list(range(2 * self.nxa)))
            for a in range(self.nxa):
                for j in range(2):
                    _, wt = next(it)

                    def evac(tc, p, n, j=j):
                        c.op("act", lambda: nc.scalar.copy(out=kraw[j][:, 0:n], in_=p[:, 0:n]), reads=[p], writes=[kraw[j]])
                    self.proj_block(wt, self.memT, NMEM, psb, evac)
                outs = [view("kx%d" % (a * 2 + j), kxT[:, a * 2 + j, :]) for j in range(2)]
                self.fnorm((sq, ssum, rstd), kraw, [self.pv[:, PV_XK + li * 2 + j:PV_XK + li * 2 + j + 1] for j in range(2)], outs, NMEM, 256.0)
            c.barrier()
            for g in range(self.nxa // 2):
                for pc in range(4):
                    s = self.w_fetch(wv_d[g * 4 + pc, :, :])
                    self.w_cast(s, wvb, wvb[:, pc * 2048:(pc + 1) * 2048])
                for mt in range(2):
                    p = psb[self.pj]
                    self.pj ^= 1
                    c.group("pe", [(lambda kt=kt: nc.tensor.matmul(p[:, 0:512], self.memT[:, kt, mt * 128:(mt + 1) * 128], wvb[:, kt * 512:(kt + 1) * 512],
                                                                  start=(kt == 0), stop=(kt == KT - 1))) for kt in range(KT)],
                            reads=[self.memT, wvb], writes=[p])
                    c.op("act", lambda: nc.scalar.copy(out=vx[:, mt, g * 512:(g + 1) * 512], in_=p[:, 0:512]), reads=[p], writes=[vx])
            c.barrier()
        return kxT, vx

    def xa_heads(self, li, hT, w_d, xq_idx, z_idx, kxT, vx, psb):
        c = self.ctx
        nc = self.nc
        with Scope(c) as sc:
            qraw = [sc.sb("xqraw%d" % j, [128, T], F32) for j in range(2)]
            qn = [sc.sb("xqn%d" % j, [128, T], BF16) for j in range(2)]
            sz = [sc.sb("xsz%d" % j, [128, T], BF16) for j in range(2)]
            yb = [sc.sb("xyb%d" % j, [128, T], BF16) for j in range(2)]
            sq = [sc.sb("xsq%d" % j, [128, 512], F32) for j in range(2)]
            rstd = sc.sb("xrstd", [128, 512], F32)
            pe_ = [sc.sb("xp%d" % j, [128, 512], BF16) for j in range(2)]
            rden = sc.sb("xrden", [128, 512], F32)
            t1 = sc.sb("xt1", [128, 512], F32)
            ssum = sc.ps("xssum", [128, 512], F32)
            sps = [sc.ps("xs%d" % j, [128, 512], F32) for j in range(2)]
            den = sc.ps("xden", [128, 512], F32)
            ops = sc.ps("xo", [128, 512], F32)
            onesb = self.cstb[:, C_ONES:C_ONES + 128]
            for a in range(self.nxa):
                idxs = [xq_idx[a * 2], xq_idx[a * 2 + 1], z_idx[a * 2], z_idx[a * 2 + 1]]
                it = self.wblocks(w_d, idxs)
                for j in range(2):
                    _, wt = next(it)

                    def evac(tc, p, n, j=j):
                        c.op("act", lambda: nc.scalar.copy(out=qraw[j][:, tc * 512:tc * 512 + n], in_=p[:, 0:n]), reads=[p], writes=[qraw[j]])
                    self.proj_block(wt, hT, T, psb, evac)
                for j in range(2):
                    _, wt = next(it)

                    def evac(tc, p, n, j=j):
                        c.op("act", lambda: nc.scalar.activation(out=sz[j][:, tc * 512:tc * 512 + n], in_=p[:, 0:n], func=AF.Silu), reads=[p], writes=[sz[j]])
                    self.proj_block(wt, hT, T, psb, evac)
                self.fnorm((sq, ssum, rstd), qraw, [self.gsc[:, GS_XQ + li * 2 + j:GS_XQ + li * 2 + j + 1] for j in range(2)], qn, T, 256.0)
                for tc in range(4):
                    sl = slice(tc * 512, (tc + 1) * 512)
                    for mt in range(2):
                        c.group("pe", [(lambda j=j: nc.tensor.matmul(sps[mt][:, :], kxT[:, a * 2 + j, mt * 128:(mt + 1) * 128], qn[j][:, sl],
                                                                    start=(j == 0), stop=(j == 1))) for j in range(2)],
                                reads=[kxT, qn[0], qn[1]], writes=[sps[mt]])
                        c.op("act", lambda: nc.scalar.activation(out=pe_[mt][:, :], in_=sps[mt][:, :], func=AF.Exp), reads=[sps[mt]], writes=[pe_[mt]])
                    c.group("pe", [(lambda mt=mt: nc.tensor.matmul(den[:, :], onesb, pe_[mt][:, :], start=(mt == 0), stop=(mt == 1))) for mt in range(2)],
                            reads=[self.cstb, pe_[0], pe_[1]], writes=[den])
                    c.op("dve", lambda: nc.vector.reciprocal(rden[:, :], den[:, :]), reads=[den], writes=[rden])
                    for eb in range(2):
                        c.group("pe", [(lambda mt=mt: nc.tensor.matmul(ops[:, :], vx[:, mt, a * 256 + eb * 128:a * 256 + (eb + 1) * 128], pe_[mt][:, :],
                                                                      start=(mt == 0), stop=(mt == 1))) for mt in range(2)],
                                reads=[vx, pe_[0], pe_[1]], writes=[ops])
                        c.op("dve", lambda: nc.vector.tensor_tensor(t1[:, :], ops[:, :], rden[:, :], ALU.mult), reads=[ops, rden], writes=[t1])
                        c.op("pool", lambda: nc.gpsimd.tensor_tensor(yb[eb][:, sl], t1[:, :], sz[eb][:, sl], ALU.mult), reads=[t1, sz[eb]], writes=[yb[eb]])
                for eb in range(2):
                    c.dma("sp", self.yT_d[self.nh + a * 2 + eb, :, :], yb[eb][:, :], reads=[yb[eb]])
            c.barrier()

    def out_proj(self, wo_d, x_src, x_dst):
        c = self.ctx
        nc = self.nc
        with Scope(c) as sc:
            wo = sc.sb("wo", [128, NYB, D], BF16)
            yt = [sc.sb("yt%d" % i, [128, NYB, 128], BF16) for i in range(2)]
            xr = [sc.sb("xr%d" % i, [128, D], F32) for i in range(2)]
            psb = [sc.ps("op%d" % i, [128, 512], F32) for i in range(4)]
            for et in range(NYB):
                s = self.w_fetch(wo_d[et * 128:(et + 1) * 128, :])
                eng = ("pool", "act", "dve")[et % 3]
                dst_ap = wo[:, et, :]
                src_ap = self.wst[s][:, :]
                if eng == "pool":
                    c.op("pool", lambda: nc.gpsimd.tensor_copy(dst_ap, src_ap), reads=[self.wst[s]], writes=[wo])
                elif eng == "act":
                    c.op("act", lambda: nc.scalar.copy(out=dst_ap, in_=src_ap), reads=[self.wst[s]], writes=[wo])
                else:
                    c.op("dve", lambda: nc.vector.tensor_copy(dst_ap, src_ap), reads=[self.wst[s]], writes=[wo])
            pi = 0

            def loads(tt):
                c.dma("sp", yt[tt % 2][:, :, :], self.yT_d[:, :, tt * 128:(tt + 1) * 128].rearrange("e p t -> p e t"), writes=[yt[tt % 2]])
                c.dma("sp", xr[tt % 2][:, :], x_src[tt * 128:(tt + 1) * 128, :], writes=[xr[tt % 2]])
            loads(0)
            for tt in range(T // 128):
                y = yt[tt % 2]
                x = xr[tt % 2]
                if tt + 1 < T // 128 and tt >= 1:
                    loads(tt + 1)
                for ch in range(4):
                    p = psb[pi]
                    pi = (pi + 1) % 4
                    c.group("pe", [(lambda et=et: nc.tensor.matmul(p[:, :], y[:, et, :], wo[:, et, ch * 512:(ch + 1) * 512],
                                                                  start=(et == 0), stop=(et == NYB - 1))) for et in range(NYB)],
                            reads=[y, wo], writes=[p])
                    c.op("dve", lambda: nc.vector.tensor_tensor(x[:, ch * 512:(ch + 1) * 512], p[:, :], x[:, ch * 512:(ch + 1) * 512], ALU.add),
                         reads=[p, x], writes=[x])
                c.dma("sp", x_dst[tt * 128:(tt + 1) * 128, :], x[:, :], reads=[x])
                if tt == 0 and T // 128 > 1:
                    loads(1)
            c.barrier()

    def sb_heads(self, hT, w_d, q_idx, k_idx, z_idx, wv_d, psb, heads=None):
        c = self.ctx
        nc = self.nc
        heads = list(range(NH)) if heads is None else heads
        with Scope(c) as sc:
            qraw = sc.sb("qraw", [128, T], F32)
            kraw = sc.sb("kraw", [128, T], F32)
            qT = sc.sb("qT", [128, T], BF16)
            kT = sc.sb("kT", [128, T], BF16)
            sz = sc.sb("sz", [128, T], BF16)
            yb = [sc.sb("yb%d" % i, [128, T], BF16) for i in range(2)]
            vtok = sc.sb("vtok", [128, 16, 512], BF16)
            wvb = sc.sb("wvb", [128, 8192], BF16)
            sq = [sc.sb("sq0", [128, 512], F32)]
            rstd = sc.sb("rstd", [128, 512], F32)
            Eb = [sc.sb("E%d" % i, [128, 512], F32) for i in range(2)]
            spb = [sc.sb("sp%d" % i, [128, 512], F32) for i in range(2)]
            Lhi = [sc.sb("Lhi%d" % i, [128, 512], BF16) for i in range(2)]
            Llo = [sc.sb("Llo%d" % i, [128, 512], BF16) for i in range(2)]
            zms = [sc.sb("zms%d" % i, [128, 512], F32) for i in range(2)]
            wb = [sc.sb("wb%d" % i, [128, 512], BF16) for i in range(2)]
            ssum = sc.ps("ssum", [128, 512], F32)
            zps = [sc.ps("zps%d" % i, [128, 512], F32) for i in range(2)]
            Pps = sc.ps("Pps", [128, 512], F32)
            Ops = sc.ps("Ops", [128, 512], F32)
            Mlow = self.cstb[:, C_LOW:C_LOW + 128]
            Mtri = self.cstb[:, C_TRI:C_TRI + 128]
            strict = self.cst[:, C_STRICT:C_STRICT + 128]
            negs = self.cst[:, C_NEGS:C_NEGS + 128]
            if len(heads) < NH:
                c.op("dve", lambda: nc.vector.memset(yb[0][:, :], 0.0), writes=[yb[0]])
                for h in range(NH):
                    if h not in heads:
                        c.dma("sp", self.yT_d[h, :, :], yb[0][:, :], reads=[yb[0]])
            for hi, h in enumerate(heads):
                if hi % 4 == 0:
                    g = h // 4
                    for pc in range(4):
                        s = self.w_fetch(wv_d[g * 4 + pc, :, :])
                        self.w_cast(s, wvb, wvb[:, pc * 2048:(pc + 1) * 2048])
                    for tt in range(16):
                        p = psb[self.pj]
                        self.pj ^= 1
                        c.group("pe", [(lambda kt=kt: nc.tensor.matmul(p[:, 0:512], hT[:, kt, tt * 128:(tt + 1) * 128], wvb[:, kt * 512:(kt + 1) * 512],
                                                                      start=(kt == 0), stop=(kt == KT - 1))) for kt in range(KT)],
                                reads=[hT, wvb], writes=[p])
                        c.op("act", lambda: nc.scalar.copy(out=vtok[:, tt, :], in_=p[:, 0:512]), reads=[p], writes=[vtok])
                hh = h % 4
                it = self.wblocks(w_d, [z_idx[h], q_idx[h], k_idx[h]])
                _, wt = next(it)

                def evz(tc, p, n):
                    c.op("act", lambda: nc.scalar.activation(out=sz[:, tc * 512:tc * 512 + n], in_=p[:, 0:n], func=AF.Silu), reads=[p], writes=[sz])
                self.proj_block(wt, hT, T, psb, evz)
                _, wt = next(it)

                def evq(tc, p, n):
                    c.op("act", lambda: nc.scalar.copy(out=qraw[:, tc * 512:tc * 512 + n], in_=p[:, 0:n]), reads=[p], writes=[qraw])
                self.proj_block(wt, hT, T, psb, evq)
                _, wt = next(it)

                def evk(tc, p, n):
                    c.op("act", lambda: nc.scalar.copy(out=kraw[:, tc * 512:tc * 512 + n], in_=p[:, 0:n]), reads=[p], writes=[kraw])
                self.proj_block(wt, hT, T, psb, evk)
                self.fnorm((sq, ssum, rstd), [qraw], [self.gsc[:, GS_SBQ:GS_SBQ + 1]], [qT], T, 128.0)
                self.fnorm((sq, ssum, rstd), [kraw], [self.pv[:, PV_SBK:PV_SBK + 1]], [kT], T, 128.0)
                y = yb[hi % 2]
                blocks = [(qc, kb) for qc in range(4) for kb in range(4 * qc + 3, -1, -1)]

                def stageA(i):
                    qc, kb = blocks[i]
                    b = i % 2
                    lo = max(0, kb * 128 - qc * 512)
                    diag = kb >= 4 * qc
                    z = zps[b]
                    c.op("pe", lambda: nc.tensor.matmul(z[:, lo:512], kT[:, kb * 128:(kb + 1) * 128], qT[:, qc * 512 + lo:(qc + 1) * 512], start=True, stop=True),
                         reads=[kT, qT], writes=[z])
                    c.op("act", lambda: nc.scalar.activation(out=Eb[b][:, lo:512], in_=z[:, lo:512], func=AF.Exp), reads=[z], writes=[Eb[b]])
                    c.op("act", lambda: nc.scalar.activation(out=spb[b][:, lo:512], in_=Eb[b][:, lo:512], func=AF.Ln, bias=self.onec[:, 0:1], scale=1.0),
                         reads=[Eb[b], self.onec], writes=[spb[b]])
                    if diag:
                        c.op("pool", lambda: nc.gpsimd.tensor_tensor(spb[b][:, lo:lo + 128], spb[b][:, lo:lo + 128], strict, ALU.mult),
                             reads=[spb[b], self.cst], writes=[spb[b]])
                    c.op("pool", lambda: nc.gpsimd.tensor_scalar(out=Lhi[b][:, lo:512], in0=spb[b][:, lo:512], scalar1=-1.0, scalar2=None, op0=ALU.mult),
                         reads=[spb[b]], writes=[Lhi[b]])
                    c.op("dve", lambda: nc.vector.scalar_tensor_tensor(out=Llo[b][:, lo:512], in0=spb[b][:, lo:512], scalar=-1.0, in1=Lhi[b][:, lo:512],
                                                                       op0=ALU.mult, op1=ALU.subtract),
                         reads=[spb[b], Lhi[b]], writes=[Llo[b]])
                    c.op("dve", lambda: nc.vector.tensor_tensor(zms[b][:, lo:512], z[:, lo:512], spb[b][:, lo:512], ALU.subtract),
                         reads=[z, spb[b]], writes=[zms[b]])
                    if diag:
                        c.op("pool", lambda: nc.gpsimd.tensor_tensor(zms[b][:, lo:lo + 128], zms[b][:, lo:lo + 128], negs, ALU.add),
                             reads=[zms[b], self.cst], writes=[zms[b]])

                def stageB(i):
                    qc, kb = blocks[i]
                    b = i % 2
                    lo = max(0, kb * 128 - qc * 512)
                    first = kb == 4 * qc + 3
                    last = kb == 0
                    c.group("pe", [lambda: nc.tensor.matmul(Pps[:, lo:512], Mlow, Lhi[b][:, lo:512], start=first, stop=False),
                                   lambda: nc.tensor.matmul(Pps[:, lo:512], Mlow, Llo[b][:, lo:512], start=False, stop=last)],
                            reads=[self.cstb, Lhi[b], Llo[b]], writes=[Pps])
                    c.op("dve", lambda: nc.vector.tensor_tensor(zms[b][:, lo:512], Pps[:, lo:512], zms[b][:, lo:512], ALU.add),
                         reads=[Pps, zms[b]], writes=[zms[b]])
                    c.op("act", lambda: nc.scalar.activation(out=wb[b][:, lo:512], in_=zms[b][:, lo:512], func=AF.Exp), reads=[zms[b]], writes=[wb[b]])
                    if not last:
                        c.group("pe", [lambda: nc.tensor.matmul(Pps[:, lo:512], Mtri, Lhi[b][:, lo:512], start=False, stop=False),
                                       lambda: nc.tensor.matmul(Pps[:, lo:512], Mtri, Llo[b][:, lo:512], start=False, stop=False)],
                                reads=[self.cstb, Lhi[b], Llo[b]], writes=[Pps])
                    c.op("pe", lambda: nc.tensor.matmul(Ops[:, lo:512], vtok[:, kb, hh * 128:(hh + 1) * 128], wb[b][:, lo:512], start=first, stop=last),
                         reads=[vtok, wb[b]], writes=[Ops])
                    if last:
                        c.op("dve", lambda: nc.vector.tensor_tensor(y[:, qc * 512:(qc + 1) * 512], Ops[:, :], sz[:, qc * 512:(qc + 1) * 512], ALU.mult),
                             reads=[Ops, sz], writes=[y])

                for i in range(len(blocks) + 1):
                    if i < len(blocks):
                        stageA(i)
                    if i >= 1:
                        stageB(i - 1)
                c.dma("sp", self.yT_d[h, :, :], y[:, :], reads=[y])
            c.barrier()

    def layer(self, li, kind, W, x_src, x_dst, add=None, xout=None, first=False):
        c = self.ctx
        with Scope(c) as sA:
            hT = sA.sb("hT", [128, KT, T], BF16)
            with Scope(c) as s1:
                if first:
                    self.norm_to_T(s1, self.mem_in, NMEM, 2 * D, self.memT)
            c.barrier()
            with Scope(c) as s1:
                self.norm_to_T(s1, x_src, T, li * D, hT, add=add, xout=xout)
                c.barrier()
            with Scope(c) as s2:
                psb = [s2.ps("pj%d" % i, [128, 512], F32) for i in range(2)]
                with Scope(c) as sx:
                    kxT, vx = self.xa_prep(sx, li, W["wk"], W["wv"], psb)
                    self.xa_heads2(li, hT, W["w"], W["xq_idx"], W["zx_idx"], kxT, vx, psb)
                    c.barrier()
                if kind == "sb":
                    if self.dbg.get("v1"):
                        self.sb_heads(hT, W["w"], W["q_idx"], W["k_idx"], W["z_idx"], W["wvm"], psb, heads=self.dbg.get("heads"))
                    else:
                        self.sb_heads2(hT, W, psb)
                elif self.dbg.get("v1"):
                    self.dn_heads(hT, W, psb)
                elif self.dbg.get("v2"):
                    self.dn_heads2(hT, W, psb)
                else:
                    self.dn_heads3(hT, W, psb)
                c.barrier()
        xs = xout if xout is not None else x_src
        self.out_proj(W["wo"], xs, x_dst)


def sb_weights_host(inp):
    W = inp["sb_w_in"][0]
    cols = [h * 128 for h in range(24)] + [3072 + h * 128 for h in range(24)] + [9216 + j * 128 for j in range(8)] + [10240 + j * 128 for j in range(32)]
    w = lhsT_blocks(W, cols)
    wvm = np.concatenate([rhs_pieces(W, 6144 + g * 512, 512) for g in range(6)], 0)
    return {"w1": w, "wvm1": wvm}


def kv_weights_host(inp, li):
    Wkv = inp["mem_w_kv"][li]
    wk = lhsT_blocks(Wkv, [j * 128 for j in range(8)])
    wv = np.concatenate([rhs_pieces(Wkv, 1024 + g * 512, 512) for g in range(2)], 0)
    return {"wk%d" % li: wk, "wv%d" % li: wv}


def sb_decl(P, li=1):
    c = P.ctx
    return {
        "w": c.dram("w%d" % li, [88, 128, 2048], F32, "ExternalInput"),
        "wvm": c.dram("wvm%d" % li, [24, 128, 2048], F32, "ExternalInput"),
        "wk": c.dram("wk%d" % li, [8, 128, 2048], F32, "ExternalInput"),
        "wv": c.dram("wv%d" % li, [8, 128, 2048], F32, "ExternalInput"),
        "wo": c.dram("wo%d" % li, [4096, 2048], F32, "ExternalInput"),
        "q_idx": list(range(24)), "k_idx": list(range(24, 48)), "xq_idx": list(range(48, 56)),
        "z_idx": list(range(56, 80)), "zx_idx": list(range(80, 88)),
    }


def dn_weights_host(inp):
    W = inp["dn_w_in"][0]
    cols = [h * 128 for h in range(12)] + [1536 + h * 128 for h in range(12)] + [3072 + h * 128 for h in range(24)] \
        + [6192 + j * 128 for j in range(8)] + [7216 + j * 128 for j in range(32)]
    w = lhsT_blocks(W, cols)
    wab = np.ascontiguousarray(W[:, 6144:6192].reshape(16, 128, 48).transpose(1, 0, 2).reshape(128, 768))
    return {"w0": w, "wab0": wab}


def dn_decl(P, li=0):
    c = P.ctx
    return {
        "w": c.dram("w%d" % li, [88, 128, 2048], F32, "ExternalInput"),
        "wab": c.dram("wab%d" % li, [128, 768], F32, "ExternalInput"),
        "wk": c.dram("wk%d" % li, [8, 128, 2048], F32, "ExternalInput"),
        "wv": c.dram("wv%d" % li, [8, 128, 2048], F32, "ExternalInput"),
        "wo": c.dram("wo%d" % li, [4096, 2048], F32, "ExternalInput"),
        "q_idx": list(range(12)), "k_idx": list(range(12, 24)), "v_idx": list(range(24, 48)), "xq_idx": list(range(48, 56)),
        "z_idx": list(range(56, 80)), "zx_idx": list(range(80, 88)),
    }


def dn_heads(self, hT, W, psb):
    c = self.ctx
    nc = self.nc
    pairs = self.dbg.get("pairs")
    pairs = list(range(12)) if pairs is None else pairs
    NC_ = T // 128
    with Scope(c) as sc:
        onesf = self.cst[:, C_ONES:C_ONES + 128]
        trif = self.cst[:, C_TRI:C_TRI + 128]
        negi = self.cst[:, C_NEGI:C_NEGI + 128]
        strictf = self.cst[:, C_STRICT:C_STRICT + 128]
        identf = self.cst[:, C_ID:C_ID + 128]
        identb = self.cstb[:, C_ID:C_ID + 128]
        g_all = sc.sb("g_all", [128, NC_, 24], F32)
        bt_all = sc.sb("bt_all", [128, NC_, 24], F32)
        nbt_all = sc.sb("nbt_all", [128, NC_, 24], F32)
        gc_all = sc.sb("gc_all", [128, NC_, 24], F32)
        egc_all = sc.sb("egc_all", [128, NC_, 24], F32)
        ekd_all = sc.sb("ekd_all", [128, NC_, 24], F32)
        ecd_all = sc.sb("ecd_all", [128, NC_, 24], F32)
        negA = sc.sb("negA", [128, 24], F32)
        wabf = sc.sb("wabf", [128, 768], F32)
        wabb = sc.sb("wabb", [128, 768], BF16)
        gtmp = sc.sb("gtmp", [128, 24], F32)
        bankA = sc.ps("bankA", [128, 512], F32)
        bankB = sc.ps("bankB", [128, 512], F32)
        bankC = sc.ps("bankC", [128, 512], F32)
        bankD = sc.ps("bankD", [128, 512], F32)
        bankE = sc.ps("bankE", [128, 1024], BF16)
        ssum = sc.ps("ssum", [128, 512], F32)
        c.dma("sp", wabf[:, :], W["wab"][:, :], writes=[wabf])
        c.op("dve", lambda: nc.vector.tensor_copy(wabb[:, :], wabf[:, :]), reads=[wabf], writes=[wabb])
        c.op("act", lambda: nc.scalar.activation(out=negA[:, :], in_=self.pv[:, PV_ALOG:PV_ALOG + 24], func=AF.Exp), reads=[self.pv], writes=[negA])
        c.op("dve", lambda: nc.vector.tensor_scalar(out=negA[:, :], in0=negA[:, :], scalar1=-1.0, scalar2=None, op0=ALU.mult), reads=[negA], writes=[negA])
        abp = view("abp", bankA[:, 0:48], bankA)
        gcp = view("gcp", bankB[:, 0:24], bankB)
        glp = view("glp", bankC[:, 0:24], bankC)
        for tt in range(NC_):
            c.group("pe", [(lambda kt=kt: nc.tensor.matmul(abp[:, :], hT[:, kt, tt * 128:(tt + 1) * 128], wabb[:, kt * 48:(kt + 1) * 48],
                                                          start=(kt == 0), stop=(kt == KT - 1))) for kt in range(KT)],
                    reads=[hT, wabb], writes=[abp])
            c.op("dve", lambda: nc.vector.tensor_tensor(gtmp[:, :], abp[:, 0:24], self.pv[:, PV_DTB:PV_DTB + 24], ALU.add), reads=[abp, self.pv], writes=[gtmp])
            c.op("act", lambda: nc.scalar.activation(out=gtmp[:, :], in_=gtmp[:, :], func=AF.Exp), reads=[gtmp], writes=[gtmp])
            c.op("act", lambda: nc.scalar.activation(out=gtmp[:, :], in_=gtmp[:, :], func=AF.Ln, bias=self.onec[:, 0:1], scale=1.0), reads=[gtmp, self.onec], writes=[gtmp])
            c.op("dve", lambda: nc.vector.tensor_tensor(g_all[:, tt, :], gtmp[:, :], negA[:, :], ALU.mult), reads=[gtmp, negA], writes=[g_all])
            c.op("act", lambda: nc.scalar.activation(out=bt_all[:, tt, :], in_=abp[:, 24:48], func=AF.Sigmoid), reads=[abp], writes=[bt_all])
            c.op("pool", lambda: nc.gpsimd.tensor_scalar(out=nbt_all[:, tt, :], in0=bt_all[:, tt, :], scalar1=-1.0, scalar2=None, op0=ALU.mult), reads=[bt_all], writes=[nbt_all])
            c.op("pe", lambda: nc.tensor.matmul(gcp[:, :], trif, g_all[:, tt, :], start=True, stop=True), reads=[self.cst, g_all], writes=[gcp])
            c.op("pe", lambda: nc.tensor.matmul(glp[:, :], onesf, g_all[:, tt, :], start=True, stop=True), reads=[self.cst, g_all], writes=[glp])
            c.op("act", lambda: nc.scalar.copy(out=gc_all[:, tt, :], in_=gcp[:, :]), reads=[gcp], writes=[gc_all])
            c.op("act", lambda: nc.scalar.activation(out=egc_all[:, tt, :], in_=gcp[:, :], func=AF.Exp), reads=[gcp], writes=[egc_all])
            c.op("act", lambda: nc.scalar.activation(out=ecd_all[:, tt, :], in_=glp[:, :], func=AF.Exp), reads=[glp], writes=[ecd_all])
            c.op("dve", lambda: nc.vector.tensor_tensor(ekd_all[:, tt, :], glp[:, :], gc_all[:, tt, :], ALU.subtract), reads=[glp, gc_all], writes=[ekd_all])
            c.op("act", lambda: nc.scalar.activation(out=ekd_all[:, tt, :], in_=ekd_all[:, tt, :], func=AF.Exp), reads=[ekd_all], writes=[ekd_all])
        c.barrier()
        raw = sc.sb("raw", [128, T + 3], F32)
        acc = sc.sb("acc", [128, T], F32)
        sil = sc.sb("sil", [128, T], F32)
        qT = sc.sb("qT", [128, T], BF16)
        kT = sc.sb("kT", [128, T], BF16)
        vT = [sc.sb("vT%d" % i, [128, T], BF16) for i in range(2)]
        sz = [sc.sb("sz%d" % i, [128, T], BF16) for i in range(2)]
        yb = [sc.sb("yb%d" % i, [128, T], BF16) for i in range(2)]
        sq = [sc.sb("sq", [128, 512], F32)]
        rstd = sc.sb("rstd", [128, 512], F32)
        S = [sc.sb("S%d" % i, [128, 128], F32) for i in range(2)]
        Sb = [sc.sb("Sb%d" % i, [128, 128], BF16) for i in range(2)]
        c.op("dve", lambda: nc.vector.memset(raw[:, 0:3], 0.0), writes=[raw])

        def cb(name, dt, n=128):
            return [sc.sb("%s%d" % (name, i), [128, n], dt) for i in range(2)]
        rhsg, argT, DT, egB, tmp = cb("rhsg", F32), cb("argT", F32), cb("DT", F32), cb("egB", F32), cb("tmp", F32)
        XTb, Xb, PTb = cb("XTb", BF16), cb("Xb", BF16), cb("PTb", BF16)
        XT2, X2, PT2 = cb("XT2", BF16), cb("X2", BF16), cb("PT2", BF16)
        qkm, qd, rhsR, kd = cb("qkm", BF16), cb("qd", BF16), cb("rhsR", BF16, 256), cb("kd", BF16)
        u0, wsb, wT, ub, on = cb("u0", F32), cb("wsb", BF16), cb("wT", BF16), cb("ub", BF16), cb("on", BF16)
        oss, ors = cb("oss", F32, 1), cb("ors", F32, 1)
        junk = sc.sb("junk", [128, 128], BF16)
        gcB = [view("gcB%d" % i, bankA[:, i * 128:(i + 1) * 128], bankA) for i in range(2)]
        kkp = view("kkp", bankA[:, 256:384], bankA)
        qkp = view("qkp", bankA[:, 384:512], bankA)
        Xp = view("Xp", bankB[:, 0:128], bankB)
        PTp = view("PTp", bankB[:, 256:384], bankB)
        XTp = view("XTp", bankD[:, 128:256], bankD)
        Rp = view("Rp", bankC[:, 0:256], bankC)
        wSp = view("wSp", bankC[:, 256:384], bankC)
        op_ = view("op", bankC[:, 384:512], bankC)
        Snp = view("Snp", bankD[:, 0:128], bankD)
        tp = [view("tp%d" % i, bankE[:, i * 128:(i + 1) * 128], bankE) for i in range(8)]
        X0Tp, wTp, oTp, ktp, vtp = tp[0], tp[1], tp[2], tp[3], [tp[4], tp[5]]

        def conv_silu(cbi, out_tile, out_dt_is_f32):
            wc = [self.pv[:, PV_CONV + cbi * 4 + k:PV_CONV + cbi * 4 + k + 1] for k in range(4)]
            c.op("dve", lambda: nc.vector.tensor_scalar(out=acc[:, :], in0=raw[:, 3:T + 3], scalar1=wc[3], scalar2=None, op0=ALU.mult), reads=[raw, self.pv], writes=[acc])
            for k in (2, 1, 0):
                c.op("dve", lambda: nc.vector.scalar_tensor_tensor(out=acc[:, :], in0=raw[:, k:T + k], scalar=wc[k], in1=acc[:, :], op0=ALU.mult, op1=ALU.add),
                     reads=[raw, acc, self.pv], writes=[acc])
            c.op("act", lambda: nc.scalar.activation(out=out_tile[:, :], in_=acc[:, :], func=AF.Silu), reads=[acc], writes=[out_tile])

        def evraw(tc, p, n):
            c.op("act", lambda: nc.scalar.copy(out=raw[:, 3 + tc * 512:3 + tc * 512 + n], in_=p[:, 0:n]), reads=[p], writes=[raw])

        for hp in pairs:
            idxs = [W["q_idx"][hp], W["k_idx"][hp], W["v_idx"][2 * hp], W["v_idx"][2 * hp + 1], W["z_idx"][2 * hp], W["z_idx"][2 * hp + 1]]
            it = self.wblocks(W["w"], idxs)
            _, wt = next(it)
            self.proj_block(wt, hT, T, psb, evraw)
            conv_silu(hp, sil, True)
            self.fnorm((sq, ssum, rstd), [sil], [self.gsc[:, 5:6]], [qT], T, 1.0)
            _, wt = next(it)
            self.proj_block(wt, hT, T, psb, evraw)
            conv_silu(12 + hp, sil, True)
            self.fnorm((sq, ssum, rstd), [sil], [self.gsc[:, 6:7]], [kT], T, 1.0)
            for i in range(2):
                _, wt = next(it)
                self.proj_block(wt, hT, T, psb, evraw)
                conv_silu(24 + 2 * hp + i, vT[i], False)
            for i in range(2):
                _, wt = next(it)

                def evz(tc, p, n, i=i):
                    c.op("act", lambda: nc.scalar.activation(out=sz[i][:, tc * 512:tc * 512 + n], in_=p[:, 0:n], func=AF.Silu), reads=[p], writes=[sz[i]])
                self.proj_block(wt, hT, T, psb, evz)
            for i in range(2):
                c.op("dve", lambda: nc.vector.memset(S[i][:, :], 0.0), writes=[S[i]])
                c.op("dve", lambda: nc.vector.memset(Sb[i][:, :], 0.0), writes=[Sb[i]])
            for ci in range(NC_):
                csl = slice(ci * 128, (ci + 1) * 128)
                c.op("pe", lambda: nc.tensor.matmul(kkp[:, :], kT[:, csl], kT[:, csl], start=True, stop=True), reads=[kT], writes=[kkp])
                c.op("pe", lambda: nc.tensor.matmul(qkp[:, :], kT[:, csl], qT[:, csl], start=True, stop=True), reads=[kT, qT], writes=[qkp])
                c.op("pe", lambda: nc.tensor.transpose(ktp[:, :], kT[:, csl], identb), reads=[kT, self.cstb], writes=[ktp])
                for i in range(2):
                    vh = 2 * hp + i
                    b = i
                    gcol = g_all[:, ci, vh:vh + 1]
                    gccol = gc_all[:, ci, vh:vh + 1]
                    btcol = bt_all[:, ci, vh:vh + 1]
                    nbcol = nbt_all[:, ci, vh:vh + 1]
                    c.op("pe", lambda: nc.tensor.transpose(vtp[i][:, :], vT[i][:, csl], identb), reads=[vT[i], self.cstb], writes=[vtp[i]])
                    c.op("act", lambda: nc.scalar.copy(out=rhsR[b][:, 0:128], in_=vtp[i][:, :]), reads=[vtp[i]], writes=[rhsR[b]])
                    c.op("dve", lambda: nc.vector.tensor_scalar(out=rhsR[b][:, 128:256], in0=ktp[:, :], scalar1=egc_all[:, ci, vh:vh + 1], scalar2=None, op0=ALU.mult),
                         reads=[ktp, egc_all], writes=[rhsR[b]])
                    c.op("dve", lambda: nc.vector.tensor_scalar(out=kd[b][:, :], in0=ktp[:, :], scalar1=ekd_all[:, ci, vh:vh + 1], scalar2=None, op0=ALU.mult),
                         reads=[ktp, ekd_all], writes=[kd[b]])
                    c.op("pool", lambda: nc.gpsimd.tensor_scalar(out=rhsg[b][:, :], in0=trif, scalar1=gcol, scalar2=None, op0=ALU.mult), reads=[self.cst, g_all], writes=[rhsg[b]])
                    c.op("pe", lambda: nc.tensor.matmul(gcB[b][:, :], onesf, rhsg[b][:, :], start=True, stop=True), reads=[self.cst, rhsg[b]], writes=[gcB[b]])
                    c.op("dve", lambda: nc.vector.scalar_tensor_tensor(out=argT[b][:, :], in0=gcB[b][:, :], scalar=gccol, in1=negi, op0=ALU.subtract, op1=ALU.add),
                         reads=[gcB[b], gc_all, self.cst], writes=[argT[b]])
                    c.op("act", lambda: nc.scalar.activation(out=DT[b][:, :], in_=argT[b][:, :], func=AF.Exp), reads=[argT[b]], writes=[DT[b]])
                    c.op("act", lambda: nc.scalar.activation(out=egB[b][:, :], in_=gcB[b][:, :], func=AF.Exp), reads=[gcB[b]], writes=[egB[b]])
                    c.op("dve", lambda: nc.vector.scalar_tensor_tensor(out=tmp[b][:, :], in0=kkp[:, :], scalar=nbcol, in1=DT[b][:, :], op0=ALU.mult, op1=ALU.mult),
                         reads=[kkp, nbt_all, DT[b]], writes=[tmp[b]])
                    c.op("pool", lambda: nc.gpsimd.tensor_tensor(XTb[b][:, :], tmp[b][:, :], strictf, ALU.mult), reads=[tmp[b], self.cst], writes=[XTb[b]])
                    c.op("pool", lambda: nc.gpsimd.tensor_tensor(PTb[b][:, :], XTb[b][:, :], identf, ALU.add), reads=[XTb[b], self.cst], writes=[PTb[b]])
                    c.op("pe", lambda: nc.tensor.transpose(X0Tp[:, :], XTb[b][:, :], identb), reads=[XTb[b], self.cstb], writes=[X0Tp])
                    c.op("act", lambda: nc.scalar.copy(out=Xb[b][:, :], in_=X0Tp[:, :]), reads=[X0Tp], writes=[Xb[b]])
                    c.op("dve", lambda: nc.vector.tensor_tensor(qkm[b][:, :], qkp[:, :], DT[b][:, :], ALU.mult), reads=[qkp, DT[b]], writes=[qkm[b]])
                    c.op("pool", lambda: nc.gpsimd.tensor_tensor(qd[b][:, :], qT[:, csl], egB[b][:, :], ALU.mult), reads=[qT, egB[b]], writes=[qd[b]])
                    Xc, XTc, PTc = Xb[b], XTb[b], PTb[b]
                    Xn_, XTn_, PTn_ = X2[b], XT2[b], PT2[b]
                    for l in range(1, 7):
                        c.op("pe", lambda: nc.tensor.matmul(Xp[:, :], XTc[:, :], Xc[:, :], start=True, stop=True), reads=[XTc, Xc], writes=[Xp])
                        if l < 6:
                            c.op("pe", lambda: nc.tensor.matmul(XTp[:, :], Xc[:, :], XTc[:, :], start=True, stop=True), reads=[XTc, Xc], writes=[XTp])
                        c.op("act", lambda: nc.scalar.copy(out=Xn_[:, :], in_=Xp[:, :]), reads=[Xp], writes=[Xn_])
                        if l < 6:
                            c.op("dve", lambda: nc.vector.tensor_copy(XTn_[:, :], XTp[:, :]), reads=[XTp], writes=[XTn_])
                        c.group("pe", [lambda: nc.tensor.matmul(PTp[:, :], identb, PTc[:, :], start=True, stop=False),
                                       lambda: nc.tensor.matmul(PTp[:, :], Xn_[:, :], PTc[:, :], start=False, stop=True)],
                                reads=[self.cstb, PTc, Xn_], writes=[PTp])
                        if l % 2:
                            c.op("act", lambda: nc.scalar.copy(out=PTn_[:, :], in_=PTp[:, :]), reads=[PTp], writes=[PTn_])
                        else:
                            c.op("dve", lambda: nc.vector.tensor_copy(PTn_[:, :], PTp[:, :]), reads=[PTp], writes=[PTn_])
                        Xc, Xn_ = Xn_, Xc
                        XTc, XTn_ = XTn_, XTc
                        PTc, PTn_ = PTn_, PTc
                    c.op("pe", lambda: nc.tensor.matmul(Rp[:, :], PTc[:, :], rhsR[b][:, :], start=True, stop=True), reads=[PTc, rhsR[b]], writes=[Rp])
                    c.op("dve", lambda: nc.vector.tensor_scalar(out=u0[b][:, :], in0=Rp[:, 0:128], scalar1=btcol, scalar2=None, op0=ALU.mult), reads=[Rp, bt_all], writes=[u0[b]])
                    c.op("dve", lambda: nc.vector.tensor_scalar(out=wsb[b][:, :], in0=Rp[:, 128:256], scalar1=btcol, scalar2=None, op0=ALU.mult), reads=[Rp, bt_all], writes=[wsb[b]])
                    c.op("pe", lambda: nc.tensor.transpose(wTp[:, :], wsb[b][:, :], identb), reads=[wsb[b], self.cstb], writes=[wTp])
                    c.op("act", lambda: nc.scalar.copy(out=wT[b][:, :], in_=wTp[:, :]), reads=[wTp], writes=[wT[b]])
                    c.op("pe", lambda: nc.tensor.matmul(wSp[:, :], wT[b][:, :], Sb[i][:, :], start=True, stop=True), reads=[wT[b], Sb[i]], writes=[wSp])
                    c.op("dve", lambda: nc.vector.tensor_tensor(ub[b][:, :], u0[b][:, :], wSp[:, :], ALU.subtract), reads=[u0[b], wSp], writes=[ub[b]])
                    c.group("pe", [lambda: nc.tensor.matmul(op_[:, :], qd[b][:, :], Sb[i][:, :], start=True, stop=False),
                                   lambda: nc.tensor.matmul(op_[:, :], qkm[b][:, :], ub[b][:, :], start=False, stop=True)],
                            reads=[qd[b], Sb[i], qkm[b], ub[b]], writes=[op_])
                    c.op("pe", lambda: nc.tensor.matmul(Snp[:, :], kd[b][:, :], ub[b][:, :], start=True, stop=True), reads=[kd[b], ub[b]], writes=[Snp])
                    c.op("dve", lambda: nc.vector.scalar_tensor_tensor(out=S[i][:, :], in0=S[i][:, :], scalar=ecd_all[:, ci, vh:vh + 1], in1=Snp[:, :], op0=ALU.mult, op1=ALU.add),
                         reads=[S[i], ecd_all, Snp], writes=[S[i]])
                    c.op("pool", lambda: nc.gpsimd.tensor_copy(Sb[i][:, :], S[i][:, :]), reads=[S[i]], writes=[Sb[i]])
                    c.op("act", lambda: nc.scalar.activation(out=junk[:, :], in_=op_[:, :], func=AF.Square, accum_out=oss[b][:, 0:1]), reads=[op_], writes=[junk, oss[b]], multi=True)
                    c.op("act", lambda: nc.scalar.activation(out=ors[b][:, :], in_=oss[b][:, :], func=AF.Sqrt, bias=self.epsc[:, 0:1], scale=1.0 / 128.0),
                         reads=[oss[b], self.epsc], writes=[ors[b]])
                    c.op("dve", lambda: nc.vector.reciprocal(ors[b][:, :], ors[b][:, :]), reads=[ors[b]], writes=[ors[b]])
                    c.op("dve", lambda: nc.vector.tensor_scalar(out=on[b][:, :], in0=op_[:, :], scalar1=ors[b][:, 0:1], scalar2=None, op0=ALU.mult), reads=[op_, ors[b]], writes=[on[b]])
                    c.op("pe", lambda: nc.tensor.transpose(oTp[:, :], on[b][:, :], identb), reads=[on[b], self.cstb], writes=[oTp])
                    c.op("dve", lambda: nc.vector.scalar_tensor_tensor(out=yb[i][:, csl], in0=oTp[:, :], scalar=self.pv[:, PV_DNO:PV_DNO + 1], in1=sz[i][:, csl], op0=ALU.mult, op1=ALU.mult),
                         reads=[oTp, self.pv, sz[i]], writes=[yb[i]])
            for i in range(2):
                c.dma("sp", self.yT_d[2 * hp + i, :, :], yb[i][:, :], reads=[yb[i]])
        if len(pairs) < 12:
            zt = sc.sb("zt", [128, T], BF16)
            c.op("dve", lambda: nc.vector.memset(zt[:, :], 0.0), writes=[zt])
            for h in range(NH):
                if h // 2 not in pairs:
                    c.dma("sp", self.yT_d[h, :, :], zt[:, :], reads=[zt])
        c.barrier()


Layers.dn_heads = dn_heads


def build_full():
    nc = bass.Bass("TRN2", target_bir_lowering=False)
    P = Layers(nc)
    c = P.ctx
    out = c.dram("out", [T, D], F32, "ExternalOutput")
    W0 = dn_decl(P, 0)
    W1 = sb_decl(P, 1)
    P.load_consts()
    P.layer(0, "dn", W0, P.x_in, P.x1_d, first=True)
    P.layer(1, "sb", W1, P.x1_d, out)
    c.finish()
    return nc, c


def kernel(x, mem, norm_g, mem_norm_g, mem_w_kv, xa_q_norm_g, xa_k_norm_g, w_out,
           dn_w_in, dn_conv_w, dn_a_log, dn_dt_bias, dn_out_norm_g,
           sb_w_in, sb_q_norm_g, sb_k_norm_g):
    inp = {k: np.asarray(v, dtype=np.float32) for k, v in dict(
        x=x, mem=mem, norm_g=norm_g, mem_norm_g=mem_norm_g, mem_w_kv=mem_w_kv, xa_q_norm_g=xa_q_norm_g,
        xa_k_norm_g=xa_k_norm_g, w_out=w_out, dn_w_in=dn_w_in, dn_conv_w=dn_conv_w, dn_a_log=dn_a_log,
        dn_dt_bias=dn_dt_bias, dn_out_norm_g=dn_out_norm_g, sb_w_in=sb_w_in, sb_q_norm_g=sb_q_norm_g,
        sb_k_norm_g=sb_k_norm_g).items()}
    nc, c = build_full()
    g = np.concatenate([inp["norm_g"][0], inp["norm_g"][1], inp["mem_norm_g"]])
    shared = {"gB": np.ascontiguousarray(np.broadcast_to(g, (128, 3 * D))), "cst": make_consts(), "pv": make_pv(inp),
              "wo0": np.ascontiguousarray(inp["w_out"][0]), "wo1": np.ascontiguousarray(inp["w_out"][1])}
    shared.update(dn_weights_host(inp))
    shared.update(sb_weights_host(inp))
    shared.update(kv_weights_host(inp, 0))
    shared.update(kv_weights_host(inp, 1))
    B = inp["x"].shape[0]
    in_maps = []
    for core in range(8):
        m = dict(shared)
        if core < B:
            m["x"] = np.ascontiguousarray(inp["x"][core])
            m["mem"] = np.ascontiguousarray(inp["mem"][core])
        else:
            m["x"] = np.zeros((T, D), np.float32)
            m["mem"] = np.zeros((NMEM, D), np.float32)
        in_maps.append(m)
    res = run_bass_kernel_spmd(nc, in_maps, core_ids=list(range(8)))
    return np.stack([res.results[b]["out"] for b in range(B)], 0).astype(np.float32)


def run_streams(gens, weights=None):
    gens = list(gens)
    weights = list(weights) if weights is not None else [1] * len(gens)
    items = list(zip(gens, weights))
    while items:
        for it in list(items):
            g, w = it
            for _ in range(w):
                try:
                    next(g)
                except StopIteration:
                    items.remove(it)
                    break


class _B:
    pass


def _pool_scale(nc, out, in0, s):
    return nc.gpsimd.tensor_scalar(out=out, in0=in0, scalar1=s, scalar2=0.0, op0=ALU.mult, op1=ALU.add)


def sb_heads2(self, hT, W, psb):
    c = self.ctx
    nc = self.nc
    heads = self.dbg.get("heads")
    heads = list(range(self.nh)) if heads is None else heads
    w_d, q_idx, k_idx, z_idx, wv_d = W["w"], W["q_idx"], W["k_idx"], W["z_idx"], W["wvm"]
    with Scope(c) as sc:
        if len(heads) < self.nh:
            zt = sc.sb("zt", [128, T], BF16)
            c.op("dve", lambda: nc.vector.memset(zt[:, :], 0.0), writes=[zt])
            for h in range(self.nh):
                if h not in heads:
                    c.dma("sp", self.yT_d[h, :, :], zt[:, :], reads=[zt])
        with Scope(c) as s0:
            wvb = s0.sb("wvb", [128, 8192], BF16)
            vb = [s0.sb("vb%d" % i, [128, 512], BF16) for i in range(2)]
            for g in sorted({h // 4 for h in heads}):
                for pc in range(4):
                    s = self.w_fetch(wv_d[g * 4 + pc, :, :])
                    self.w_cast(s, wvb, wvb[:, pc * 2048:(pc + 1) * 2048])
                for tt in range(16):
                    p = psb[tt % 2]
                    v = vb[tt % 2]
                    c.group("pe", [(lambda kt=kt: nc.tensor.matmul(p[:, 0:512], hT[:, kt, tt * 128:(tt + 1) * 128], wvb[:, kt * 512:(kt + 1) * 512],
                                                                  start=(kt == 0), stop=(kt == KT - 1))) for kt in range(KT)],
                            reads=[hT, wvb], writes=[p])
                    c.op("act", lambda: nc.scalar.copy(out=v[:, :], in_=p[:, 0:512]), reads=[p], writes=[v])
                    c.dma("sp", self.vtok_d[tt, :, g * 512:(g + 1) * 512], v[:, :], reads=[v])
            c.barrier()
        banks = list(psb) + [sc.ps("sbk%d" % i, [128, 512], F32) for i in range(6)]
        nmask = sc.sb("nmask", [128, 256], F32)
        mtmp = sc.sb("mtmp", [128, 128], F32)
        c.op("dve", lambda: nc.vector.tensor_tensor(mtmp[:, :], self.cst[:, C_LOW:C_LOW + 128], self.cst[:, C_ID:C_ID + 128], ALU.add), reads=[self.cst], writes=[mtmp])
        c.op("dve", lambda: nc.vector.tensor_scalar(out=nmask[:, 0:128].bitcast(F32R), in0=mtmp[:, :], scalar1=-1.0, scalar2=None, op0=ALU.mult),
             reads=[mtmp], writes=[nmask])
        c.op("dve", lambda: nc.vector.tensor_scalar(out=nmask[:, 128:256].bitcast(F32R), in0=self.cst[:, C_STRICT:C_STRICT + 128], scalar1=-1.0, scalar2=None, op0=ALU.mult),
             reads=[self.cst], writes=[nmask])
        onesr_t = sc.sb("onesr", [128, 128], F32)
        c.op("dve", lambda: nc.vector.tensor_copy(onesr_t[:, :].bitcast(F32R), self.cst[:, C_ONES:C_ONES + 128]), reads=[self.cst], writes=[onesr_t])
        onesr = onesr_t[:, :].bitcast(F32R)
        Mge = nmask[:, 0:128].bitcast(F32R)
        Mlt = nmask[:, 128:256].bitcast(F32R)
        strict = self.cst[:, C_STRICT:C_STRICT + 128]
        negs = self.cst[:, C_NEGS:C_NEGS + 128]
        onesf = self.cst[:, C_ONES:C_ONES + 128]

        def stream(s, hs, delay):
            for _ in range(delay):
                yield
            b0, b1, b2, b3 = banks[4 * s:4 * s + 4]
            zps = [b0, b1]
            Pps = b2
            Ops = b3
            wst = self.wst[s]
            wbf = self.wbf if s == 0 else [sc.sb("wbfs%d" % i, [128, 2048], BF16) for i in range(2)]
            qT = sc.sb("qT%d" % s, [128, T], BF16)
            kT = sc.sb("kT%d" % s, [128, T], BF16)
            nkT = sc.sb("nkT%d" % s, [128, T], BF16)
            sz = sc.sb("sz%d" % s, [128, T], BF16)
            y = sc.sb("y%d" % s, [128, T], BF16)
            vt = sc.sb("vt%d" % s, [128, 16, 128], BF16)
            sq = sc.sb("sq%d" % s, [128, 512], F32)
            rstd = sc.sb("rstd%d" % s, [128, 512], F32)
            sqr = sc.sb("sqr%d" % s, [128, 512], F32)
            ez = [sc.sb("ez%d_%d" % (s, i), [128, 512], F32) for i in range(2)]
            spr = [sc.sb("spr%d_%d" % (s, i), [128, 512], F32) for i in range(2)]
            wb = [sc.sb("wb%d_%d" % (s, i), [128, 512], BF16) for i in range(2)]
            pj = 0
            for h in hs:
                c.dma("sp", vt[:, :, :], self.vtok_d[:, :, h * 128:(h + 1) * 128].rearrange("t p e -> p t e"), writes=[vt])
                yield
                widx = [z_idx[h], q_idx[h], k_idx[h]]
                c.dma("sp", wst[:, :], w_d[widx[0], :, :], writes=[wst])
                for bi in range(3):
                    wt = wbf[bi % 2]
                    c.op("pool", lambda: nc.gpsimd.tensor_copy(wt[:, :], wst[:, :]), reads=[wst], writes=[wt])
                    if bi + 1 < 3:
                        c.dma("sp", wst[:, :], w_d[widx[bi + 1], :, :], writes=[wst])
                    yield
                    for tc in range(4):
                        sl = slice(tc * 512, (tc + 1) * 512)
                        p = zps[pj]
                        pj ^= 1
                        for part in range(KT // SBP):
                            c.group("pe", [(lambda kt=kt: nc.tensor.matmul(p[:, :], wt[:, kt * 128:(kt + 1) * 128], hT[:, kt, sl],
                                                                          start=(kt == 0), stop=(kt == KT - 1))) for kt in range(part * SBP, part * SBP + SBP)],
                                    reads=[wt, hT], writes=[p] if part in (0, KT // SBP - 1) else [], lhs=[wt])
                            yield
                        if bi == 0:
                            c.op("act", lambda: nc.scalar.activation(out=sq[:, :], in_=p[:, :], func=AF.Exp, scale=-1.0), reads=[p], writes=[sq])
                            yield
                            c.op("act", lambda: nc.scalar.activation(out=sq[:, :], in_=sq[:, :], func=AF.Ln, bias=self.onec[:, 0:1], scale=1.0), reads=[sq, self.onec], writes=[sq])
                            yield
                            c.op("act", lambda: nc.scalar.activation(out=sq[:, :], in_=sq[:, :], func=AF.Exp, scale=-1.0), reads=[sq], writes=[sq])
                            yield
                            c.op("dve", lambda: nc.vector.tensor_tensor(sz[:, sl], p[:, :], sq[:, :], ALU.mult), reads=[p, sq], writes=[sz])
                            yield
                        else:
                            dst = qT if bi == 1 else kT
                            gcol = self.gsc[:, GS_SBQ:GS_SBQ + 1] if bi == 1 else self.pv[:, PV_SBK:PV_SBK + 1]
                            c.op("act", lambda: nc.scalar.activation(out=sqr[:, :].bitcast(F32R), in_=p[:, :], func=AF.Square), reads=[p], writes=[sqr])
                            yield
                            c.op("pe", lambda: nc.tensor.matmul(Pps[:, :], onesr, sqr[:, :].bitcast(F32R), start=True, stop=True), reads=[onesr_t, sqr], writes=[Pps], lhs=[onesr_t])
                            yield
                            c.op("act", lambda: nc.scalar.activation(out=rstd[:, :], in_=Pps[:, :], func=AF.Ln, bias=self.epsc[:, 0:1], scale=1.0 / 128.0),
                                 reads=[Pps, self.epsc], writes=[rstd])
                            yield
                            c.op("act", lambda: nc.scalar.activation(out=rstd[:, :], in_=rstd[:, :], func=AF.Exp, scale=-0.5), reads=[rstd], writes=[rstd])
                            yield
                            c.op("dve", lambda: nc.vector.scalar_tensor_tensor(out=dst[:, sl], in0=p[:, :], scalar=gcol, in1=rstd[:, :], op0=ALU.mult, op1=ALU.mult),
                                 reads=[p, rstd, self.gsc, self.pv], writes=[dst])
                            yield
                blocks = [(qc, kb) for qc in range(4) for kb in range(4 * qc + 3, -1, -1)]

                c.op("pool", lambda: _pool_scale(nc, nkT[:, :], kT[:, :], -1.0), reads=[kT], writes=[nkT])
                yield

                def stageA(i):
                    qc, kb = blocks[i]
                    b = i % 2
                    lo = max(0, kb * 128 - qc * 512)
                    diag = kb >= 4 * qc
                    z = zps[b]
                    e = ez[b]
                    sp_ = spr[b]
                    c.op("pe", lambda: nc.tensor.matmul(z[:, lo:512], kT[:, kb * 128:(kb + 1) * 128], qT[:, qc * 512 + lo:(qc + 1) * 512], start=True, stop=True),
                         reads=[kT, qT], writes=[z], lhs=[kT])
                    yield
                    c.op("act", lambda: nc.scalar.activation(out=e[:, lo:512], in_=z[:, lo:512], func=AF.Exp), reads=[z], writes=[e])
                    yield
                    c.op("act", lambda: nc.scalar.activation(out=sp_[:, lo:512].bitcast(F32R), in_=e[:, lo:512], func=AF.Ln, bias=self.onec[:, 0:1], scale=1.0),
                         reads=[e, self.onec], writes=[sp_])
                    yield
                    if diag:
                        c.op("dve", lambda: nc.vector.tensor_tensor(sp_[:, lo:lo + 128].bitcast(F32R), sp_[:, lo:lo + 128], strict, ALU.mult), reads=[sp_, self.cst], writes=[sp_])
                        yield

                def stageB(i):
                    qc, kb = blocks[i]
                    b = i % 2
                    lo = max(0, kb * 128 - qc * 512)
                    first = kb == 4 * qc + 3
                    last = kb == 0
                    diag = kb >= 4 * qc
                    sp_ = spr[b]
                    ksl = slice(kb * 128, (kb + 1) * 128)
                    qsl = slice(qc * 512 + lo, (qc + 1) * 512)
                    c.group("pe", [lambda: nc.tensor.matmul(Pps[:, lo:512], Mge, sp_[:, lo:512].bitcast(F32R), start=first, stop=False),
                                   lambda: nc.tensor.matmul(Pps[:, lo:512], kT[:, ksl], qT[:, qsl], start=False, stop=last)],
                            reads=[nmask, sp_, kT, qT], writes=[Pps], lhs=[nmask, kT])
                    yield
                    c.op("act", lambda: nc.scalar.activation(out=wb[b][:, lo:512], in_=Pps[:, lo:512], func=AF.Exp), reads=[Pps], writes=[wb[b]])
                    yield
                    if diag:
                        c.op("pool", lambda: nc.gpsimd.tensor_tensor(wb[b][:, lo:lo + 128], wb[b][:, lo:lo + 128], strict, ALU.mult), reads=[wb[b], self.cst], writes=[wb[b]])
                        yield
                    if not last:
                        c.group("pe", [lambda: nc.tensor.matmul(Pps[:, lo:512], nkT[:, ksl], qT[:, qsl], start=False, stop=False),
                                       lambda: nc.tensor.matmul(Pps[:, lo:512], Mlt, sp_[:, lo:512].bitcast(F32R), start=False, stop=False)],
                                reads=[nmask, sp_, nkT, qT], writes=[Pps], lhs=[nmask, nkT])
                        yield
                    c.op("pe", lambda: nc.tensor.matmul(Ops[:, lo:512], vt[:, kb, :], wb[b][:, lo:512], start=first, stop=last), reads=[vt, wb[b]], writes=[Ops], lhs=[vt])
                    yield
                    if last:
                        c.op("dve", lambda: nc.vector.tensor_tensor(y[:, qc * 512:(qc + 1) * 512], Ops[:, :], sz[:, qc * 512:(qc + 1) * 512], ALU.mult),
                             reads=[Ops, sz], writes=[y])
                        yield

                def att():
                    for i in range(len(blocks) + 1):
                        if i < len(blocks):
                            yield from stageA(i)
                        if i >= 1:
                            yield from stageB(i - 1)
                kk_ = 0
                for _ in att():
                    kk_ += 1
                    if kk_ % SB_K == 0:
                        yield
                c.dma("sp", self.yT_d[h, :, :], y[:, :], reads=[y])
                yield

        run_streams([stream(0, heads[0::2], 0), stream(1, heads[1::2], SB_DELAY)])
        c.barrier()


Layers.sb_heads2 = sb_heads2


def dn_heads2(self, hT, W, psb):
    c = self.ctx
    nc = self.nc
    pairs = self.dbg.get("pairs")
    pairs = list(range(12)) if pairs is None else pairs
    NC_ = T // 128
    with Scope(c) as sc:
        onesf = self.cst[:, C_ONES:C_ONES + 128]
        trif = self.cst[:, C_TRI:C_TRI + 128]
        negi = self.cst[:, C_NEGI:C_NEGI + 128]
        strictf = self.cst[:, C_STRICT:C_STRICT + 128]
        identf = self.cst[:, C_ID:C_ID + 128]
        identb = self.cstb[:, C_ID:C_ID + 128]
        g_all = sc.sb("g_all", [128, NC_, 24], F32)
        bt_all = sc.sb("bt_all", [128, NC_, 24], F32)
        nbt_all = sc.sb("nbt_all", [128, NC_, 24], F32)
        gc_all = sc.sb("gc_all", [128, NC_, 24], F32)
        egc_all = sc.sb("egc_all", [128, NC_, 24], F32)
        ekd_all = sc.sb("ekd_all", [128, NC_, 24], F32)
        ecd_all = sc.sb("ecd_all", [128, NC_, 24], F32)
        bankA = sc.ps("bankA", [128, 512], F32)
        bankB = [sc.ps("bankB%d" % i, [128, 512], F32) for i in range(2)]
        bankC = [sc.ps("bankC%d" % i, [128, 512], F32) for i in range(2)]
        bankE = sc.ps("bankE", [128, 1024], BF16)
        with Scope(c) as s0:
            negA = s0.sb("negA", [128, 24], F32)
            wabf = s0.sb("wabf", [128, 768], F32)
            wabb = s0.sb("wabb", [128, 768], BF16)
            gtmp = s0.sb("gtmp", [128, 48], F32)
            c.dma("sp", wabf[:, :], W["wab"][:, :], writes=[wabf])
            c.op("dve", lambda: nc.vector.tensor_copy(wabb[:, :], wabf[:, :]), reads=[wabf], writes=[wabb])
            c.op("act", lambda: nc.scalar.activation(out=negA[:, :], in_=self.pv[:, PV_ALOG:PV_ALOG + 24], func=AF.Exp), reads=[self.pv], writes=[negA])
            c.op("dve", lambda: nc.vector.tensor_scalar(out=negA[:, :], in0=negA[:, :], scalar1=-1.0, scalar2=None, op0=ALU.mult), reads=[negA], writes=[negA])
            abp = view("abp", bankA[:, 0:48], bankA)
            gcp = view("gcp", bankB[0][:, 0:24], bankB[0])
            glp = view("glp", bankC[0][:, 0:24], bankC[0])
            for tt in range(NC_):
                c.group("pe", [(lambda kt=kt: nc.tensor.matmul(abp[:, :], hT[:, kt, tt * 128:(tt + 1) * 128], wabb[:, kt * 48:(kt + 1) * 48],
                                                              start=(kt == 0), stop=(kt == KT - 1))) for kt in range(KT)],
                        reads=[hT, wabb], writes=[abp])
                c.op("dve", lambda: nc.vector.tensor_tensor(gtmp[:, 0:24], abp[:, 0:24], self.pv[:, PV_DTB:PV_DTB + 24], ALU.add), reads=[abp, self.pv], writes=[gtmp])
                c.op("dve", lambda: nc.vector.tensor_scalar(out=gtmp[:, 24:48], in0=abp[:, 24:48], scalar1=-1.0, scalar2=None, op0=ALU.mult), reads=[abp], writes=[gtmp])
                c.op("act", lambda: nc.scalar.activation(out=gtmp[:, :], in_=gtmp[:, :], func=AF.Exp), reads=[gtmp], writes=[gtmp])
                c.op("act", lambda: nc.scalar.activation(out=gtmp[:, :], in_=gtmp[:, :], func=AF.Ln, bias=self.onec[:, 0:1], scale=1.0), reads=[gtmp, self.onec], writes=[gtmp])
                c.op("dve", lambda: nc.vector.tensor_tensor(g_all[:, tt, :], gtmp[:, 0:24], negA[:, :], ALU.mult), reads=[gtmp, negA], writes=[g_all])
                c.op("act", lambda: nc.scalar.activation(out=bt_all[:, tt, :], in_=gtmp[:, 24:48], func=AF.Exp, scale=-1.0), reads=[gtmp], writes=[bt_all])
                c.op("pool", lambda: _pool_scale(nc, nbt_all[:, tt, :], bt_all[:, tt, :], -1.0), reads=[bt_all], writes=[nbt_all])
                c.op("pe", lambda: nc.tensor.matmul(gcp[:, :], trif, g_all[:, tt, :], start=True, stop=True), reads=[self.cst, g_all], writes=[gcp])
                c.op("pe", lambda: nc.tensor.matmul(glp[:, :], onesf, g_all[:, tt, :], start=True, stop=True), reads=[self.cst, g_all], writes=[glp])
                c.op("act", lambda: nc.scalar.copy(out=gc_all[:, tt, :], in_=gcp[:, :]), reads=[gcp], writes=[gc_all])
                c.op("act", lambda: nc.scalar.activation(out=egc_all[:, tt, :], in_=gcp[:, :], func=AF.Exp), reads=[gcp], writes=[egc_all])
                c.op("act", lambda: nc.scalar.activation(out=ecd_all[:, tt, :], in_=glp[:, :], func=AF.Exp), reads=[glp], writes=[ecd_all])
                c.op("dve", lambda: nc.vector.tensor_tensor(ekd_all[:, tt, :], glp[:, :], gc_all[:, tt, :], ALU.subtract), reads=[glp, gc_all], writes=[ekd_all])
                c.op("act", lambda: nc.scalar.activation(out=ekd_all[:, tt, :], in_=ekd_all[:, tt, :], func=AF.Exp), reads=[ekd_all], writes=[ekd_all])
            c.barrier()
        raw = sc.sb("raw", [128, T + 3], F32)
        acc = sc.sb("acc", [128, 512], F32)
        stmp = sc.sb("stmp", [128, 512], F32)
        sil = acc
        sqb = stmp
        rstd = sc.sb("rstd", [128, 512], F32)
        vTc = sc.sb("vTc", [128, 512], BF16)
        c.op("dve", lambda: nc.vector.memset(raw[:, 0:3], 0.0), writes=[raw])
        OUT = []
        for i in range(2):
            o = _B()
            o.qT = sc.sb("qT%d" % i, [128, T], BF16)
            o.kT = sc.sb("kT%d" % i, [128, T], BF16)
            o.ktok = sc.sb("ktok%d" % i, [128, NC_, 128], BF16)
            o.vtok = [sc.sb("vtok%d_%d" % (i, j), [128, NC_, 128], BF16) for j in range(2)]
            o.sz = [sc.sb("sz%d_%d" % (i, j), [128, T], BF16) for j in range(2)]
            OUT.append(o)
        kkp = view("kkp", bankA[:, 256:384], bankA)
        qkp = view("qkp", bankA[:, 384:512], bankA)
        Snp = [view("Snp%d" % i, bankA[:, i * 128:(i + 1) * 128], bankA) for i in range(2)]
        tp = [view("tp%d" % i, bankE[:, i * 128:(i + 1) * 128], bankE) for i in range(8)]
        VB = []
        for i in range(2):
            b = _B()
            b.gcB = view("gcB%d" % i, bankB[i][:, 0:128], bankB[i])
            b.Xp = view("Xp%d" % i, bankB[i][:, 128:256], bankB[i])
            b.XTp = view("XTp%d" % i, bankB[i][:, 256:384], bankB[i])
            b.PTp = view("PTp%d" % i, bankB[i][:, 384:512], bankB[i])
            b.Rp = view("Rp%d" % i, bankC[i][:, 0:256], bankC[i])
            b.wSp = view("wSp%d" % i, bankC[i][:, 256:384], bankC[i])
            b.op_ = view("op%d" % i, bankC[i][:, 384:512], bankC[i])
            b.X0Tp, b.wTp, b.oTp = tp[i * 3], tp[i * 3 + 1], tp[i * 3 + 2]
            for nm, dt in (("rhsg", F32), ("argT", F32), ("DT", F32), ("egB", F32), ("tmp", F32)):
                setattr(b, nm, sc.sb("%s_%d" % (nm, i), [128, 128], dt))
            for nm in ("XTa", "Xa", "PTa", "XTb", "Xb", "PTb", "kg", "wsb", "ub", "on", "junk", "Sb"):
                setattr(b, nm, sc.sb("%s_%d" % (nm, i), [128, 128], BF16))
            b.S = sc.sb("S_%d" % i, [128, 128], F32)
            b.oss = sc.sb("oss_%d" % i, [128, 1], F32)
            b.ors = sc.sb("ors_%d" % i, [128, 1], F32)
            b.u0 = [sc.sb("u0_%d_%d" % (i, j), [128, 128], F32) for j in range(2)]
            for nm in ("wT", "qd", "qkm", "kd"):
                setattr(b, nm, [sc.sb("%s_%d_%d" % (nm, i, j), [128, 128], BF16) for j in range(2)])
            b.y = [sc.sb("y_%d_%d" % (i, j), [128, 128], BF16) for j in range(2)]
            VB.append(b)
        ptp = [tp[6], tp[7]]
        st = _B()
        st.pj = 0
        st.tpj = 0

        def silu_to(src_ap, n, dst_ap, reads, dst_tile):
            c.op("act", lambda: nc.scalar.activation(out=stmp[:, 0:n], in_=src_ap, func=AF.Exp, scale=-1.0), reads=reads, writes=[stmp])
            yield
            c.op("act", lambda: nc.scalar.activation(out=stmp[:, 0:n], in_=stmp[:, 0:n], func=AF.Ln, bias=self.onec[:, 0:1], scale=1.0), reads=[stmp, self.onec], writes=[stmp])
            yield
            c.op("act", lambda: nc.scalar.activation(out=stmp[:, 0:n], in_=stmp[:, 0:n], func=AF.Exp, scale=-1.0), reads=[stmp], writes=[stmp])
            yield
            c.op("dve", lambda: nc.vector.tensor_tensor(dst_ap, src_ap, stmp[:, 0:n], ALU.mult), reads=reads + [stmp], writes=[dst_tile])
            yield

        def prologue(hp, o):
            idxs = [W["q_idx"][hp], W["k_idx"][hp], W["v_idx"][2 * hp], W["v_idx"][2 * hp + 1], W["z_idx"][2 * hp], W["z_idx"][2 * hp + 1]]
            cbis = [hp, 12 + hp, 24 + 2 * hp, 24 + 2 * hp + 1]
            slot = self.w_fetch(W["w"][idxs[0], :, :])
            for bi in range(6):
                wt = self.w_cast(slot)
                if bi + 1 < 6:
                    slot = self.w_fetch(W["w"][idxs[bi + 1], :, :])
                yield
                for tc in range(4):
                    sl = slice(tc * 512, (tc + 1) * 512)
                    p = psb[st.pj]
                    st.pj ^= 1
                    for part in range(4):
                        c.group("pe", [(lambda kt=kt: nc.tensor.matmul(p[:, :], wt[:, kt * 128:(kt + 1) * 128], hT[:, kt, sl],
                                                                      start=(kt == 0), stop=(kt == KT - 1))) for kt in range(part * 4, part * 4 + 4)],
                                reads=[wt, hT], writes=[p] if part in (0, 3) else [])
                        yield
                    if bi >= 4:
                        yield from silu_to(p[:, :], 512, o.sz[bi - 4][:, sl], [p], o.sz[bi - 4])
                        continue
                    c.op("act", lambda: nc.scalar.copy(out=raw[:, 3 + tc * 512:3 + (tc + 1) * 512], in_=p[:, :]), reads=[p], writes=[raw])
                    yield
                    wc = [self.pv[:, PV_CONV + cbis[bi] * 4 + k:PV_CONV + cbis[bi] * 4 + k + 1] for k in range(4)]
                    c.op("dve", lambda: nc.vector.tensor_scalar(out=acc[:, :], in0=raw[:, 3 + tc * 512:3 + (tc + 1) * 512], scalar1=wc[3], scalar2=None, op0=ALU.mult),
                         reads=[raw, self.pv], writes=[acc])
                    yield
                    for k in (2, 1, 0):
                        c.op("dve", lambda: nc.vector.scalar_tensor_tensor(out=acc[:, :], in0=raw[:, k + tc * 512:k + (tc + 1) * 512], scalar=wc[k], in1=acc[:, :],
                                                                           op0=ALU.mult, op1=ALU.add), reads=[raw, acc, self.pv], writes=[acc])
                        yield
                    if bi < 2:
                        yield from silu_to(acc[:, :], 512, sil[:, :], [acc], sil)
                        c.op("act", lambda: nc.scalar.activation(out=sqb[:, :], in_=sil[:, :], func=AF.Square), reads=[sil], writes=[sqb])
                        yield
                        p2 = psb[st.pj]
                        st.pj ^= 1
                        c.op("pe", lambda: nc.tensor.matmul(p2[:, :], onesf, sqb[:, :], start=True, stop=True), reads=[self.cst, sqb], writes=[p2])
                        yield
                        c.op("act", lambda: nc.scalar.activation(out=rstd[:, :], in_=p2[:, :], func=AF.Ln, bias=self.epsc[:, 0:1], scale=1.0), reads=[p2, self.epsc], writes=[rstd])
                        yield
                        c.op("act", lambda: nc.scalar.activation(out=rstd[:, :], in_=rstd[:, :], func=AF.Exp, scale=-0.5), reads=[rstd], writes=[rstd])
                        yield
                        dst = o.qT if bi == 0 else o.kT
                        gcol = self.gsc[:, 5:6] if bi == 0 else self.gsc[:, 6:7]
                        c.op("dve", lambda: nc.vector.scalar_tensor_tensor(out=dst[:, sl], in0=sil[:, :], scalar=gcol, in1=rstd[:, :], op0=ALU.mult, op1=ALU.mult),
                             reads=[sil, rstd, self.gsc], writes=[dst])
                        yield
                        if bi == 1:
                            for j in range(4):
                                tk = ptp[st.tpj]
                                st.tpj ^= 1
                                c.op("pe", lambda: nc.tensor.transpose(tk[:, :], dst[:, tc * 512 + j * 128:tc * 512 + (j + 1) * 128], identb), reads=[dst, self.cstb], writes=[tk])
                                yield
                                c.op("act", lambda: nc.scalar.copy(out=o.ktok[:, tc * 4 + j, :], in_=tk[:, :]), reads=[tk], writes=[o.ktok])
                                yield
                    else:
                        yield from silu_to(acc[:, :], 512, vTc[:, :], [acc], vTc)
                        for j in range(4):
                            tk = ptp[st.tpj]
                            st.tpj ^= 1
                            c.op("pe", lambda: nc.tensor.transpose(tk[:, :], vTc[:, j * 128:(j + 1) * 128], identb), reads=[vTc, self.cstb], writes=[tk])
                            yield
                            c.op("act", lambda: nc.scalar.copy(out=o.vtok[bi - 2][:, tc * 4 + j, :], in_=tk[:, :]), reads=[tk], writes=[o.vtok[bi - 2]])
                            yield

        prog = _B()

        def front(i, hp, o):
            b = VB[i]
            vh = 2 * hp + i
            for ci in range(NC_):
                par = ci % 2
                csl = slice(ci * 128, (ci + 1) * 128)
                while prog.gdone[i] < ci - 1:
                    yield
                if i == 0:
                    while prog.fdone[1] < ci:
                        yield
                    c.op("pe", lambda: nc.tensor.matmul(kkp[:, :], o.kT[:, csl], o.kT[:, csl], start=True, stop=True), reads=[o.kT], writes=[kkp])
                    yield
                    c.op("pe", lambda: nc.tensor.matmul(qkp[:, :], o.kT[:, csl], o.qT[:, csl], start=True, stop=True), reads=[o.kT, o.qT], writes=[qkp])
                    yield
                    prog.common = ci + 1
                else:
                    while prog.common < ci + 1:
                        yield
                gcol = g_all[:, ci, vh:vh + 1]
                gccol = gc_all[:, ci, vh:vh + 1]
                btcol = bt_all[:, ci, vh:vh + 1]
                nbcol = nbt_all[:, ci, vh:vh + 1]
                c.op("pool", lambda: _pool_scale(nc, b.rhsg[:, :], trif, gcol), reads=[self.cst, g_all], writes=[b.rhsg])
                yield
                c.op("pe", lambda: nc.tensor.matmul(b.gcB[:, :], onesf, b.rhsg[:, :], start=True, stop=True), reads=[self.cst, b.rhsg], writes=[b.gcB])
                yield
                c.op("dve", lambda: nc.vector.scalar_tensor_tensor(out=b.argT[:, :], in0=b.gcB[:, :], scalar=gccol, in1=negi, op0=ALU.subtract, op1=ALU.add),
                     reads=[b.gcB, gc_all, self.cst], writes=[b.argT])
                yield
                c.op("act", lambda: nc.scalar.activation(out=b.egB[:, :], in_=b.gcB[:, :], func=AF.Exp), reads=[b.gcB], writes=[b.egB])
                yield
                c.op("act", lambda: nc.scalar.activation(out=b.DT[:, :], in_=b.argT[:, :], func=AF.Exp), reads=[b.argT], writes=[b.DT])
                yield
                c.op("dve", lambda: nc.vector.scalar_tensor_tensor(out=b.tmp[:, :], in0=kkp[:, :], scalar=nbcol, in1=b.DT[:, :], op0=ALU.mult, op1=ALU.mult),
                     reads=[kkp, nbt_all, b.DT], writes=[b.tmp])
                yield
                c.op("dve", lambda: nc.vector.tensor_tensor(b.qkm[par][:, :], qkp[:, :], b.DT[:, :], ALU.mult), reads=[qkp, b.DT], writes=[b.qkm[par]])
                yield
                prog.fread[i] = ci + 1
                c.op("pool", lambda: nc.gpsimd.tensor_tensor(b.XTa[:, :], b.tmp[:, :], strictf, ALU.mult), reads=[b.tmp, self.cst], writes=[b.XTa])
                yield
                c.op("pool", lambda: nc.gpsimd.tensor_tensor(b.PTa[:, :], b.XTa[:, :], identf, ALU.add), reads=[b.XTa, self.cst], writes=[b.PTa])
                yield
                c.op("pe", lambda: nc.tensor.transpose(b.X0Tp[:, :], b.XTa[:, :], identb), reads=[b.XTa, self.cstb], writes=[b.X0Tp])
                yield
                c.op("act", lambda: nc.scalar.copy(out=b.Xa[:, :], in_=b.X0Tp[:, :]), reads=[b.X0Tp], writes=[b.Xa])
                yield
                c.op("pool", lambda: nc.gpsimd.tensor_tensor(b.qd[par][:, :], o.qT[:, csl], b.egB[:, :], ALU.mult), reads=[o.qT, b.egB], writes=[b.qd[par]])
                yield
                c.op("pool", lambda: _pool_scale(nc, b.kg[:, :], o.ktok[:, ci, :], egc_all[:, ci, vh:vh + 1]), reads=[o.ktok, egc_all], writes=[b.kg])
                yield
                c.op("pool", lambda: _pool_scale(nc, b.kd[par][:, :], o.ktok[:, ci, :], ekd_all[:, ci, vh:vh + 1]), reads=[o.ktok, ekd_all], writes=[b.kd[par]])
                yield
                Xc, XTc, PTc = b.Xa, b.XTa, b.PTa
                Xn_, XTn_, PTn_ = b.Xb, b.XTb, b.PTb
                for l in range(1, 7):
                    c.op("pe", lambda: nc.tensor.matmul(b.Xp[:, :], XTc[:, :], Xc[:, :], start=True, stop=True), reads=[XTc, Xc], writes=[b.Xp])
                    yield
                    if l < 6:
                        c.op("pe", lambda: nc.tensor.matmul(b.XTp[:, :], Xc[:, :], XTc[:, :], start=True, stop=True), reads=[XTc, Xc], writes=[b.XTp])
                        yield
                    c.op("act", lambda: nc.scalar.copy(out=Xn_[:, :], in_=b.Xp[:, :]), reads=[b.Xp], writes=[Xn_])
                    yield
                    if l < 6:
                        c.op("dve", lambda: nc.vector.tensor_copy(XTn_[:, :], b.XTp[:, :]), reads=[b.XTp], writes=[XTn_])
                        yield
                    c.group("pe", [lambda: nc.tensor.matmul(b.PTp[:, :], identb, PTc[:, :], start=True, stop=False),
                                   lambda: nc.tensor.matmul(b.PTp[:, :], Xn_[:, :], PTc[:, :], start=False, stop=True)],
                            reads=[self.cstb, PTc, Xn_], writes=[b.PTp])
                    yield
                    if l % 2:
                        c.op("act", lambda: nc.scalar.copy(out=PTn_[:, :], in_=b.PTp[:, :]), reads=[b.PTp], writes=[PTn_])
                    else:
                        c.op("dve", lambda: nc.vector.tensor_copy(PTn_[:, :], b.PTp[:, :]), reads=[b.PTp], writes=[PTn_])
                    yield
                    Xc, Xn_ = Xn_, Xc
                    XTc, XTn_ = XTn_, XTc
                    PTc, PTn_ = PTn_, PTc
                c.group("pe", [lambda: nc.tensor.matmul(b.Rp[:, 0:128], PTc[:, :], o.vtok[i][:, ci, :], start=True, stop=True),
                               lambda: nc.tensor.matmul(b.Rp[:, 128:256], PTc[:, :], b.kg[:, :], start=True, stop=True)],
                        reads=[PTc, o.vtok[i], b.kg], writes=[b.Rp])
                yield
                c.op("act", lambda: nc.scalar.activation(out=b.u0[par][:, :], in_=b.Rp[:, 0:128], func=AF.Copy, scale=btcol), reads=[b.Rp, bt_all], writes=[b.u0[par]])
                yield
                c.op("dve", lambda: nc.vector.tensor_scalar(out=b.wsb[:, :], in0=b.Rp[:, 128:256], scalar1=btcol, scalar2=None, op0=ALU.mult), reads=[b.Rp, bt_all], writes=[b.wsb])
                yield
                c.op("pe", lambda: nc.tensor.transpose(b.wTp[:, :], b.wsb[:, :], identb), reads=[b.wsb, self.cstb], writes=[b.wTp])
                yield
                c.op("act", lambda: nc.scalar.copy(out=b.wT[par][:, :], in_=b.wTp[:, :]), reads=[b.wTp], writes=[b.wT[par]])
                yield
                prog.fdone[i] = ci + 1

        def back(i, hp, o):
            b = VB[i]
            vh = 2 * hp + i
            c.op("dve", lambda: nc.vector.memset(b.S[:, :], 0.0), writes=[b.S])
            c.op("dve", lambda: nc.vector.memset(b.Sb[:, :], 0.0), writes=[b.Sb])
            yield
            for ci in range(NC_):
                par = ci % 2
                csl = slice(ci * 128, (ci + 1) * 128)
                while prog.fdone[i] < ci + 1:
                    yield
                c.op("pe", lambda: nc.tensor.matmul(b.wSp[:, :], b.wT[par][:, :], b.Sb[:, :], start=True, stop=True), reads=[b.wT[par], b.Sb], writes=[b.wSp])
                yield
                c.op("dve", lambda: nc.vector.tensor_tensor(b.ub[:, :], b.u0[par][:, :], b.wSp[:, :], ALU.subtract), reads=[b.u0[par], b.wSp], writes=[b.ub])
                yield
                c.group("pe", [lambda: nc.tensor.matmul(b.op_[:, :], b.qd[par][:, :], b.Sb[:, :], start=True, stop=False),
                               lambda: nc.tensor.matmul(b.op_[:, :], b.qkm[par][:, :], b.ub[:, :], start=False, stop=True)],
                        reads=[b.qd[par], b.Sb, b.qkm[par], b.ub], writes=[b.op_])
                yield
                c.op("pe", lambda: nc.tensor.matmul(Snp[i][:, :], b.kd[par][:, :], b.ub[:, :], start=True, stop=True), reads=[b.kd[par], b.ub], writes=[Snp[i]])
                yield
                c.op("dve", lambda: nc.vector.scalar_tensor_tensor(out=b.S[:, :], in0=b.S[:, :], scalar=ecd_all[:, ci, vh:vh + 1], in1=Snp[i][:, :], op0=ALU.mult, op1=ALU.add),
                     reads=[b.S, ecd_all, Snp[i]], writes=[b.S])
                yield
                c.op("pool", lambda: nc.gpsimd.tensor_copy(b.Sb[:, :], b.S[:, :]), reads=[b.S], writes=[b.Sb])
                yield
                c.op("act", lambda: nc.scalar.activation(out=b.junk[:, :], in_=b.op_[:, :], func=AF.Square, accum_out=b.oss[:, 0:1]), reads=[b.op_], writes=[b.junk, b.oss], multi=True)
                yield
                c.op("act", lambda: nc.scalar.activation(out=b.ors[:, :], in_=b.oss[:, :], func=AF.Ln, bias=self.epsc[:, 0:1], scale=1.0 / 128.0),
                     reads=[b.oss, self.epsc], writes=[b.ors])
                yield
                c.op("act", lambda: nc.scalar.activation(out=b.ors[:, :], in_=b.ors[:, :], func=AF.Exp, scale=-0.5), reads=[b.ors], writes=[b.ors])
                yield
                c.op("act", lambda: nc.scalar.activation(out=b.on[:, :], in_=b.op_[:, :], func=AF.Copy, scale=b.ors[:, 0:1]), reads=[b.op_, b.ors], writes=[b.on])
                yield
                c.op("pe", lambda: nc.tensor.transpose(b.oTp[:, :], b.on[:, :], identb), reads=[b.on, self.cstb], writes=[b.oTp])
                yield
                yv = b.y[par]
                c.op("dve", lambda: nc.vector.scalar_tensor_tensor(out=yv[:, :], in0=b.oTp[:, :], scalar=self.pv[:, PV_DNO:PV_DNO + 1], in1=o.sz[i][:, csl], op0=ALU.mult, op1=ALU.mult),
                     reads=[b.oTp, self.pv, o.sz[i]], writes=[yv])
                yield
                c.dma("sp", self.yT_d[vh, :, csl], yv[:, :], reads=[yv])
                yield
                prog.gdone[i] = ci + 1

        run_streams([prologue(pairs[0], OUT[0])])
        for n, hp in enumerate(pairs):
            o = OUT[n % 2]
            prog.fdone = [0, 0]
            prog.gdone = [0, 0]
            prog.fread = [0, 0]
            prog.common = 0
            gens = [front(0, hp, o), front(1, hp, o), back(0, hp, o), back(1, hp, o)]
            if n + 1 < len(pairs):
                gens.append(prologue(pairs[n + 1], OUT[(n + 1) % 2]))
            run_streams(gens)
        if len(pairs) < 12:
            with Scope(c) as sz_:
                zt = sz_.sb("zt", [128, T], BF16)
                c.op("dve", lambda: nc.vector.memset(zt[:, :], 0.0), writes=[zt])
                for h in range(NH):
                    if h // 2 not in pairs:
                        c.dma("sp", self.yT_d[h, :, :], zt[:, :], reads=[zt])
                c.barrier()
        c.barrier()


Layers.dn_heads2 = dn_heads2


def dn_heads3(self, hT, W, psb):
    c = self.ctx
    nc = self.nc
    pairs = self.dbg.get("pairs")
    pairs = list(range(self.nh // 2)) if pairs is None else pairs
    NC_ = T // 128
    with Scope(c) as sc:
        onesf = self.cst[:, C_ONES:C_ONES + 128]
        trif = self.cst[:, C_TRI:C_TRI + 128]
        negi = self.cst[:, C_NEGI:C_NEGI + 128]
        strictf = self.cst[:, C_STRICT:C_STRICT + 128]
        identf = self.cst[:, C_ID:C_ID + 128]
        identb = self.cstb[:, C_ID:C_ID + 128]
        g_all = sc.sb("g_all", [128, NC_, 24], F32)
        bt_all = sc.sb("bt_all", [128, NC_, 24], F32)
        nbt_all = sc.sb("nbt_all", [128, NC_, 24], F32)
        gc_all = sc.sb("gc_all", [128, NC_, 24], F32)
        egc_all = sc.sb("egc_all", [128, NC_, 24], F32)
        ekd_all = sc.sb("ekd_all", [128, NC_, 24], F32)
        ecd_all = sc.sb("ecd_all", [128, NC_, 24], F32)
        bankA = sc.ps("bankA", [128, 512], F32)
        bankB = [sc.ps("bankB%d" % i, [128, 512], F32) for i in range(2)]
        bankC = [sc.ps("bankC%d" % i, [128, 512], F32) for i in range(2)]
        bankE = sc.ps("bankE", [128, 1024], BF16)
        with Scope(c) as s0:
            negA = s0.sb("negA", [128, 24], F32)
            wabf = s0.sb("wabf", [128, 768], F32)
            wabb = s0.sb("wabb", [128, 768], BF16)
            gtmp = s0.sb("gtmp", [128, 48], F32)
            c.dma("sp", wabf[:, :], W["wab"][:, :], writes=[wabf])
            c.op("dve", lambda: nc.vector.tensor_copy(wabb[:, :], wabf[:, :]), reads=[wabf], writes=[wabb])
            c.op("act", lambda: nc.scalar.activation(out=negA[:, :], in_=self.pv[:, PV_ALOG:PV_ALOG + 24], func=AF.Exp), reads=[self.pv], writes=[negA])
            c.op("dve", lambda: nc.vector.tensor_scalar(out=negA[:, :], in0=negA[:, :], scalar1=-1.0, scalar2=None, op0=ALU.mult), reads=[negA], writes=[negA])
            abp = view("abp", bankA[:, 0:48], bankA)
            gcp = view("gcp", bankB[0][:, 0:24], bankB[0])
            glp = view("glp", bankC[0][:, 0:24], bankC[0])
            for tt in range(NC_):
                c.group("pe", [(lambda kt=kt: nc.tensor.matmul(abp[:, :], hT[:, kt, tt * 128:(tt + 1) * 128], wabb[:, kt * 48:(kt + 1) * 48],
                                                              start=(kt == 0), stop=(kt == KT - 1))) for kt in range(KT)],
                        reads=[hT, wabb], writes=[abp])
                c.op("dve", lambda: nc.vector.tensor_tensor(gtmp[:, 0:24], abp[:, 0:24], self.pv[:, PV_DTB:PV_DTB + 24], ALU.add), reads=[abp, self.pv], writes=[gtmp])
                c.op("dve", lambda: nc.vector.tensor_scalar(out=gtmp[:, 24:48], in0=abp[:, 24:48], scalar1=-1.0, scalar2=None, op0=ALU.mult), reads=[abp], writes=[gtmp])
                c.op("act", lambda: nc.scalar.activation(out=gtmp[:, :], in_=gtmp[:, :], func=AF.Exp), reads=[gtmp], writes=[gtmp])
                c.op("act", lambda: nc.scalar.activation(out=gtmp[:, :], in_=gtmp[:, :], func=AF.Ln, bias=self.onec[:, 0:1], scale=1.0), reads=[gtmp, self.onec], writes=[gtmp])
                c.op("dve", lambda: nc.vector.tensor_tensor(g_all[:, tt, :], gtmp[:, 0:24], negA[:, :], ALU.mult), reads=[gtmp, negA], writes=[g_all])
                c.op("act", lambda: nc.scalar.activation(out=bt_all[:, tt, :], in_=gtmp[:, 24:48], func=AF.Exp, scale=-1.0), reads=[gtmp], writes=[bt_all])
                c.op("pool", lambda: _pool_scale(nc, nbt_all[:, tt, :], bt_all[:, tt, :], -1.0), reads=[bt_all], writes=[nbt_all])
                c.op("pe", lambda: nc.tensor.matmul(gcp[:, :], trif, g_all[:, tt, :], start=True, stop=True), reads=[self.cst, g_all], writes=[gcp])
                c.op("pe", lambda: nc.tensor.matmul(glp[:, :], onesf, g_all[:, tt, :], start=True, stop=True), reads=[self.cst, g_all], writes=[glp])
                c.op("act", lambda: nc.scalar.copy(out=gc_all[:, tt, :], in_=gcp[:, :]), reads=[gcp], writes=[gc_all])
                c.op("act", lambda: nc.scalar.activation(out=egc_all[:, tt, :], in_=gcp[:, :], func=AF.Exp), reads=[gcp], writes=[egc_all])
                c.op("act", lambda: nc.scalar.activation(out=ecd_all[:, tt, :], in_=glp[:, :], func=AF.Exp), reads=[glp], writes=[ecd_all])
                c.op("dve", lambda: nc.vector.tensor_tensor(ekd_all[:, tt, :], glp[:, :], gc_all[:, tt, :], ALU.subtract), reads=[glp, gc_all], writes=[ekd_all])
                c.op("act", lambda: nc.scalar.activation(out=ekd_all[:, tt, :], in_=ekd_all[:, tt, :], func=AF.Exp), reads=[ekd_all], writes=[ekd_all])
            c.barrier()
        raw = sc.sb("raw", [128, T + 3], F32)
        acc = sc.sb("acc", [128, 512], F32)
        stmp = sc.sb("stmp", [128, 512], F32)
        sil = acc
        sqb = stmp
        rstd = sc.sb("rstd", [128, 512], F32)
        vTc = sc.sb("vTc", [128, 512], BF16)
        c.op("dve", lambda: nc.vector.memset(raw[:, 0:3], 0.0), writes=[raw])
        OUT = []
        for i in range(2):
            o = _B()
            o.qT = sc.sb("qT%d" % i, [128, T], BF16)
            o.kT = sc.sb("kT%d" % i, [128, T], BF16)
            o.ktok = sc.sb("ktok%d" % i, [128, NC_, 128], BF16)
            o.vtok = [sc.sb("vtok%d_%d" % (i, j), [128, NC_, 128], BF16) for j in range(2)]
            o.sz = [sc.sb("sz%d_%d" % (i, j), [128, T], BF16) for j in range(2)]
            OUT.append(o)
        kkp = view("kkp", bankA[:, 256:384], bankA)
        qkp = view("qkp", bankA[:, 384:512], bankA)
        Snp = view("Snp", bankA[:, 0:256], bankA)
        tp = [view("tp%d" % i, bankE[:, i * 128:(i + 1) * 128], bankE) for i in range(8)]
        XXp = view("XXp", bankB[0][:, :], bankB[0])
        PTp = view("PTp", bankB[1][:, 0:256], bankB[1])
        gcB = view("gcB", bankB[1][:, 256:512], bankB[1])
        Rp = view("Rp", bankC[0][:, :], bankC[0])
        wSp = view("wSp", bankC[1][:, 0:256], bankC[1])
        op_ = view("op", bankC[1][:, 256:512], bankC[1])
        X0Tp = view("X0Tp", bankE[:, 0:256], bankE)
        wTp = view("wTp", bankE[:, 256:512], bankE)
        oTp = view("oTp", bankE[:, 512:768], bankE)
        rhsg = sc.sb("rhsg", [128, 256], F32)
        argT = sc.sb("argT", [128, 256], F32)
        DT = sc.sb("DT", [128, 256], F32)
        egB = sc.sb("egB", [128, 256], F32)
        tmp = sc.sb("tmp", [128, 256], F32)
        XX = [sc.sb("XX%d" % j, [128, 512], F32) for j in range(2)]
        PT = [sc.sb("PT%d" % j, [128, 256], F32) for j in range(2)]
        PTb16 = sc.sb("PTb16", [128, 256], BF16)
        kg = sc.sb("kg", [128, 256], BF16)
        wsb = sc.sb("wsb", [128, 256], BF16)
        ub = sc.sb("ub", [128, 256], BF16)
        on = sc.sb("on", [128, 256], BF16)
        junk = sc.sb("junk", [128, 128], BF16)
        S = sc.sb("S", [128, 256], F32)
        Sb = sc.sb("Sb", [128, 256], BF16)
        oss = sc.sb("oss", [128, 2], F32)
        ors = sc.sb("ors", [128, 2], F32)
        u0 = [sc.sb("u0_%d" % j, [128, 256], F32) for j in range(2)]
        wT = [sc.sb("wT_%d" % j, [128, 256], BF16) for j in range(2)]
        qd = [sc.sb("qd_%d" % j, [128, 256], BF16) for j in range(2)]
        qkm = [sc.sb("qkm_%d" % j, [128, 256], BF16) for j in range(2)]
        kd = [sc.sb("kd_%d" % j, [128, 256], BF16) for j in range(2)]
        yv = [sc.sb("yv_%d" % j, [128, 256], BF16) for j in range(2)]
        ptp = [tp[6], tp[7]]
        st = _B()
        st.pj = 0
        st.tpj = 0

        def silu_to(src_ap, n, dst_ap, reads, dst_tile):
            c.op("act", lambda: nc.scalar.activation(out=stmp[:, 0:n], in_=src_ap, func=AF.Exp, scale=-1.0), reads=reads, writes=[stmp])
            yield
            c.op("act", lambda: nc.scalar.activation(out=stmp[:, 0:n], in_=stmp[:, 0:n], func=AF.Ln, bias=self.onec[:, 0:1], scale=1.0), reads=[stmp, self.onec], writes=[stmp])
            yield
            c.op("act", lambda: nc.scalar.activation(out=stmp[:, 0:n], in_=stmp[:, 0:n], func=AF.Exp, scale=-1.0), reads=[stmp], writes=[stmp])
            yield
            c.op("dve", lambda: nc.vector.tensor_tensor(dst_ap, src_ap, stmp[:, 0:n], ALU.mult), reads=reads + [stmp], writes=[dst_tile])
            yield

        def prologue(hp, o):
            idxs = [W["q_idx"][hp], W["k_idx"][hp], W["v_idx"][2 * hp], W["v_idx"][2 * hp + 1], W["z_idx"][2 * hp], W["z_idx"][2 * hp + 1]]
            cbis = [hp, 12 + hp, 24 + 2 * hp, 24 + 2 * hp + 1]
            slot = self.w_fetch(W["w"][idxs[0], :, :])
            for bi in range(6):
                wt = self.w_cast(slot)
                if bi + 1 < 6:
                    slot = self.w_fetch(W["w"][idxs[bi + 1], :, :])
                yield
                for tc in range(4):
                    sl = slice(tc * 512, (tc + 1) * 512)
                    p = psb[st.pj]
                    st.pj ^= 1
                    nparts = KT // PSL
                    for part in range(nparts):
                        c.group("pe", [(lambda kt=kt: nc.tensor.matmul(p[:, :], wt[:, kt * 128:(kt + 1) * 128], hT[:, kt, sl],
                                                                      start=(kt == 0), stop=(kt == KT - 1))) for kt in range(part * PSL, part * PSL + PSL)],
                                reads=[wt, hT], writes=[p] if part in (0, nparts - 1) else [])
                        yield
                    if bi >= 4:
                        yield from silu_to(p[:, :], 512, o.sz[bi - 4][:, sl], [p], o.sz[bi - 4])
                        continue
                    c.op("act", lambda: nc.scalar.copy(out=raw[:, 3 + tc * 512:3 + (tc + 1) * 512], in_=p[:, :]), reads=[p], writes=[raw])
                    yield
                    wc = [self.pv[:, PV_CONV + cbis[bi] * 4 + k:PV_CONV + cbis[bi] * 4 + k + 1] for k in range(4)]
                    c.op("dve", lambda: nc.vector.tensor_scalar(out=acc[:, :], in0=raw[:, 3 + tc * 512:3 + (tc + 1) * 512], scalar1=wc[3], scalar2=None, op0=ALU.mult),
                         reads=[raw, self.pv], writes=[acc])
                    yield
                    for k in (2, 1, 0):
                        c.op("dve", lambda: nc.vector.scalar_tensor_tensor(out=acc[:, :], in0=raw[:, k + tc * 512:k + (tc + 1) * 512], scalar=wc[k], in1=acc[:, :],
                                                                           op0=ALU.mult, op1=ALU.add), reads=[raw, acc, self.pv], writes=[acc])
                        yield
                    if bi < 2:
                        yield from silu_to(acc[:, :], 512, sil[:, :], [acc], sil)
                        c.op("act", lambda: nc.scalar.activation(out=sqb[:, :], in_=sil[:, :], func=AF.Square), reads=[sil], writes=[sqb])
                        yield
                        p2 = psb[st.pj]
                        st.pj ^= 1
                        c.op("pe", lambda: nc.tensor.matmul(p2[:, :], onesf, sqb[:, :], start=True, stop=True), reads=[self.cst, sqb], writes=[p2])
                        yield
                        c.op("act", lambda: nc.scalar.activation(out=rstd[:, :], in_=p2[:, :], func=AF.Ln, bias=self.epsc[:, 0:1], scale=1.0), reads=[p2, self.epsc], writes=[rstd])
                        yield
                        c.op("act", lambda: nc.scalar.activation(out=rstd[:, :], in_=rstd[:, :], func=AF.Exp, scale=-0.5), reads=[rstd], writes=[rstd])
                        yield
                        dst = o.qT if bi == 0 else o.kT
                        gcol = self.gsc[:, 5:6] if bi == 0 else self.gsc[:, 6:7]
                        c.op("dve", lambda: nc.vector.scalar_tensor_tensor(out=dst[:, sl], in0=sil[:, :], scalar=gcol, in1=rstd[:, :], op0=ALU.mult, op1=ALU.mult),
                             reads=[sil, rstd, self.gsc], writes=[dst])
                        yield
                        if bi == 1:
                            for j in range(4):
                                tk = ptp[st.tpj]
                                st.tpj ^= 1
                                c.op("pe", lambda: nc.tensor.transpose(tk[:, :], dst[:, tc * 512 + j * 128:tc * 512 + (j + 1) * 128], identb), reads=[dst, self.cstb], writes=[tk])
                                yield
                                c.op("act", lambda: nc.scalar.copy(out=o.ktok[:, tc * 4 + j, :], in_=tk[:, :]), reads=[tk], writes=[o.ktok])
                                yield
                    else:
                        yield from silu_to(acc[:, :], 512, vTc[:, :], [acc], vTc)
                        for j in range(4):
                            tk = ptp[st.tpj]
                            st.tpj ^= 1
                            c.op("pe", lambda: nc.tensor.transpose(tk[:, :], vTc[:, j * 128:(j + 1) * 128], identb), reads=[vTc, self.cstb], writes=[tk])
                            yield
                            c.op("act", lambda: nc.scalar.copy(out=o.vtok[bi - 2][:, tc * 4 + j, :], in_=tk[:, :]), reads=[tk], writes=[o.vtok[bi - 2]])
                            yield

        prog = _B()
        H = (slice(0, 128), slice(128, 256))

        def front(hp, o):
            for ci in range(NC_):
                par = ci % 2
                csl = slice(ci * 128, (ci + 1) * 128)
                while prog.gdone < ci - 1:
                    yield
                vh = [2 * hp, 2 * hp + 1]
                c.op("pe", lambda: nc.tensor.matmul(kkp[:, :], o.kT[:, csl], o.kT[:, csl], start=True, stop=True), reads=[o.kT], writes=[kkp])
                yield
                c.op("pe", lambda: nc.tensor.matmul(qkp[:, :], o.kT[:, csl], o.qT[:, csl], start=True, stop=True), reads=[o.kT, o.qT], writes=[qkp])
                yield
                for h in range(2):
                    c.op("pool", lambda: _pool_scale(nc, rhsg[:, H[h]], trif, g_all[:, ci, vh[h]:vh[h] + 1]), reads=[self.cst, g_all], writes=[rhsg])
                    yield
                c.op("pe", lambda: nc.tensor.matmul(gcB[:, :], onesf, rhsg[:, :], start=True, stop=True), reads=[self.cst, rhsg], writes=[gcB])
                yield
                for h in range(2):
                    c.op("dve", lambda: nc.vector.scalar_tensor_tensor(out=argT[:, H[h]], in0=gcB[:, H[h]], scalar=gc_all[:, ci, vh[h]:vh[h] + 1], in1=negi,
                                                                       op0=ALU.subtract, op1=ALU.add), reads=[gcB, gc_all, self.cst], writes=[argT])
                    yield
                c.op("act", lambda: nc.scalar.activation(out=egB[:, :], in_=gcB[:, :], func=AF.Exp), reads=[gcB], writes=[egB])
                yield
                c.op("act", lambda: nc.scalar.activation(out=DT[:, :], in_=argT[:, :], func=AF.Exp), reads=[argT], writes=[DT])
                yield
                cur, nxt = XX[0], XX[1]
                PTc, PTn = PT[0], PT[1]
                for h in range(2):
                    c.op("dve", lambda: nc.vector.tensor_tensor(qkm[par][:, H[h]], qkp[:, :], DT[:, H[h]], ALU.mult), reads=[qkp, DT], writes=[qkm[par]])
                    yield
                    c.op("pool", lambda: nc.gpsimd.tensor_tensor(tmp[:, H[h]], DT[:, H[h]], strictf, ALU.mult), reads=[DT, self.cst], writes=[tmp])
                    yield
                    c.op("dve", lambda: nc.vector.scalar_tensor_tensor(out=cur[:, 256 + h * 128:256 + (h + 1) * 128].bitcast(F32R), in0=kkp[:, :],
                                                                       scalar=nbt_all[:, ci, vh[h]:vh[h] + 1], in1=tmp[:, H[h]], op0=ALU.mult, op1=ALU.mult),
                         reads=[kkp, nbt_all, tmp], writes=[cur])
                    yield
                    c.op("dve", lambda: nc.vector.tensor_tensor(PTc[:, H[h]].bitcast(F32R), cur[:, 256 + h * 128:256 + (h + 1) * 128], identf, ALU.add),
                         reads=[cur, self.cst], writes=[PTc])
                    yield
                c.group("pe", [(lambda h=h: nc.tensor.transpose(XXp[:, H[h]], cur[:, 256 + h * 128:256 + (h + 1) * 128], identf)) for h in range(2)],
                        reads=[cur, self.cst], writes=[XXp])
                yield
                c.op("act", lambda: nc.scalar.copy(out=cur[:, 0:256].bitcast(F32R), in_=XXp[:, 0:256]), reads=[XXp], writes=[cur])
                yield
                for h in range(2):
                    c.op("pool", lambda: nc.gpsimd.tensor_tensor(qd[par][:, H[h]], o.qT[:, csl], egB[:, H[h]], ALU.mult), reads=[o.qT, egB], writes=[qd[par]])
                    yield
                    c.op("pool", lambda: _pool_scale(nc, kg[:, H[h]], o.ktok[:, ci, :], egc_all[:, ci, vh[h]:vh[h] + 1]), reads=[o.ktok, egc_all], writes=[kg])
                    yield
                    c.op("pool", lambda: _pool_scale(nc, kd[par][:, H[h]], o.ktok[:, ci, :], ekd_all[:, ci, vh[h]:vh[h] + 1]), reads=[o.ktok, ekd_all], writes=[kd[par]])
                    yield
                for l in range(1, 7):
                    nx = 2 if l < 6 else 1
                    mm = []
                    for h in range(2):
                        mm.append(lambda h=h: nc.tensor.matmul(XXp[:, H[h]], cur[:, 256 + h * 128:256 + (h + 1) * 128].bitcast(F32R), cur[:, H[h]].bitcast(F32R), start=True, stop=True))
                    if l < 6:
                        for h in range(2):
                            mm.append(lambda h=h: nc.tensor.matmul(XXp[:, 256 + h * 128:256 + (h + 1) * 128], cur[:, H[h]].bitcast(F32R),
                                                                   cur[:, 256 + h * 128:256 + (h + 1) * 128].bitcast(F32R), start=True, stop=True))
                    c.group("pe", mm, reads=[cur], writes=[XXp])
                    yield
                    w_ = 256 * nx
                    c.op("act", lambda: nc.scalar.copy(out=nxt[:, 0:w_].bitcast(F32R), in_=XXp[:, 0:w_]), reads=[XXp], writes=[nxt])
                    yield
                    c.group("pe", [(lambda h=h: nc.tensor.matmul(PTp[:, H[h]], nxt[:, H[h]].bitcast(F32R), PTc[:, H[h]].bitcast(F32R), start=True, stop=True)) for h in range(2)],
                            reads=[PTc, nxt], writes=[PTp])
                    yield
                    if l < 6:
                        c.op("dve", lambda: nc.vector.tensor_tensor(PTn[:, :].bitcast(F32R), PTp[:, :], PTc[:, :], ALU.add), reads=[PTp, PTc], writes=[PTn])
                    else:
                        c.op("dve", lambda: nc.vector.tensor_tensor(PTb16[:, :], PTp[:, :], PTc[:, :], ALU.add), reads=[PTp, PTc], writes=[PTb16])
                    yield
                    cur, nxt = nxt, cur
                    PTc, PTn = PTn, PTc
                PTc = PTb16
                mm = []
                for h in range(2):
                    mm.append(lambda h=h: nc.tensor.matmul(Rp[:, 256 * h:256 * h + 128], PTc[:, H[h]], o.vtok[h][:, ci, :], start=True, stop=True))
                    mm.append(lambda h=h: nc.tensor.matmul(Rp[:, 256 * h + 128:256 * h + 256], PTc[:, H[h]], kg[:, H[h]], start=True, stop=True))
                c.group("pe", mm, reads=[PTc, o.vtok[0], o.vtok[1], kg], writes=[Rp])
                yield
                for h in range(2):
                    btcol = bt_all[:, ci, vh[h]:vh[h] + 1]
                    c.op("act", lambda: nc.scalar.activation(out=u0[par][:, H[h]], in_=Rp[:, 256 * h:256 * h + 128], func=AF.Copy, scale=btcol), reads=[Rp, bt_all], writes=[u0[par]])
                    yield
                    c.op("dve", lambda: nc.vector.tensor_scalar(out=wsb[:, H[h]], in0=Rp[:, 256 * h + 128:256 * h + 256], scalar1=btcol, scalar2=None, op0=ALU.mult),
                         reads=[Rp, bt_all], writes=[wsb])
                    yield
                c.group("pe", [(lambda h=h: nc.tensor.transpose(wTp[:, H[h]], wsb[:, H[h]], identb)) for h in range(2)], reads=[wsb, self.cstb], writes=[wTp])
                yield
                c.op("act", lambda: nc.scalar.copy(out=wT[par][:, :], in_=wTp[:, :]), reads=[wTp], writes=[wT[par]])
                yield
                prog.fdone = ci + 1

        def back(hp, o):
            vh = [2 * hp, 2 * hp + 1]
            c.op("dve", lambda: nc.vector.memset(S[:, :], 0.0), writes=[S])
            c.op("dve", lambda: nc.vector.memset(Sb[:, :], 0.0), writes=[Sb])
            yield
            for ci in range(NC_):
                par = ci % 2
                csl = slice(ci * 128, (ci + 1) * 128)
                while prog.fdone < ci + 1:
                    yield
                c.group("pe", [(lambda h=h: nc.tensor.matmul(wSp[:, H[h]], wT[par][:, H[h]], Sb[:, H[h]], start=True, stop=True)) for h in range(2)],
                        reads=[wT[par], Sb], writes=[wSp])
                yield
                c.op("dve", lambda: nc.vector.tensor_tensor(ub[:, :], u0[par][:, :], wSp[:, :], ALU.subtract), reads=[u0[par], wSp], writes=[ub])
                yield
                mm = []
                for h in range(2):
                    mm.append(lambda h=h: nc.tensor.matmul(op_[:, H[h]], qd[par][:, H[h]], Sb[:, H[h]], start=True, stop=False))
                    mm.append(lambda h=h: nc.tensor.matmul(op_[:, H[h]], qkm[par][:, H[h]], ub[:, H[h]], start=False, stop=True))
                c.group("pe", mm, reads=[qd[par], Sb, qkm[par], ub], writes=[op_])
                yield
                c.group("pe", [(lambda h=h: nc.tensor.matmul(Snp[:, H[h]], kd[par][:, H[h]], ub[:, H[h]], start=True, stop=True)) for h in range(2)],
                        reads=[kd[par], ub], writes=[Snp])
                yield
                for h in range(2):
                    c.op("dve", lambda: nc.vector.scalar_tensor_tensor(out=S[:, H[h]], in0=S[:, H[h]], scalar=ecd_all[:, ci, vh[h]:vh[h] + 1], in1=Snp[:, H[h]],
                                                                       op0=ALU.mult, op1=ALU.add), reads=[S, ecd_all, Snp], writes=[S])
                    yield
                c.op("pool", lambda: nc.gpsimd.tensor_copy(Sb[:, :], S[:, :]), reads=[S], writes=[Sb])
                yield
                for h in range(2):
                    c.op("act", lambda: nc.scalar.activation(out=junk[:, :], in_=op_[:, H[h]], func=AF.Square, accum_out=oss[:, h:h + 1]), reads=[op_], writes=[junk, oss], multi=True)
                    yield
                c.op("act", lambda: nc.scalar.activation(out=ors[:, :], in_=oss[:, :], func=AF.Ln, bias=self.epsc[:, 0:1], scale=1.0 / 128.0), reads=[oss, self.epsc], writes=[ors])
                yield
                c.op("act", lambda: nc.scalar.activation(out=ors[:, :], in_=ors[:, :], func=AF.Exp, scale=-0.5), reads=[ors], writes=[ors])
                yield
                for h in range(2):
                    c.op("act", lambda: nc.scalar.activation(out=on[:, H[h]], in_=op_[:, H[h]], func=AF.Copy, scale=ors[:, h:h + 1]), reads=[op_, ors], writes=[on])
                    yield
                c.group("pe", [(lambda h=h: nc.tensor.transpose(oTp[:, H[h]], on[:, H[h]], identb)) for h in range(2)], reads=[on, self.cstb], writes=[oTp])
                yield
                y_ = yv[par]
                for h in range(2):
                    c.op("dve", lambda: nc.vector.scalar_tensor_tensor(out=y_[:, H[h]], in0=oTp[:, H[h]], scalar=self.pv[:, PV_DNO:PV_DNO + 1], in1=o.sz[h][:, csl],
                                                                       op0=ALU.mult, op1=ALU.mult), reads=[oTp, self.pv, o.sz[h]], writes=[y_])
                    yield
                c.dma("sp", self.yT_d[2 * hp:2 * hp + 2, :, csl].rearrange("i p t -> p i t"), y_[:, :].rearrange("p (i t) -> p i t", i=2), reads=[y_])
                yield
                prog.gdone = ci + 1

        run_streams([prologue(pairs[0], OUT[0])])
        for n, hp in enumerate(pairs):
            o = OUT[n % 2]
            prog.fdone = 0
            prog.gdone = 0
            gens = [front(hp, o), back(hp, o)]
            wts = [DN_W[0], DN_W[1]]
            if n + 1 < len(pairs):
                gens.append(prologue(pairs[n + 1], OUT[(n + 1) % 2]))
                wts.append(DN_W[2])
            run_streams(gens, wts)
        if len(pairs) < self.nh // 2:
            with Scope(c) as sz_:
                zt = sz_.sb("zt", [128, 512], BF16)
                c.op("dve", lambda: nc.vector.memset(zt[:, :], 0.0), writes=[zt])
                for h in range(self.nh):
                    if h // 2 not in pairs:
                        for q4 in range(4):
                            c.dma("sp", self.yT_d[h, :, q4 * 512:(q4 + 1) * 512], zt[:, :], reads=[zt])
                c.barrier()
        c.barrier()


Layers.dn_heads3 = dn_heads3


def xa_heads2(self, li, hT, w_d, xq_idx, z_idx, kxT, vx, psb):
    c = self.ctx
    nc = self.nc
    with Scope(c) as sc:
        banks = list(psb) + [sc.ps("xbk%d" % i, [128, 512], F32) for i in range(6)]
        onesb = self.cstb[:, C_ONES:C_ONES + 128]
        onesf = self.cst[:, C_ONES:C_ONES + 128]

        def stream(s, hs):
            for _ in range(XAD * s):
                yield
            b0, b1, b2, b3 = banks[4 * s:4 * s + 4]
            acc_b = [b0, b1]
            wst = self.wst[s]
            wts = [sc.sb("xw%d_%d" % (s, i), [128, 2048], BF16) for i in range(4)]
            qraw = [sc.sb("xqr%d_%d" % (s, i), [128, 512], F32) for i in range(2)]
            sq = [sc.sb("xsq%d_%d" % (s, i), [128, 512], F32) for i in range(2)]
            qn = [sc.sb("xqn%d_%d" % (s, i), [128, 512], BF16) for i in range(2)]
            sz = [sc.sb("xsz%d_%d" % (s, i), [128, 512], BF16) for i in range(2)]
            yb = [sc.sb("xyb%d_%d" % (s, i), [128, 512], BF16) for i in range(2)]
            pe_ = [sc.sb("xpe%d_%d" % (s, i), [128, 512], BF16) for i in range(2)]
            stmp = sc.sb("xst%d" % s, [128, 512], F32)
            rstd = sc.sb("xrs%d" % s, [128, 512], F32)
            rden = sc.sb("xrd%d" % s, [128, 512], F32)
            t1 = sc.sb("xt1%d" % s, [128, 512], F32)
            pj = 0
            for a in hs:
                idxs = [xq_idx[a * 2], xq_idx[a * 2 + 1], z_idx[a * 2], z_idx[a * 2 + 1]]
                for i in range(4):
                    c.dma("sp", wst[:, :], w_d[idxs[i], :, :], writes=[wst])
                    c.op("pool", lambda: nc.gpsimd.tensor_copy(wts[i][:, :], wst[:, :]), reads=[wst], writes=[wts[i]])
                    yield
                for tc in range(4):
                    sl = slice(tc * 512, (tc + 1) * 512)
                    for i in range(4):
                        p = acc_b[pj]
                        pj ^= 1
                        for part in range(4):
                            c.group("pe", [(lambda kt=kt: nc.tensor.matmul(p[:, :], wts[i][:, kt * 128:(kt + 1) * 128], hT[:, kt, sl],
                                                                          start=(kt == 0), stop=(kt == KT - 1))) for kt in range(part * 4, part * 4 + 4)],
                                    reads=[wts[i], hT], writes=[p] if part in (0, 3) else [], lhs=[wts[i]])
                            yield
                        if i < 2:
                            c.op("act", lambda: nc.scalar.copy(out=qraw[i][:, :], in_=p[:, :]), reads=[p], writes=[qraw[i]])
                            yield
                            c.op("act", lambda: nc.scalar.activation(out=sq[i][:, :], in_=qraw[i][:, :], func=AF.Square), reads=[qraw[i]], writes=[sq[i]])
                            yield
                        else:
                            j = i - 2
                            c.op("act", lambda: nc.scalar.activation(out=stmp[:, :], in_=p[:, :], func=AF.Exp, scale=-1.0), reads=[p], writes=[stmp])
                            yield
                            c.op("act", lambda: nc.scalar.activation(out=stmp[:, :], in_=stmp[:, :], func=AF.Ln, bias=self.onec[:, 0:1], scale=1.0), reads=[stmp, self.onec], writes=[stmp])
                            yield
                            c.op("act", lambda: nc.scalar.activation(out=stmp[:, :], in_=stmp[:, :], func=AF.Exp, scale=-1.0), reads=[stmp], writes=[stmp])
                            yield
                            c.op("dve", lambda: nc.vector.tensor_tensor(sz[j][:, :], p[:, :], stmp[:, :], ALU.mult), reads=[p, stmp], writes=[sz[j]])
                            yield
                    c.group("pe", [(lambda j=j: nc.tensor.matmul(b2[:, :], onesf, sq[j][:, :], start=(j == 0), stop=(j == 1))) for j in range(2)],
                            reads=[self.cst, sq[0], sq[1]], writes=[b2])
                    yield
                    c.op("act", lambda: nc.scalar.activation(out=rstd[:, :], in_=b2[:, :], func=AF.Ln, bias=self.epsc[:, 0:1], scale=1.0 / 256.0), reads=[b2, self.epsc], writes=[rstd])
                    yield
                    c.op("act", lambda: nc.scalar.activation(out=rstd[:, :], in_=rstd[:, :], func=AF.Exp, scale=-0.5), reads=[rstd], writes=[rstd])
                    yield
                    for j in range(2):
                        gcol = self.gsc[:, GS_XQ + li * 2 + j:GS_XQ + li * 2 + j + 1]
                        c.op("dve", lambda: nc.vector.scalar_tensor_tensor(out=qn[j][:, :], in0=qraw[j][:, :], scalar=gcol, in1=rstd[:, :], op0=ALU.mult, op1=ALU.mult),
                             reads=[qraw[j], rstd, self.gsc], writes=[qn[j]])
                        yield
                    for mt in range(2):
                        sp_ = acc_b[mt]
                        c.group("pe", [(lambda j=j: nc.tensor.matmul(sp_[:, :], kxT[:, a * 2 + j, mt * 128:(mt + 1) * 128], qn[j][:, :],
                                                                    start=(j == 0), stop=(j == 1))) for j in range(2)],
                                reads=[kxT, qn[0], qn[1]], writes=[sp_], lhs=[kxT])
                        yield
                        c.op("act", lambda: nc.scalar.activation(out=pe_[mt][:, :], in_=sp_[:, :], func=AF.Exp), reads=[sp_], writes=[pe_[mt]])
                        yield
                    c.group("pe", [(lambda mt=mt: nc.tensor.matmul(b2[:, :], onesb, pe_[mt][:, :], start=(mt == 0), stop=(mt == 1))) for mt in range(2)],
                            reads=[self.cstb, pe_[0], pe_[1]], writes=[b2], lhs=[self.cstb])
                    yield
                    c.op("dve", lambda: nc.vector.reciprocal(rden[:, :], b2[:, :]), reads=[b2], writes=[rden])
                    yield
                    for eb in range(2):
                        c.group("pe", [(lambda mt=mt: nc.tensor.matmul(b3[:, :], vx[:, mt, a * 256 + eb * 128:a * 256 + (eb + 1) * 128], pe_[mt][:, :],
                                                                      start=(mt == 0), stop=(mt == 1))) for mt in range(2)],
                                reads=[vx, pe_[0], pe_[1]], writes=[b3], lhs=[vx])
                        yield
                        c.op("dve", lambda: nc.vector.tensor_tensor(t1[:, :], b3[:, :], rden[:, :], ALU.mult), reads=[b3, rden], writes=[t1])
                        yield
                        c.op("pool", lambda: nc.gpsimd.tensor_tensor(yb[eb][:, :], t1[:, :], sz[eb][:, :], ALU.mult), reads=[t1, sz[eb]], writes=[yb[eb]])
                        yield
                        c.dma("sp", self.yT_d[self.nh + a * 2 + eb, :, sl], yb[eb][:, :], reads=[yb[eb]])
                        yield

        hs = list(range(self.nxa))
        run_streams([stream(0, hs[0::2]), stream(1, hs[1::2])])
        c.barrier()


Layers.xa_heads2 = xa_heads2
```
